# Optimizing a Trainium2 kernel written in Bass

```python
import jax, jax.numpy as jnp
from jax import lax
import numpy as np

D_MODEL = 1024
BATCH = 16
SEQ = 2048
DEPTH = 1

MEM_LEN = 256
D_MIX = 2 * D_MODEL
EPS = 1e-6

ATT_HEADS = 8
ATT_KV_HEADS = 2
ATT_HEAD_DIM = 64
ATT_Q_W = ATT_HEADS * ATT_HEAD_DIM
ATT_KV_W = ATT_KV_HEADS * ATT_HEAD_DIM
WINDOW = 128
ATT_BLOCK = 128
ROPE_THETA = 500000.0
ROPE_DIM = ATT_HEAD_DIM // 4

SSD_HEADS = 16
SSD_HEAD_DIM = 64
SSD_WIDTH = SSD_HEADS * SSD_HEAD_DIM
SSD_GROUPS = 2
SSD_STATE = 128
SSD_CONV = 5
SSD_CHUNK = 128
SSD_XBC_W = SSD_WIDTH + 2 * SSD_GROUPS * SSD_STATE
SSD_DT_W = 2 * SSD_HEADS

XATT_HEADS = 4
XATT_HEAD_DIM = 128
XATT_W = XATT_HEADS * XATT_HEAD_DIM

D_FF = 4 * D_MODEL

IN_SPLITS = [ATT_Q_W, ATT_KV_W, ATT_KV_W, SSD_WIDTH, SSD_XBC_W, SSD_DT_W, XATT_W]
D_IN = ATT_Q_W + 2 * ATT_KV_W + SSD_WIDTH + SSD_XBC_W + SSD_DT_W + XATT_W

kernel_name = "hybrid_swa_ssd_memxattn_block"


def rmsnorm(t, w):
    t32 = t.astype(jnp.float32)
    t32 = t32 * lax.rsqrt(jnp.mean(t32 * t32, axis=-1, keepdims=True) + EPS)
    return t32.astype(t.dtype) * w


def partial_rope(t, pos):
    half = ROPE_DIM // 2
    inv = ROPE_THETA ** (-jnp.arange(0, ROPE_DIM, 2, dtype=jnp.float32) / ROPE_DIM)
    ang = pos.astype(jnp.float32)[:, None] * inv[None, :]
    cos = jnp.cos(ang)[None, :, None, :]
    sin = jnp.sin(ang)[None, :, None, :]
    t1, t2, tp = t[..., :half], t[..., half:ROPE_DIM], t[..., ROPE_DIM:]
    rot = jnp.concatenate([t1 * cos - t2 * sin, t2 * cos + t1 * sin], axis=-1)
    return jnp.concatenate([rot.astype(t.dtype), tp], axis=-1)


def windowed_gqa(q, k, v, sink):
    b, L = q.shape[:2]
    nb = L // ATT_BLOCK
    R = ATT_HEADS // ATT_KV_HEADS
    d = ATT_HEAD_DIM
    qb = q.reshape(b, nb, ATT_BLOCK, ATT_KV_HEADS, R, d)
    pad = ((0, 0), (ATT_BLOCK, ATT_BLOCK), (0, 0), (0, 0))
    kp = jnp.pad(k, pad).reshape(b, nb + 2, ATT_BLOCK, ATT_KV_HEADS, d)
    vp = jnp.pad(v, pad).reshape(b, nb + 2, ATT_BLOCK, ATT_KV_HEADS, d)
    kb = jnp.concatenate([kp[:, :-2], kp[:, 1:-1], kp[:, 2:]], axis=2)
    vb = jnp.concatenate([vp[:, :-2], vp[:, 1:-1], vp[:, 2:]], axis=2)
    s = jnp.einsum('bnqgrd,bnkgd->bngrqk', qb, kb).astype(jnp.float32) * (d ** -0.5)
    blk = jnp.arange(nb)[:, None]
    qpos = blk * ATT_BLOCK + jnp.arange(ATT_BLOCK)[None, :]
    kpos = (blk - 1) * ATT_BLOCK + jnp.arange(3 * ATT_BLOCK)[None, :]
    rel = kpos[:, None, :] - qpos[:, :, None]
    valid = (jnp.abs(rel) <= WINDOW) & (kpos[:, None, :] >= 0) & (kpos[:, None, :] < L)
    s = jnp.where(valid[None, :, None, None], s, -1e30)
    sink_col = jnp.broadcast_to(
        sink.astype(jnp.float32).reshape(1, 1, ATT_KV_HEADS, R, 1, 1), s.shape[:-1] + (1,))
    p = jax.nn.softmax(jnp.concatenate([s, sink_col], axis=-1), axis=-1)[..., :-1]
    o = jnp.einsum('bngrqk,bnkgd->bnqgrd', p.astype(v.dtype), vb)
    return o.reshape(b, L, ATT_Q_W)


def centred_depthwise_conv(u, w, bias):
    ch = u.shape[-1]
    out = lax.conv_general_dilated(
        u, w[:, None, :].astype(u.dtype), window_strides=(1,),
        padding=[(SSD_CONV // 2, SSD_CONV // 2)],
        dimension_numbers=('NWC', 'WIO', 'NWC'), feature_group_count=ch)
    return out + bias


def segsum_exp(a_cs):
    Q = a_cs.shape[-1]
    mask = jnp.tril(jnp.ones((Q, Q), dtype=bool))
    diff = a_cs[..., :, None] - a_cs[..., None, :]
    return jnp.where(mask, jnp.exp(jnp.where(mask, diff, 0.0)), 0.0)


def ssd_chunked(xdt, dtA, B, C):
    b, L, H, P = xdt.shape
    G, N = B.shape[-2:]
    R = H // G
    Q = SSD_CHUNK
    nc = L // Q
    xc = xdt.reshape(b, nc, Q, G, R, P)
    Bc = B.reshape(b, nc, Q, G, N)
    Cc = C.reshape(b, nc, Q, G, N)
    a = dtA.astype(jnp.float32).reshape(b, nc, Q, G, R).transpose(0, 3, 4, 1, 2)
    a_cs = jnp.cumsum(a, axis=-1)
    Lm = segsum_exp(a_cs)
    cb = jnp.einsum('bclgn,bcsgn->bgcls', Cc, Bc)
    y_diag = jnp.einsum('bgcls,bgrcls,bcsgrp->bclgrp', cb, Lm, xc)
    decay_states = jnp.exp(a_cs[..., -1:] - a_cs)
    states = jnp.einsum('bcsgn,bgrcs,bcsgrp->bcgrpn', Bc, decay_states, xc)
    chunk_decay = jnp.exp(a_cs[..., -1])

    def step(h, inp):
        dec, st = inp
        return dec[..., None, None] * h + st, h

    init = jnp.zeros((b, G, R, P, N), dtype=states.dtype)
    _, prev = lax.scan(step, init, (jnp.moveaxis(chunk_decay, -1, 0), jnp.moveaxis(states, 1, 0)))
    prev = jnp.moveaxis(prev, 0, 1)
    y_off = jnp.einsum('bclgn,bgrcl,bcgrpn->bclgrp', Cc, jnp.exp(a_cs), prev)
    return (y_diag + y_off).reshape(b, L, H, P).astype(xdt.dtype)


def bidirectional_ssd(z, xbc, dt_raw, conv_w, conv_b, dt_bias_f, dt_bias_b,
                      a_log_f, a_log_b, ssd_d, ssd_norm_w):
    b, L = z.shape[:2]
    xbc = jax.nn.silu(centred_depthwise_conv(xbc, conv_w, conv_b))
    xs = xbc[..., :SSD_WIDTH].reshape(b, L, SSD_HEADS, SSD_HEAD_DIM)
    Bm = xbc[..., SSD_WIDTH:SSD_WIDTH + SSD_GROUPS * SSD_STATE].reshape(b, L, SSD_GROUPS, SSD_STATE)
    Cm = xbc[..., SSD_WIDTH + SSD_GROUPS * SSD_STATE:].reshape(b, L, SSD_GROUPS, SSD_STATE)
    dt_f = jax.nn.softplus(dt_raw[..., :SSD_HEADS].astype(jnp.float32) + dt_bias_f.astype(jnp.float32))
    dt_b = jax.nn.softplus(dt_raw[..., SSD_HEADS:].astype(jnp.float32) + dt_bias_b.astype(jnp.float32))
    A_f = -jnp.exp(a_log_f.astype(jnp.float32))
    A_b = -jnp.exp(a_log_b.astype(jnp.float32))
    y_f = ssd_chunked((xs * dt_f[..., None]).astype(xs.dtype), dt_f * A_f, Bm, Cm)
    flip = lambda t: jnp.flip(t, axis=1)
    y_b = flip(ssd_chunked(flip((xs * dt_b[..., None]).astype(xs.dtype)), flip(dt_b * A_b),
                           flip(Bm), flip(Cm)))
    y = y_f + y_b + ssd_d[:, None] * xs
    y = y.reshape(b, L, SSD_WIDTH) * jax.nn.silu(z)
    yg = rmsnorm(y.reshape(b, L, SSD_GROUPS, SSD_WIDTH // SSD_GROUPS),
                 ssd_norm_w.reshape(SSD_GROUPS, SSD_WIDTH // SSD_GROUPS))
    return yg.reshape(b, L, SSD_WIDTH)


def memory_cross_attention(qx, mem, mem_norm_w, w_mem_kv, xq_norm_w, xk_norm_w):
    b, L = qx.shape[:2]
    M = mem.shape[1]
    kv = rmsnorm(mem, mem_norm_w) @ w_mem_kv
    km = rmsnorm(kv[..., :XATT_W].reshape(b, M, XATT_HEADS, XATT_HEAD_DIM), xk_norm_w)
    vm = kv[..., XATT_W:].reshape(b, M, XATT_HEADS, XATT_HEAD_DIM)
    q = rmsnorm(qx.reshape(b, L, XATT_HEADS, XATT_HEAD_DIM), xq_norm_w)
    s = jnp.einsum('blhd,bmhd->bhlm', q, km).astype(jnp.float32) * (XATT_HEAD_DIM ** -0.5)
    p = jax.nn.softmax(s, axis=-1)
    o = jnp.einsum('bhlm,bmhd->blhd', p.astype(vm.dtype), vm)
    return o.reshape(b, L, XATT_W)


def setup_inputs(seed: int = 0) -> dict:
    key = jax.random.key(seed)
    ks = jax.random.split(key, 24)
    nrm = lambda k, shape, fan_in: jax.random.normal(k, shape, jnp.float32) * (fan_in ** -0.5)
    gain = lambda k, n: 1.0 + 0.02 * jax.random.normal(k, (DEPTH, n), jnp.float32)
    dt0 = jnp.exp(jax.random.uniform(ks[10], (DEPTH, 2, SSD_HEADS), jnp.float32,
                                     minval=np.log(1e-3), maxval=np.log(1e-1)))
    dt_bias = dt0 + jnp.log(-jnp.expm1(-dt0))
    a_log = jnp.log(jax.random.uniform(ks[11], (DEPTH, 2, SSD_HEADS), jnp.float32,
                                       minval=1.0, maxval=16.0))
    return {
        "x": jax.random.normal(ks[0], (BATCH, SEQ, D_MODEL), jnp.float32),
        "mem": jax.random.normal(ks[1], (BATCH, MEM_LEN, D_MODEL), jnp.float32),
        "norm_mix_w": gain(ks[2], D_MODEL),
        "w_in": nrm(ks[3], (DEPTH, D_MODEL, D_IN), D_MODEL),
        "q_norm_w": gain(ks[4], ATT_HEAD_DIM),
        "k_norm_w": gain(ks[5], ATT_HEAD_DIM),
        "attn_sink": 0.5 * jax.random.normal(ks[6], (DEPTH, ATT_HEADS), jnp.float32),
        "conv_w": nrm(ks[7], (DEPTH, SSD_CONV, SSD_XBC_W), SSD_CONV),
        "conv_b": 0.02 * jax.random.normal(ks[8], (DEPTH, SSD_XBC_W), jnp.float32),
        "dt_bias_f": dt_bias[:, 0],
        "dt_bias_b": dt_bias[:, 1],
        "a_log_f": a_log[:, 0],
        "a_log_b": a_log[:, 1],
        "ssd_d": 1.0 + 0.1 * jax.random.normal(ks[9], (DEPTH, SSD_HEADS), jnp.float32),
        "ssd_norm_w": gain(ks[12], SSD_WIDTH),
        "mem_norm_w": gain(ks[13], D_MODEL),
        "w_mem_kv": nrm(ks[14], (DEPTH, D_MODEL, 2 * XATT_W), D_MODEL),
        "xq_norm_w": gain(ks[15], XATT_HEAD_DIM),
        "xk_norm_w": gain(ks[16], XATT_HEAD_DIM),
        "w_out": nrm(ks[17], (DEPTH, D_MIX, D_MODEL), D_MIX),
        "norm_mlp_w": gain(ks[18], D_MODEL),
        "w_mlp_up": nrm(ks[19], (DEPTH, D_MODEL, D_FF), D_MODEL),
        "w_mlp_down": nrm(ks[20], (DEPTH, D_FF, D_MODEL), D_FF),
    }


def reference(x, mem, norm_mix_w, w_in, q_norm_w, k_norm_w, attn_sink, conv_w, conv_b,
              dt_bias_f, dt_bias_b, a_log_f, a_log_b, ssd_d, ssd_norm_w, mem_norm_w,
              w_mem_kv, xq_norm_w, xk_norm_w, w_out, norm_mlp_w, w_mlp_up, w_mlp_down):
    b, L, _ = x.shape
    pos = jnp.arange(L, dtype=jnp.int32)
    split_idx = [int(c) for c in np.cumsum(IN_SPLITS)[:-1]]
    for i in range(DEPTH):
        h = rmsnorm(x, norm_mix_w[i])
        proj = h @ w_in[i]
        q, k, v, z, xbc, dt_raw, qx = jnp.split(proj, split_idx, axis=-1)
        q = partial_rope(rmsnorm(q.reshape(b, L, ATT_HEADS, ATT_HEAD_DIM), q_norm_w[i]), pos)
        k = partial_rope(rmsnorm(k.reshape(b, L, ATT_KV_HEADS, ATT_HEAD_DIM), k_norm_w[i]), pos)
        v = v.reshape(b, L, ATT_KV_HEADS, ATT_HEAD_DIM)
        attn_out = windowed_gqa(q, k, v, attn_sink[i])
        ssd_out = bidirectional_ssd(z, xbc, dt_raw, conv_w[i], conv_b[i], dt_bias_f[i],
                                    dt_bias_b[i], a_log_f[i], a_log_b[i], ssd_d[i],
                                    ssd_norm_w[i])
        xatt_out = memory_cross_attention(qx, mem, mem_norm_w[i], w_mem_kv[i],
                                          xq_norm_w[i], xk_norm_w[i])
        mix = jnp.concatenate([attn_out, ssd_out, xatt_out], axis=-1) @ w_out[i]
        x = x + mix
        u = rmsnorm(x, norm_mlp_w[i]) @ w_mlp_up[i]
        x = x + jnp.square(jax.nn.relu(u)) @ w_mlp_down[i]
    return x
```

```python
import numpy as np
import ml_dtypes
from contextlib import ExitStack
import concourse.bass as bass
import concourse.mybir as mybir
from concourse.bass_utils import run_bass_kernel_spmd

F32 = mybir.dt.float32
BF16 = mybir.dt.bfloat16
AF = mybir.ActivationFunctionType
ALU = mybir.AluOpType

NCORES = 8
SEQ_PER_CORE = 2
L = 2048
D = 1024
NT = 16
EPS = 1e-6
DEBUG = False
STOP = 99


class _Stop(Exception):
    pass
NSEQ_RUN = SEQ_PER_CORE

C_Q, C_K, C_V, C_Z, C_XBC, C_DT, C_QX = 0, 512, 640, 768, 1792, 3328, 3360

(CB_ID, CB_ONES, CB_BLK, CB_OZ0, CB_OZ1, CB_LSF, CB_LSB, CB_ROT, CB_MPREV, CB_MNEXT, CB_MF, CB_MB) = range(12)
NCB = 12
(CF_TUI, CF_TLI, CF_TUS, CF_TLS, CF_ONES, CF_ID) = range(6)
NCF = 6
PP_NWMIX, PP_NWMLP, PP_NWMEM, PP_SSDNW = 0, 8, 16, 24
PP_CONVW = 32
PP_CONVB = 92
PP_QW, PP_KW, PP_XQW = 104, 105, 106
PP_SINK = 107
PP_DTB = 111
PP_ALOG = 143
PP_SSDD = 175
PP_XKW = 191
PP_M0, PP_M1 = 320, 321
NPP = 322

ENG = ["pe", "act", "dve", "pool", "sp"]


class Res:
    __slots__ = ("w", "r")

    def __init__(self):
        self.w = None
        self.r = {}


def RL(n):
    return [Res() for _ in range(n)]


class Bld:
    def __init__(self):
        self.streams = {e: [] for e in ENG}
        self.cnt = {}
        self.known = {e: {} for e in ENG}
        self.psi = 0

    def need(self, eng, tok, skip_same=False):
        if tok is None:
            return
        k, v = tok
        if skip_same and k == eng:
            return
        if self.known[eng].get(k, 0) >= v:
            return
        self.known[eng][k] = v
        self.streams[eng].append((0, k, v))

    def op(self, eng, fn, reads=(), writes=(), inc=True):
        for r in reads:
            self.need(eng, r.w)
        for w in writes:
            self.need(eng, w.w, True)
            for k, v in w.r.items():
                self.need(eng, (k, v), True)
        c = self.cnt.get(eng, 0) + 1
        if inc:
            self.cnt[eng] = c
        self.streams[eng].append((1, fn, eng if inc else None, 1))
        for r in reads:
            r.r[eng] = c
        for w in writes:
            w.w = (eng, c)
            w.r = {}

    def dma(self, q, chan, fn, reads=(), writes=()):
        for r in reads:
            self.need(q, r.w)
        for w in writes:
            if not (w.w is not None and w.w[0] == chan):
                self.need(q, w.w)
            for k, v in w.r.items():
                self.need(q, (k, v))
        c = self.cnt.get(chan, 0) + 16
        self.cnt[chan] = c
        self.streams[q].append((1, fn, chan, 16))
        for r in reads:
            r.r[chan] = c
        for w in writes:
            w.w = (chan, c)
            w.r = {}

    def barrier(self):
        toks = list(self.cnt.items())
        for e in ENG:
            for t in toks:
                self.need(e, t, True)

    def mm(self, out, lhsT, rhs, start, stop, reads, writes, inc=False):
        self.op("pe", lambda e: e.matmul(out, lhsT=lhsT, rhs=rhs, start=start, stop=stop), reads, writes, inc)

    def tr(self, out, in_, ident, reads, writes, inc=False):
        self.op("pe", lambda e: e.transpose(out=out, in_=in_, identity=ident), reads, writes, inc)

    def act(self, out, in_, func, reads, writes, scale=1.0, bias=0.0, accum=None):
        if accum is None:
            self.op("act", lambda e: e.activation(out=out, in_=in_, func=func, bias=bias, scale=scale), reads, writes)
        else:
            self.op("act", lambda e: e.activation(out=out, in_=in_, func=func, bias=bias, scale=scale,
                                                  accum_out=accum), reads, writes)

    def tt(self, eng, out, in0, in1, op, reads, writes):
        self.op(eng, lambda e: e.tensor_tensor(out=out, in0=in0, in1=in1, op=op), reads, writes)

    def ts(self, eng, out, in0, s1, s2, op0, op1, reads, writes):
        if s2 is None:
            self.op(eng, lambda e: e.tensor_scalar(out=out, in0=in0, scalar1=s1, scalar2=None, op0=op0), reads, writes)
        else:
            self.op(eng, lambda e: e.tensor_scalar(out=out, in0=in0, scalar1=s1, scalar2=s2, op0=op0, op1=op1),
                    reads, writes)

    def stt(self, eng, out, in0, scalar, in1, op0, op1, reads, writes):
        self.op(eng, lambda e: e.scalar_tensor_tensor(out=out, in0=in0, scalar=scalar, in1=in1, op0=op0, op1=op1),
                reads, writes)

    def cp(self, eng, out, in_, reads, writes):
        if eng == "act":
            self.op("act", lambda e: e.activation(out=out, in_=in_, func=AF.Copy), reads, writes)
        else:
            self.op(eng, lambda e: e.tensor_copy(out=out, in_=in_), reads, writes)

    def ms(self, eng, out, val, reads, writes):
        self.op(eng, lambda e: e.memset(out, val), reads, writes)


def run_pipelined(gens):
    prev = None
    for g in gens:
        next(g)
        if prev is not None:
            for _ in prev:
                pass
        prev = g
    if prev is not None:
        for _ in prev:
            pass


def build_program():
    nc = bass.Bass("TRN2", target_bir_lowering=False)
    dx = nc.dram_tensor("x", [SEQ_PER_CORE, L, D], F32, kind="ExternalInput").ap()
    dmem = nc.dram_tensor("mem", [SEQ_PER_CORE, 256, D], F32, kind="ExternalInput").ap()
    dwin = nc.dram_tensor("w_in", [D, 3872], F32, kind="ExternalInput").ap()
    dwkv = nc.dram_tensor("w_mem_kv", [D, 1024], F32, kind="ExternalInput").ap()
    dwout = nc.dram_tensor("w_out", [2048, D], F32, kind="ExternalInput").ap()
    dwup = nc.dram_tensor("w_up", [D, 4096], F32, kind="ExternalInput").ap()
    dwdn = nc.dram_tensor("w_down", [4096, D], F32, kind="ExternalInput").ap()
    dcb = nc.dram_tensor("cbf", [128, NCB, 128], BF16, kind="ExternalInput").ap()
    dcf = nc.dram_tensor("cf32", [128, NCF, 128], F32, kind="ExternalInput").ap()
    dpp = nc.dram_tensor("pp", [128, NPP], F32, kind="ExternalInput").ap()
    dcs = nc.dram_tensor("cossin", [128, 2, L], F32, kind="ExternalInput").ap()
    dout = nc.dram_tensor("out", [SEQ_PER_CORE, L, D], F32, kind="ExternalOutput").ap()
    if DEBUG:
        ddbg = nc.dram_tensor("dbg", [SEQ_PER_CORE, 128, 16, L], BF16, kind="ExternalOutput").ap()

    win_v = dwin.rearrange("(k p) n -> p k n", p=128)
    wkv_v = dwkv.rearrange("(k p) n -> p k n", p=128)
    wout_v = dwout.rearrange("(k p) n -> p k n", p=128)
    wup_v = dwup.rearrange("(k p) n -> p k n", p=128)
    wdn_v = dwdn.rearrange("(k p) n -> p k n", p=128)

    B = Bld()
    es = ExitStack()
    ARENA_ELEMS = 106400
    arena = es.enter_context(nc.sbuf_tensor("arena", [128, ARENA_ELEMS], BF16))
    PF = [es.enter_context(nc.psum_tensor(f"pf{i}", [128, 512], F32)) for i in range(6)]
    PB = [es.enter_context(nc.psum_tensor(f"pb{i}", [128, 1024], BF16)) for i in range(2)]
    PFR = RL(6)
    PBR = RL(2)
    pst = {"f": 0, "b": 0, "n": 6}

    def psf():
        i = pst["f"] % pst["n"]
        pst["f"] = (i + 1) % pst["n"]
        return PF[i], PFR[i]

    def psb():
        i = pst["b"]
        pst["b"] = (i + 1) % 2
        return PB[i], PBR[i]

    def vb(off, n):
        assert off % 4 == 0 and off // 2 + n <= ARENA_ELEMS, (off, n)
        return arena[:, off // 2: off // 2 + n]

    def vf(off, n):
        assert off % 4 == 0 and off // 2 + 2 * n <= ARENA_ELEMS, (off, n)
        return arena[:, off // 2: off // 2 + 2 * n].bitcast(F32)

    o = 0
    O_CB = o; o += NCB * 256
    O_CF = o; o += NCF * 512
    O_PP = o; o += NPP * 4
    O_SM = o; o += 1024
    O_HT = o; o += 32768
    O_MX = o; o += 32768
    O_LO = o; o += 32768
    O_HI = o; o += 65536
    O_WK = o
    WK_SIZE = ARENA_ELEMS * 2 - O_WK
    assert WK_SIZE >= 33000, WK_SIZE

    cb = vb(O_CB, NCB * 128).rearrange("p (m n) -> p m n", m=NCB)
    cf = vf(O_CF, NCF * 128).rearrange("p (m n) -> p m n", m=NCF)
    pp = vf(O_PP, NPP)
    esink = vf(O_SM, 4)
    aneg = vf(O_SM + 16, 32)
    R_const = Res()

    hT = vb(O_HT, 8 * L).rearrange("p (k t) -> p k t", k=8)
    mxA = vb(O_MX, 4 * L).rearrange("p (k t) -> p k t", k=4)
    mxX = vb(O_MX + 16384, 4 * L).rearrange("p (k t) -> p k t", k=4)
    lo = vb(O_LO, NT * 1024).rearrange("p (c f) -> p c f", c=NT)

    ident = cb[:, CB_ID, :]

    B.dma("sp", "c0", lambda e: e.dma_start(out=cb, in_=dcb), (), (R_const,))
    B.dma("sp", "c0", lambda e: e.dma_start(out=cf, in_=dcf), (), (R_const,))
    B.dma("sp", "c0", lambda e: e.dma_start(out=pp, in_=dpp), (), (R_const,))
    B.act(esink, pp[:, PP_SINK:PP_SINK + 4], AF.Exp, (R_const,), (R_const,))
    B.act(aneg, pp[:, PP_ALOG:PP_ALOG + 32], AF.Exp, (R_const,), (R_const,))
    B.ts("dve", aneg, aneg, -1.0, None, ALU.mult, None, (R_const,), (R_const,))

    def wload(dst, src, reads_w, chan):
        B.dma("pool", chan, lambda e: e.dma_start(out=dst, in_=src), (), reads_w)

    def hbuild(src_fn, ntiles, dstT, dst_res, nwcol, wk_off, src_res=None):
        xt = [vf(wk_off + i * 4096, 1024) for i in range(2)]
        xn = [vb(wk_off + 8192 + i * 2048, 1024) for i in range(2)]
        junk = vb(wk_off + 12288, 1024)
        st = vf(wk_off + 14336, 3 * ntiles)
        Rxt, Rxn, Rj, Rst = RL(2), RL(2), Res(), Res()
        B.ms("dve", st, 0.0, (), (Rst,))
        def hb_iter(tt):
            b = tt % 2
            if src_res is None:
                src = src_fn(tt)
                B.dma("sp", f"xt{b}", lambda e, o_=xt[b], i_=src: e.dma_start(out=o_, in_=i_), (), (Rxt[b],))
                xin, rin = xt[b], Rxt[b]
            else:
                xin, rin = src_fn(tt), src_res[tt]
            B.act(junk, xin, AF.Square, (rin, Rst), (Rj, Rst), accum=st[:, 3 * tt:3 * tt + 1])
            B.act(st[:, 3 * tt + 1:3 * tt + 2], st[:, 3 * tt:3 * tt + 1], AF.Ln, (Rst,), (Rst,), scale=1.0 / D, bias=EPS)
            B.act(st[:, 3 * tt + 2:3 * tt + 3], st[:, 3 * tt + 1:3 * tt + 2], AF.Exp, (Rst,), (Rst,), scale=-0.5)
            B.ts("dve", xn[b], xin, st[:, 3 * tt + 2:3 * tt + 3], None, ALU.mult, None, (rin, Rst), (Rxn[b],))
            yield
            pb, pbr = psb()
            for k in range(8):
                B.tr(pb[:, k * 128:(k + 1) * 128], xn[b][:, k * 128:(k + 1) * 128], ident, (Rxn[b], R_const), (pbr,),
                     inc=(k == 7))
            B.tt("dve", dstT[:, :, tt * 128:(tt + 1) * 128], pb[:, :].rearrange("p (k t) -> p k t", k=8),
                 pp[:, nwcol:nwcol + 8].unsqueeze(2).to_broadcast([128, 8, 128]), ALU.mult,
                 (pbr, R_const), (dst_res[tt],))

        run_pipelined([hb_iter(tt) for tt in range(ntiles)])

    def rstd_from_ps(psB, psBr, n_feat, lnv, rstd, Rln, Rrs):
        B.act(lnv, psB, AF.Ln, (psBr,), (Rln,), scale=1.0 / n_feat, bias=EPS)
        B.act(rstd, lnv, AF.Exp, (Rln,), (Rrs,), scale=-0.5)

    for s in range(NSEQ_RUN):
      try:
          B.barrier()
          RhT = RL(NT)
          hbuild(lambda tt: dx[s, tt * 128:(tt + 1) * 128, :], NT, hT, RhT, PP_NWMIX, O_WK)
          RhT4 = [[RhT[4 * T + i] for i in range(4)] for T in range(4)]
          B.barrier()
          if STOP == 1:
              raise _Stop()

          Rlo = RL(NT)
          prevb = vb(O_HI, NT * 1024).rearrange("p (c f) -> p c f", c=NT)
          BT = vb(O_HI + 32768, 2 * L).rearrange("p (g t) -> p g t", g=2)
          CT = vb(O_HI + 40960, 2 * L).rearrange("p (g t) -> p g t", g=2)
          Btok = vb(O_HI + 49152, NT * 256).rearrange("p (c f) -> p c f", c=NT)
          Rprevb, RBT, RCT, RBtok = RL(NT), RL(NT), RL(NT), RL(NT)
          Wz = vb(O_MX, 8 * 1024).rearrange("p (k n) -> p k n", k=8)
          RWz = Res()
          dtall = vf(O_MX + 16384, NT * 32).rearrange("p (c f) -> p c f", c=NT)
          aall = vf(O_MX + 18432, NT * 32).rearrange("p (c f) -> p c f", c=NT)
          exall = vf(O_MX + 20480, NT * 64).rearrange("p (c f) -> p c f", c=NT)
          cdec = vf(O_MX + 24576, NT * 32).rearrange("p (c f) -> p c f", c=NT)
          Rst = RL(NT)
          DI = vb(O_MX + 26624, 16 * 128).rearrange("p (h n) -> p h n", h=16)
          RDI = Res()
          wb = [vb(O_WK + i * 8192, 8 * 512).rearrange("p (k n) -> p k n", k=8) for i in range(2)]
          Rwb = RL(2)
          pre = [vb(O_WK + 16384 + i * 4112, 2052) for i in range(2)]
          Rpre = RL(2)
          actT = [vb(O_WK + 24640 + i * 4096, 2048) for i in range(2)]
          RactT = RL(2)
          diag = vb(O_HI, 5 * 128 * 2).rearrange("p (b k n) -> p b k n", b=2, k=5)
          Rdiag = RL(2)
          tmp32 = vf(O_HI + 4096, 64)
          Rtmp = Res()

          wload(Wz, win_v[:, :, C_Z:C_Z + 1024], (RWz,), "wz")
          for h in range(16):
              B.ts("dve", DI[:, h, :], cf[:, CF_ID, :], pp[:, PP_SSDD + h:PP_SSDD + h + 1], None, ALU.mult, None,
                   (R_const,), (RDI,))

          wdt = vb(O_HI + 8192, 8 * 32).rearrange("p (k n) -> p k n", k=8)
          Rwdt = Res()
          wload(wdt, win_v[:, :, C_DT:C_DT + 32], (Rwdt,), "wdt")
          for c in range(NT):
              ps, psr = psf()
              for k in range(8):
                  B.mm(ps[:, 0:32], hT[:, k, c * 128:(c + 1) * 128], wdt[:, k, :], k == 0, k == 7, (RhT[c], Rwdt), (psr,),
                       inc=(k == 7))
              B.tt("dve", tmp32[:, 0:32], ps[:, 0:32], pp[:, PP_DTB:PP_DTB + 32], ALU.add, (psr, R_const), (Rtmp,))
              B.act(tmp32[:, 32:64], tmp32[:, 0:32], AF.Exp, (Rtmp,), (Rtmp,))
              B.act(dtall[:, c, :], tmp32[:, 32:64], AF.Ln, (Rtmp,), (Rst[c],), bias=1.0)
              B.tt("dve", aall[:, c, :], dtall[:, c, :], aneg, ALU.mult, (Rst[c], R_const), (Rst[c],))
              ps, psr = psf()
              B.mm(ps[:, 0:16], cf[:, CF_TUI, :], aall[:, c, 0:16], True, True, (R_const, Rst[c]), (psr,))
              B.mm(ps[:, 16:32], cf[:, CF_TLI, :], aall[:, c, 16:32], True, True, (R_const, Rst[c]), (psr,))
              B.mm(ps[:, 32:48], cf[:, CF_TUS, :], aall[:, c, 0:16], True, True, (R_const, Rst[c]), (psr,))
              B.mm(ps[:, 48:64], cf[:, CF_TLS, :], aall[:, c, 16:32], True, True, (R_const, Rst[c]), (psr,))
              B.mm(ps[:, 64:96], cf[:, CF_ONES, :], aall[:, c, :], True, True, (R_const, Rst[c]), (psr,), inc=True)
              B.act(exall[:, c, :], ps[:, 0:64], AF.Exp, (psr,), (Rst[c],))
              B.act(cdec[:, c, :], ps[:, 64:96], AF.Exp, (psr,), (Rst[c],))

          def xbc_iter(j):
              blk, jj = j // 4, j % 4
              wbi = blk % 2
              if jj == 0:
                  wload(wb[wbi], win_v[:, :, C_XBC + blk * 512:C_XBC + (blk + 1) * 512], (Rwb[wbi],), f"wb{wbi}")
              pb_ = j % 2
              if j < 2:
                  B.ms("dve", pre[pb_][:, 0:2], 0.0, (), (Rpre[pb_],))
                  B.ms("dve", pre[pb_][:, 2050:2052], 0.0, (), (Rpre[pb_],))
              for tap in range(5):
                  B.ts("dve", diag[:, pb_, tap, :], cf[:, CF_ID, :],
                       pp[:, PP_CONVW + j * 5 + tap:PP_CONVW + j * 5 + tap + 1], None, ALU.mult, None,
                       (R_const,), (Rdiag[pb_],))
              for T in range(4):
                  ps, psr = psf()
                  for k in range(8):
                      B.mm(ps[:, :], wb[wbi][:, k, jj * 128:(jj + 1) * 128], hT[:, k, T * 512:(T + 1) * 512], k == 0, k == 7,
                           (Rwb[wbi],) + tuple(RhT4[T]), (psr,), inc=(k == 7))
                  B.cp("act", pre[pb_][:, 2 + T * 512:2 + (T + 1) * 512], ps[:, :], (psr,), (Rpre[pb_],))
              yield
              if j < 8:
                  dstF, dres = actT[pb_], None
              elif j < 10:
                  dstF = BT[:, j - 8, :]
              else:
                  dstF = CT[:, j - 10, :]
              for T in range(4):
                  ps, psr = psf()
                  for tap in range(5):
                      B.mm(ps[:, :], diag[:, pb_, tap, :], pre[pb_][:, T * 512 + tap:T * 512 + tap + 512], tap == 0, tap == 4,
                           (Rdiag[pb_], Rpre[pb_]), (psr,), inc=(tap == 4))
                  if j < 8:
                      wr = (RactT[pb_],)
                  elif j < 10:
                      wr = tuple(RBT[4 * T:4 * T + 4])
                  else:
                      wr = tuple(RCT[4 * T:4 * T + 4])
                  B.act(dstF[:, T * 512:(T + 1) * 512], ps[:, :], AF.Silu, (psr, R_const), wr,
                        bias=pp[:, PP_CONVB + j:PP_CONVB + j + 1])
              if j < 10:
                  for q4 in range(4):
                      pb, pbr = psb()
                      for i in range(4):
                          c = q4 * 4 + i
                          rd = (RactT[pb_], R_const) if j < 8 else (RBT[c], R_const)
                          B.tr(pb[:, i * 128:(i + 1) * 128], dstF[:, c * 128:(c + 1) * 128], ident, rd, (pbr,), inc=(i == 3))
                      if j < 8:
                          B.cp("act", lo[:, q4 * 4:(q4 + 1) * 4, j * 128:(j + 1) * 128],
                               pb[:, 0:512].rearrange("p (c f) -> p c f", c=4), (pbr,), tuple(Rlo[q4 * 4:q4 * 4 + 4]))
                      else:
                          B.cp("act", Btok[:, q4 * 4:(q4 + 1) * 4, (j - 8) * 128:(j - 7) * 128],
                               pb[:, 0:512].rearrange("p (c f) -> p c f", c=4), (pbr,), tuple(RBtok[q4 * 4:q4 * 4 + 4]))
          run_pipelined([xbc_iter(j) for j in range(12)])
          B.barrier()
          if STOP == 3:
              raise _Stop()

          w0 = O_WK
          Hs = vf(w0, 1024); w0 += 4096
          xd = [vb(w0 + i * 2048, 1024) for i in range(2)]; w0 += 4096
          wsm = vf(w0, 64); w0 += 256
          Ebuf = vb(w0, 4096).rearrange("p (q n) -> p q n", q=8); w0 += 8192
          rhsb2 = vb(w0, 2048); rhsb = rhsb2.rearrange("p (h n) -> p h n", h=16); w0 += 4096
          xdt = [vb(w0 + i * 2048, 1024) for i in range(2)]; w0 += 4096
          cbm = vb(w0, 512).rearrange("p (g d n) -> p g d n", g=2, d=2); w0 += 1024
          prevf = vb(w0, 1024); w0 += 2048
          szb = vb(w0, 1024); w0 += 2048
          t1 = vf(w0, 1024); w0 += 4096
          t2 = vf(w0, 1024); w0 += 4096
          gst = vf(w0, 8); w0 += 32
          ynb = vb(w0, 1024); w0 += 2048
          assert w0 - O_WK <= WK_SIZE, (w0 - O_WK, WK_SIZE)
          RH, Rxd, Rws, RE, Rrhs, Rxdt, Rcbm, Rpf, Rsz, Rt1, Rt2, Rg, Ryn = (Res(), RL(2), Res(), RL(8), Res(), RL(2),
                                                                              Res(), Res(), Res(), Res(), Res(), Res(), Res())

          def bc16(ap16):
              return ap16.unsqueeze(2).to_broadcast([128, 16, 64])

          def v3(ap1024):
              return ap1024.rearrange("p (h d) -> p h d", h=16)

          def state_prep(c, d, wcol, ecol):
              B.tt("dve", wsm[:, d * 16:(d + 1) * 16], dtall[:, c, wcol:wcol + 16], exall[:, c, ecol:ecol + 16], ALU.mult,
                   (Rst[c],), (Rws,))
              B.tt("pool", v3(xd[d]), v3(lo[:, c, :]), bc16(wsm[:, d * 16:(d + 1) * 16]), ALU.mult, (Rlo[c], Rws), (Rxd[d],))
              pss = []
              for g in range(2):
                  ps, psr = psf()
                  B.mm(ps[:, :], Btok[:, c, g * 128:(g + 1) * 128], xd[d][:, g * 512:(g + 1) * 512], True, True,
                       (RBtok[c], Rxd[d]), (psr,), inc=True)
                  pss.append((ps, psr))
              return pss

          def state_apply(c, dcol, pss):
              B.tt("dve", v3(Hs), v3(Hs), bc16(cdec[:, c, dcol:dcol + 16]), ALU.mult, (RH, Rst[c]), (RH,))
              for g in range(2):
                  B.tt("dve", Hs[:, g * 512:(g + 1) * 512], Hs[:, g * 512:(g + 1) * 512], pss[g][0][:, :], ALU.add,
                       (RH, pss[g][1]), (RH,))

          B.ms("dve", Hs, 0.0, (), (RH,))

          def p1_iter(c):
              pss = state_prep(c, 1, 16, 48) if c > 0 else None
              yield
              B.cp("act", prevb[:, c, :], Hs, (RH,), (Rprevb[c],))
              if c > 0:
                  state_apply(c, 16, pss)

          run_pipelined([p1_iter(c) for c in range(NT - 1, -1, -1)])

          E2 = [Ebuf, vb(O_HI + 57344, 4096).rearrange("p (q n) -> p q n", q=8)]
          RE2 = [RE, RL(8)]
          xdt2 = [xdt, [vb(O_WK + 4096 + 2048, 1024), vb(O_MX + 30720, 1024)]]
          Rxdt2 = [Rxdt, [Rxd[1], Res()]]
          B.ms("dve", Hs, 0.0, (), (RH,))

          def p2_iter(c):
              tsl = slice(c * 128, (c + 1) * 128)
              pi = c % 2
              Eb, REb, xdtb, Rxdtb = E2[pi], RE2[pi], xdt2[pi], Rxdt2[pi]
              for g in range(2):
                  ps, psr = psf()
                  B.mm(ps[:, 0:128], BT[:, g, tsl], CT[:, g, tsl], True, True, (RBT[c], RCT[c]), (psr,), inc=True)
                  B.tt("dve", cbm[:, g, :, :], ps[:, 0:128].unsqueeze(1).to_broadcast([128, 2, 128]),
                       cb[:, CB_MF:CB_MF + 2, :], ALU.mult, (psr, R_const), (Rcbm,))
              for d in range(2):
                  B.tt("pool", v3(xdtb[d]), v3(lo[:, c, :]), bc16(dtall[:, c, d * 16:(d + 1) * 16]), ALU.mult,
                       (Rlo[c], Rst[c]), (Rxdtb[d],))
              for d in range(2):
                  tri = cf[:, CF_TUI, :] if d == 0 else cf[:, CF_TLI, :]
                  lsm = cb[:, CB_LSF, :] if d == 0 else cb[:, CB_LSB, :]
                  B.tt("pool", rhsb, aall[:, c, d * 16:(d + 1) * 16].unsqueeze(2).to_broadcast([128, 16, 128]),
                       tri.unsqueeze(1).to_broadcast([128, 16, 128]), ALU.mult, (Rst[c], R_const), (Rrhs,))
                  for q in range(4):
                      ps, psr = psf()
                      B.mm(ps[:, :], lsm, rhsb2[:, q * 512:(q + 1) * 512], True, True, (R_const, Rrhs), (psr,), inc=True)
                      qi = d * 4 + q
                      B.act(Eb[:, qi, :], ps[:, :], AF.Exp, (psr,), (REb[qi],))
                      g = q // 2
                      B.tt("dve", Eb[:, qi, :].rearrange("p (h n) -> p h n", h=4),
                           Eb[:, qi, :].rearrange("p (h n) -> p h n", h=4),
                           cbm[:, g, d, :].unsqueeze(1).to_broadcast([128, 4, 128]), ALU.mult, (REb[qi], Rcbm), (REb[qi],))
              yield
              for hf in range(2):
                  ps, psr = psf()
                  for k in range(8):
                      B.mm(ps[:, :], hT[:, k, tsl], Wz[:, k, hf * 512:(hf + 1) * 512], k == 0, k == 7, (RhT[c], RWz), (psr,),
                           inc=(k == 7))
                  B.act(szb[:, hf * 512:(hf + 1) * 512], ps[:, :], AF.Silu, (psr,), (Rsz,))
              yps = []
              for hf in range(2):
                  ps, psr = psf()
                  for h8 in range(8):
                      h = hf * 8 + h8
                      osl = ps[:, h8 * 64:(h8 + 1) * 64]
                      B.mm(osl, Eb[:, h // 4, (h % 4) * 128:(h % 4 + 1) * 128], xdtb[0][:, h * 64:(h + 1) * 64], True, False,
                           (REb[h // 4], Rxdtb[0]), (psr,))
                      B.mm(osl, Eb[:, 4 + h // 4, (h % 4) * 128:(h % 4 + 1) * 128], xdtb[1][:, h * 64:(h + 1) * 64], False,
                           False, (REb[4 + h // 4], Rxdtb[1]), (psr,))
                      B.mm(osl, DI[:, h, :], lo[:, c, h * 64:(h + 1) * 64], False, True, (RDI, Rlo[c]), (psr,), inc=(h8 == 7))
                  yps.append((ps, psr))
              B.cp("act", prevf, Hs, (RH,), (Rpf,))
              zf, zb = [], []
              for g in range(2):
                  ps, psr = psf()
                  B.mm(ps[:, :], CT[:, g, tsl], prevf[:, g * 512:(g + 1) * 512], True, True, (RCT[c], Rpf), (psr,), inc=True)
                  zf.append((ps, psr))
              for g in range(2):
                  B.tt("dve", v3(t1)[:, g * 8:(g + 1) * 8, :], zf[g][0][:, :].rearrange("p (h d) -> p h d", h=8),
                       exall[:, c, g * 8:(g + 1) * 8].unsqueeze(2).to_broadcast([128, 8, 64]), ALU.mult,
                       (zf[g][1], Rst[c]), (Rt1,))
              for g in range(2):
                  ps, psr = psf()
                  B.mm(ps[:, :], CT[:, g, tsl], prevb[:, c, g * 512:(g + 1) * 512], True, True, (RCT[c], Rprevb[c]), (psr,),
                       inc=True)
                  zb.append((ps, psr))
              for g in range(2):
                  B.tt("dve", v3(t2)[:, g * 8:(g + 1) * 8, :], zb[g][0][:, :].rearrange("p (h d) -> p h d", h=8),
                       exall[:, c, 16 + g * 8:16 + (g + 1) * 8].unsqueeze(2).to_broadcast([128, 8, 64]), ALU.mult,
                       (zb[g][1], Rst[c]), (Rt2,))
              B.tt("pool", t1, t1, t2, ALU.add, (Rt1, Rt2), (Rt1,))
              for hf in range(2):
                  B.tt("dve", t1[:, hf * 512:(hf + 1) * 512], t1[:, hf * 512:(hf + 1) * 512], yps[hf][0][:, :], ALU.add,
                       (Rt1, yps[hf][1]), (Rt1,))
              B.tt("dve", t1, t1, szb, ALU.mult, (Rt1, Rsz), (Rt1,))
              B.ms("dve", gst, 0.0, (), (Rg,))
              for g in range(2):
                  B.act(t2[:, g * 512:(g + 1) * 512], t1[:, g * 512:(g + 1) * 512], AF.Square, (Rt1, Rg), (Rt2, Rg),
                        accum=gst[:, g:g + 1])
              B.act(gst[:, 2:4], gst[:, 0:2], AF.Ln, (Rg,), (Rg,), scale=1.0 / 512, bias=EPS)
              B.act(gst[:, 4:6], gst[:, 2:4], AF.Exp, (Rg,), (Rg,), scale=-0.5)
              for g in range(2):
                  B.ts("dve", ynb[:, g * 512:(g + 1) * 512], t1[:, g * 512:(g + 1) * 512], gst[:, 4 + g:5 + g], None, ALU.mult,
                       None, (Rt1, Rg), (Ryn,))
              if c < NT - 1:
                  pss = state_prep(c, 0, 0, 32)
                  state_apply(c, 0, pss)
              pb, pbr = psb()
              for k in range(8):
                  B.tr(pb[:, k * 128:(k + 1) * 128], ynb[:, k * 128:(k + 1) * 128], ident, (Ryn, R_const), (pbr,), inc=(k == 7))
              B.tt("dve", lo[:, c, :].rearrange("p (k t) -> p k t", k=8), pb[:, :].rearrange("p (k t) -> p k t", k=8),
                   pp[:, PP_SSDNW:PP_SSDNW + 8].unsqueeze(2).to_broadcast([128, 8, 128]), ALU.mult,
                   (pbr, R_const), (Rlo[c],))

          run_pipelined([p2_iter(c) for c in range(NT)])
          B.barrier()
          if STOP == 5:
              raise _Stop()

          qT = vb(O_HI, 4 * L).rearrange("p (k t) -> p k t", k=4)
          kz = vb(O_HI + 16384, 4 * L).rearrange("p (g h t) -> p g h t", g=2, h=2)
          vz = vb(O_HI + 32768, NT * 512).rearrange("p (c f) -> p c f", c=NT)
          cs = vf(O_HI + 49152, 2 * L).rearrange("p (a t) -> p a t", a=2)
          RqT, Rkd, Rvz, Rcs = RL(NT), RL(NT), RL(NT), Res()
          B.dma("sp", "c0", lambda e: e.dma_start(out=cs, in_=dcs), (), (Rcs,))
          wv = vb(O_WK, 8 * 512).rearrange("p (k n) -> p k n", k=8)
          wk = vb(O_WK + 8192, 8 * 256).rearrange("p (k n) -> p k n", k=8)
          wq = vb(O_WK + 12288, 8 * 512).rearrange("p (k n) -> p k n", k=8)
          Rwv, Rwk, Rwq = Res(), Res(), Res()
          def qkset(w0):
              d_ = {}
              d_["sq"] = vb(w0, 512); w0 += 1024
              d_["lnv"] = vf(w0, 512); w0 += 2048
              d_["qn"] = vb(w0, 512); w0 += 1024
              d_["ta"] = vf(w0, 512); w0 += 2048
              d_["tb"] = vf(w0, 512); w0 += 2048
              for nm in ("Rsq", "Rln", "Rqn", "Rta", "Rtb"):
                  d_[nm] = Res()
              return d_
          qks = [qkset(O_WK + 20480), qkset(O_WK)]
          w0 = O_WK + 28672
          Pt = [vb(w0 + i * 1536, 768) for i in range(2)]; w0 += 3072
          dpl = vf(w0, 512); w0 += 2048
          ktmp2 = [vb(w0 + i * 1024, 512) for i in range(2)]; w0 += 2048
          Rkt2 = RL(2)
          assert w0 - O_WK <= WK_SIZE
          RPt, Rdp = RL(2), Res()
          qkc = [0]

          B.ms("dve", vb(O_WK, 4096), 0.0, (), (Rwv,))
          for g in range(2):
              for hf in range(2):
                  c0 = (g * 2 + hf) * 128 + hf * 64
                  wload(wv[:, :, c0:c0 + 64], win_v[:, :, C_V + g * 64:C_V + (g + 1) * 64], (Rwv,), "wv")
              for hf in range(2):
                  wload(wk[:, :, g * 128 + hf * 64:g * 128 + (hf + 1) * 64], win_v[:, :, C_K + g * 64:C_K + (g + 1) * 64],
                        (Rwk,), "wk")
          wload(wq, win_v[:, :, C_Q:C_Q + 512], (Rwq,), "wq")
          for tt in range(NT):
              ps, psr = psf()
              for k in range(8):
                  B.mm(ps[:, :], hT[:, k, tt * 128:(tt + 1) * 128], wv[:, k, :], k == 0, k == 7, (RhT[tt], Rwv), (psr,),
                       inc=(k == 7))
              B.cp("act", vz[:, tt, :], ps[:, :], (psr,), (Rvz[tt],))

          def qk_chunk(wtile, wres, col0, dst_ap, dres4, pcol, T, post=None):
              S_ = qks[qkc[0] % 2]
              qkc[0] += 1
              sq, lnv, qn, ta, tb = S_["sq"], S_["lnv"], S_["qn"], S_["ta"], S_["tb"]
              Rsq, Rln, Rqn, Rta, Rtb = S_["Rsq"], S_["Rln"], S_["Rqn"], S_["Rta"], S_["Rtb"]
              tsl = slice(T * 512, (T + 1) * 512)
              psA, psAr = psf()
              for k in range(8):
                  B.mm(psA[:, :], wtile[:, k, col0:col0 + 128], hT[:, k, tsl], k == 0, k == 7, (wres,) + tuple(RhT4[T]),
                       (psAr,), inc=(k == 7))
              B.act(sq, psA[:, :], AF.Square, (psAr,), (Rsq,))
              psB, psBr = psf()
              B.mm(psB[:, :], cb[:, CB_BLK, :], sq, True, True, (R_const, Rsq), (psBr,), inc=True)
              rstd_from_ps(psB[:, :], psBr, 64, lnv, lnv, Rln, Rln)
              B.stt("dve", qn, psA[:, :], pp[:, pcol:pcol + 1], lnv, ALU.mult, ALU.mult, (psAr, R_const, Rln), (Rqn,))
              yield
              psR, psRr = psf()
              B.mm(psR[:, :], cb[:, CB_ROT, :], qn, True, True, (R_const, Rqn), (psRr,), inc=True)
              B.tt("dve", ta, psR[:, :], cs[:, 1, tsl], ALU.mult, (psRr, Rcs), (Rta,))
              B.tt("dve", tb, qn, cs[:, 0, tsl], ALU.mult, (Rqn, Rcs), (Rtb,))
              B.tt("dve", dst_ap, ta, tb, ALU.add, (Rta, Rtb), tuple(dres4))
              if post is not None:
                  post()

          B.barrier()
          def kpost(g, T):
              def f():
                  for hf in range(2):
                      B.ts("dve", kz[:, g, hf, T * 512:(T + 1) * 512], ktmp2[g], pp[:, PP_M0 + hf:PP_M0 + hf + 1], None,
                           ALU.mult, None, (Rkt2[g], R_const), tuple(Rkd[4 * T:4 * T + 4]))
              return f
          gens = []
          for T in range(4):
              for g in range(2):
                  gens.append(qk_chunk(wk, Rwk, g * 128, ktmp2[g], (Rkt2[g],), PP_KW, T, post=kpost(g, T)))
              for c4 in range(4):
                  gens.append(qk_chunk(wq, Rwq, c4 * 128, qT[:, c4, T * 512:(T + 1) * 512], RqT[4 * T:4 * T + 4], PP_QW, T))
          run_pipelined(gens)

          if STOP == 5.5:
              B.barrier()
              raise _Stop()
          pst["n"] = 4
          pst["f"] = 0
          psN, psNr = PF[4], PFR[4]
          psD, psDr = PF[5], PFR[5]

          def att_iter(n, c4):
              qsl = slice(n * 128, (n + 1) * 128)
              js = [j for j in (n - 1, n, n + 1) if 0 <= j < NT]
              g = c4 // 2
              pi = c4 % 2
              psS0, psS0r = psf()
              psS1, psS1r = psf()
              for ji, j in enumerate(js):
                  for hf in range(2):
                      slot = ji * 2 + hf
                      pS, pSr = (psS0, psS0r) if slot < 4 else (psS1, psS1r)
                      so = (slot % 4) * 128
                      B.mm(pS[:, so:so + 128], kz[:, g, hf, j * 128:(j + 1) * 128],
                           qT[:, c4, qsl], True, True, (Rkd[j], RqT[n]), (pSr,),
                           inc=(slot == 3 or slot == len(js) * 2 - 1))
              n0 = min(4, len(js) * 2)
              B.act(Pt[pi][:, 0:n0 * 128], psS0[:, 0:n0 * 128], AF.Exp, (psS0r,), (RPt[pi],), scale=0.125)
              if len(js) * 2 > 4:
                  B.act(Pt[pi][:, 512:768], psS1[:, 0:256], AF.Exp, (psS1r,), (RPt[pi],), scale=0.125)
              for ji, j in enumerate(js):
                  if j != n:
                      mk = cb[:, CB_MPREV, :] if j < n else cb[:, CB_MNEXT, :]
                      pv = Pt[pi][:, ji * 256:(ji + 1) * 256].rearrange("p (h q) -> p h q", h=2)
                      B.tt("dve", pv, pv, mk.unsqueeze(1).to_broadcast([128, 2, 128]), ALU.mult, (RPt[pi], R_const),
                           (RPt[pi],))
              yield
              nmm = len(js) * 2
              i = 0
              for ji, j in enumerate(js):
                  for hf in range(2):
                      B.mm(psN[:, c4 * 128:(c4 + 1) * 128], vz[:, j, (g * 2 + hf) * 128:(g * 2 + hf + 1) * 128],
                           Pt[pi][:, (ji * 2 + hf) * 128:(ji * 2 + hf + 1) * 128], i == 0, i == nmm - 1,
                           (Rvz[j], RPt[pi]), (psNr,))
                      i += 1
              i = 0
              for ji, j in enumerate(js):
                  for hf in range(2):
                      B.mm(psD[:, c4 * 128:(c4 + 1) * 128], cb[:, CB_OZ0 + hf, :],
                           Pt[pi][:, (ji * 2 + hf) * 128:(ji * 2 + hf + 1) * 128], i == 0, i == nmm - 1,
                           (R_const, RPt[pi]), (psDr,), inc=(i == nmm - 1))
                      i += 1
              if c4 == 3:
                  B.tt("dve", dpl.rearrange("p (c q) -> p c q", c=4), psD[:, :].rearrange("p (c q) -> p c q", c=4),
                       esink.unsqueeze(2).to_broadcast([128, 4, 128]), ALU.add, (psDr, R_const), (Rdp,))
                  B.op("dve", lambda e: e.reciprocal(out=dpl, in_=dpl), (Rdp,), (Rdp,))
                  B.tt("dve", mxA[:, :, qsl], psN[:, :].rearrange("p (c q) -> p c q", c=4),
                       dpl.rearrange("p (c q) -> p c q", c=4), ALU.mult, (psNr, Rdp), ())

          run_pipelined([att_iter(n, c4) for n in range(NT) for c4 in range(4)])
          pst["n"] = 6
          pst["f"] = 0
          B.barrier()
          if STOP == 6:
              raise _Stop()

          memT = vb(O_HI, 8 * 256).rearrange("p (k t) -> p k t", k=8)
          kmT = vb(O_HI + 4096, 4 * 256).rearrange("p (h t) -> p h t", h=4)
          vm = vb(O_HI + 6144, 2 * 512).rearrange("p (m f) -> p m f", m=2)
          kmn = vb(O_HI + 8192, 512)
          kst = vf(O_HI + 9216, 16)
          wkv = vb(O_HI + 16384, 8 * 1024).rearrange("p (k n) -> p k n", k=8)
          wqx = vb(O_HI + 32768, 8 * 512).rearrange("p (k n) -> p k n", k=8)
          RmemT, RkmT, Rvm, Rkmn, Rkst, Rwkv, Rwqx = RL(2), Res(), RL(2), Res(), Res(), Res(), Res()
          wload(wkv, wkv_v, (Rwkv,), "wkv")
          wload(wqx, win_v[:, :, C_QX:C_QX + 512], (Rwqx,), "wqx")
          hbuild(lambda tt: dmem[s, tt * 128:(tt + 1) * 128, :], 2, memT, RmemT, PP_NWMEM, O_WK)
          def xset(w0):
              d_ = {}
              d_["sq"] = vb(w0, 512); w0 += 1024
              d_["lnv"] = vf(w0, 512); w0 += 2048
              d_["qn"] = vb(w0, 512); w0 += 1024
              d_["rD"] = vf(w0, 512); w0 += 2048
              d_["Px"] = [vb(w0 + i * 1024, 512) for i in range(2)]; w0 += 2048
              for nm in ("Rsq", "Rln", "Rqn", "RrD"):
                  d_[nm] = Res()
              d_["RPx"] = RL(2)
              return d_
          xs_ = [xset(O_WK + 16384), xset(O_WK + 16384 + 8192)]
          tk = vf(O_WK + 32768, 512)
          Rtk = Res()
          assert 32768 + 2048 <= WK_SIZE
          for mt in range(2):
              msl = slice(mt * 128, (mt + 1) * 128)
              psK, psKr = psf()
              for k in range(8):
                  B.mm(psK[:, :], memT[:, k, msl], wkv[:, k, 0:512], k == 0, k == 7, (RmemT[mt], Rwkv), (psKr,), inc=(k == 7))
              B.ms("dve", kst, 0.0, (), (Rkst,))
              for h in range(4):
                  B.act(tk[:, h * 128:(h + 1) * 128], psK[:, h * 128:(h + 1) * 128], AF.Square, (psKr, Rkst), (Rtk, Rkst),
                        accum=kst[:, h:h + 1])
              B.act(kst[:, 4:8], kst[:, 0:4], AF.Ln, (Rkst,), (Rkst,), scale=1.0 / 128, bias=EPS)
              B.act(kst[:, 8:12], kst[:, 4:8], AF.Exp, (Rkst,), (Rkst,), scale=-0.5)
              B.tt("dve", tk.rearrange("p (h d) -> p h d", h=4), psK[:, :].rearrange("p (h d) -> p h d", h=4),
                   kst[:, 8:12].unsqueeze(2).to_broadcast([128, 4, 128]), ALU.mult, (psKr, Rkst), (Rtk,))
              B.tt("dve", kmn.rearrange("p (h d) -> p h d", h=4), tk.rearrange("p (h d) -> p h d", h=4),
                   pp[:, PP_XKW:PP_XKW + 128].unsqueeze(1).to_broadcast([128, 4, 128]), ALU.mult, (Rtk, R_const), (Rkmn,))
              pb, pbr = psb()
              for h in range(4):
                  B.tr(pb[:, h * 128:(h + 1) * 128], kmn[:, h * 128:(h + 1) * 128], ident, (Rkmn, R_const), (pbr,), inc=(h == 3))
              B.cp("act", kmT[:, :, msl], pb[:, 0:512].rearrange("p (h t) -> p h t", h=4), (pbr,), (RkmT,))
              psV, psVr = psf()
              for k in range(8):
                  B.mm(psV[:, :], memT[:, k, msl], wkv[:, k, 512:1024], k == 0, k == 7, (RmemT[mt], Rwkv), (psVr,), inc=(k == 7))
              B.cp("act", vm[:, mt, :], psV[:, :], (psVr,), (Rvm[mt],))
          def xat_iter(T, h, xc):
              tsl = slice(T * 512, (T + 1) * 512)
              S_ = xs_[xc % 2]
              sq, lnv, qn, rD, Px = S_["sq"], S_["lnv"], S_["qn"], S_["rD"], S_["Px"]
              Rsq, Rln, Rqn, RrD, RPx = S_["Rsq"], S_["Rln"], S_["Rqn"], S_["RrD"], S_["RPx"]
              psA, psAr = psf()
              for k in range(8):
                  B.mm(psA[:, :], wqx[:, k, h * 128:(h + 1) * 128], hT[:, k, tsl], k == 0, k == 7, (Rwqx,) + tuple(RhT4[T]),
                       (psAr,), inc=(k == 7))
              B.act(sq, psA[:, :], AF.Square, (psAr,), (Rsq,))
              psB, psBr = psf()
              B.mm(psB[:, :], cb[:, CB_ONES, :], sq, True, True, (R_const, Rsq), (psBr,), inc=True)
              rstd_from_ps(psB[:, :], psBr, 128, lnv, lnv, Rln, Rln)
              B.stt("dve", qn, psA[:, :], pp[:, PP_XQW:PP_XQW + 1], lnv, ALU.mult, ALU.mult, (psAr, R_const, Rln), (Rqn,))
              yield
              for mt in range(2):
                  psS, psSr = psf()
                  B.mm(psS[:, :], kmT[:, h, mt * 128:(mt + 1) * 128], qn, True, True, (RkmT, Rqn), (psSr,), inc=True)
                  B.act(Px[mt], psS[:, :], AF.Exp, (psSr,), (RPx[mt],), scale=128 ** -0.5)
              psN, psNr = psf()
              psD, psDr = psf()
              for mt in range(2):
                  B.mm(psN[:, :], vm[:, mt, h * 128:(h + 1) * 128], Px[mt], mt == 0, mt == 1, (Rvm[mt], RPx[mt]), (psNr,))
              for mt in range(2):
                  B.mm(psD[:, :], cb[:, CB_ONES, :], Px[mt], mt == 0, mt == 1, (R_const, RPx[mt]), (psDr,), inc=(mt == 1))
              B.op("dve", lambda e, o_=rD, i_=psD[:, :]: e.reciprocal(out=o_, in_=i_), (psDr,), (RrD,))
              B.tt("dve", mxX[:, h, tsl], psN[:, :], rD, ALU.mult, (psNr, RrD), ())

          run_pipelined([xat_iter(T, h, T * 4 + h) for T in range(4) for h in range(4)])
          B.barrier()
          if STOP == 7:
              raise _Stop()

          if DEBUG:
              B.dma("sp", "dbg", lambda e: e.dma_start(out=ddbg[s, :, 0:4, :], in_=mxA), (), ())
              B.dma("sp", "dbg", lambda e: e.dma_start(out=ddbg[s, :, 12:16, :], in_=mxX), (), ())
              for c in range(NT):
                  B.dma("sp", "dbg", lambda e, c=c: e.dma_start(out=ddbg[s, :, 4:12, c * 128:(c + 1) * 128],
                                                               in_=lo[:, c, :].rearrange("p (k t) -> p k t", k=8)), (), ())
              B.barrier()

          x1 = vf(O_HI, NT * 1024).rearrange("p (c f) -> p c f", c=NT)
          Rx1 = RL(NT)
          wo = vb(O_HT, 16 * 1024).rearrange("p (k n) -> p k n", k=16)
          Rwo = Res()
          for kh in range(2):
              wload(wo[:, kh * 8:(kh + 1) * 8, :], wout_v[:, kh * 8:(kh + 1) * 8, :], (Rwo,), "wo")
          xt = [vf(O_WK + i * 4096, 1024) for i in range(2)]
          Rxt = RL(2)
          for tt in range(NT):
              tsl = slice(tt * 128, (tt + 1) * 128)
              b = tt % 2
              B.dma("sp", f"xt{b}", lambda e, o_=xt[b], i_=dx[s, tsl, :]: e.dma_start(out=o_, in_=i_), (), (Rxt[b],))
              for hf in range(2):
                  ps, psr = psf()
                  for kc in range(16):
                      if kc < 4:
                          lhs = mxA[:, kc, tsl]
                      elif kc < 12:
                          lhs = lo[:, tt, (kc - 4) * 128:(kc - 3) * 128]
                      else:
                          lhs = mxX[:, kc - 12, tsl]
                      B.mm(ps[:, :], lhs, wo[:, kc, hf * 512:(hf + 1) * 512], kc == 0, kc == 15, (Rwo,), (psr,), inc=(kc == 15))
                  B.tt("dve", x1[:, tt, hf * 512:(hf + 1) * 512], ps[:, :], xt[b][:, hf * 512:(hf + 1) * 512], ALU.add,
                       (psr, Rxt[b]), (Rx1[tt],))
          B.barrier()
          if STOP == 8:
              raise _Stop()

          Rh2 = RL(NT)
          hbuild(lambda tt: x1[:, tt, :], NT, hT, Rh2, PP_NWMLP, O_WK, src_res=Rx1)
          Rh24 = [[Rh2[4 * T + i] for i in range(4)] for T in range(4)]
          if STOP == 8.5:
              B.barrier()
              raise _Stop()
          wu = [vb(O_MX + i * 32768, 8 * 1024).rearrange("p (k n) -> p k n", k=8) for i in range(2)]
          wd = [vb(O_MX + 16384 + i * 32768, 8 * 1024).rearrange("p (k n) -> p k n", k=8) for i in range(2)]
          Rwu, Rwd = RL(2), RL(2)
          uT = [vb(O_WK + 16384 + i * 8192, 8 * 512).rearrange("p (k t) -> p k t", k=8) for i in range(2)]
          RuT = RL(2)
          rl = [vb(O_WK + 32768 + i * 1024, 512) for i in range(2)]
          Rrl = RL(2)
          assert 32768 + 2048 <= WK_SIZE
          ui = 0

          def mlp_wload(fb_):
              wi_ = fb_ % 2
              wload(wu[wi_], wup_v[:, :, fb_ * 1024:(fb_ + 1) * 1024], (Rwu[wi_],), f"wu{wi_}")
              wload(wd[wi_], wdn_v[:, fb_ * 8:(fb_ + 1) * 8, :], (Rwd[wi_],), f"wd{wi_}")
          mlp_wload(0)
          mlp_wload(1)
          for fb in range(4):
              wi = fb % 2
              for T in range(4):
                  tsl = slice(T * 512, (T + 1) * 512)
                  u = ui % 2
                  ui += 1
                  for fc in range(8):
                      ps, psr = psf()
                      for k in range(8):
                          B.mm(ps[:, :], wu[wi][:, k, fc * 128:(fc + 1) * 128], hT[:, k, tsl], k == 0, k == 7,
                               (Rwu[wi],) + tuple(Rh24[T]), (psr,), inc=(k == 7))
                      r = fc % 2
                      B.act(rl[r], ps[:, :], AF.Relu, (psr,), (Rrl[r],))
                      B.tt("pool", uT[u][:, fc, :], rl[r], rl[r], ALU.mult, (Rrl[r],), (RuT[u],))
                  for ti in range(4):
                      tt = T * 4 + ti
                      for hf in range(2):
                          ps, psr = psf()
                          for fc in range(8):
                              B.mm(ps[:, :], uT[u][:, fc, ti * 128:(ti + 1) * 128], wd[wi][:, fc, hf * 512:(hf + 1) * 512],
                                   fc == 0, fc == 7, (RuT[u], Rwd[wi]), (psr,), inc=(fc == 7))
                          B.tt("dve", x1[:, tt, hf * 512:(hf + 1) * 512], x1[:, tt, hf * 512:(hf + 1) * 512], ps[:, :], ALU.add,
                               (Rx1[tt], psr), (Rx1[tt],))
                      if fb == 3:
                          B.dma("sp", "out", lambda e, o_=dout[s, tt * 128:(tt + 1) * 128, :], i_=x1[:, tt, :]:
                                e.dma_start(out=o_, in_=i_), (Rx1[tt],), ())
              if fb + 2 < 4:
                  mlp_wload(fb + 2)
      except _Stop:
        pass
    B.barrier()
    for k, v in list(B.cnt.items()):
        B.need("sp", (k, v))

    keys = list(B.cnt.keys())
    sems = {k: es.enter_context(nc.semaphore(f"s_{k}")) for k in keys}

    def run(e, stream):
        for it in stream:
            if it[0] == 0:
                e.wait_ge(sems[it[1]], it[2])
            else:
                ins = it[1](e)
                if it[2] is not None:
                    ins.then_inc(sems[it[2]], it[3])

    with nc.Block() as block:
        @block.tensor
        def _(e):
            run(e, B.streams["pe"])

        @block.scalar
        def _(e):
            run(e, B.streams["act"])

        @block.vector
        def _(e):
            run(e, B.streams["dve"])

        @block.gpsimd
        def _(e):
            run(e, B.streams["pool"])

        @block.sync
        def _(e):
            run(e, B.streams["sp"])
    es.close()
    return nc


def make_consts():
    j = np.arange(128)[:, None]
    l = np.arange(128)[None, :]
    cbm = np.zeros((128, NCB, 128), np.float32)
    cbm[:, CB_ID] = (j == l)
    cbm[:, CB_ONES] = 1.0
    cbm[:, CB_BLK] = (j // 64 == l // 64)
    cbm[:, CB_OZ0] = (l < 64)
    cbm[:, CB_OZ1] = (l >= 64)
    cbm[:, CB_LSF] = (j > l)
    cbm[:, CB_LSB] = (j < l)
    rot = np.zeros((128, 128), np.float32)
    for hb in (0, 64):
        for d in range(8):
            rot[hb + d + 8, hb + d] = -1.0
            rot[hb + d, hb + d + 8] = 1.0
    cbm[:, CB_ROT] = rot
    cbm[:, CB_MPREV] = (j >= l)
    cbm[:, CB_MNEXT] = (j <= l)
    cbm[:, CB_MF] = (l >= j)
    cbm[:, CB_MB] = (l <= j)
    cfm = np.zeros((128, NCF, 128), np.float32)
    cfm[:, CF_TUI] = (j <= l)
    cfm[:, CF_TLI] = (j >= l)
    cfm[:, CF_TUS] = (j > l)
    cfm[:, CF_TLS] = (j < l)
    cfm[:, CF_ONES] = 1.0
    cfm[:, CF_ID] = (j == l)
    inv = 500000.0 ** (-np.arange(0, 16, 2, dtype=np.float32) / 16)
    t = np.arange(L, dtype=np.float32)
    ang = t[None, :] * inv[:, None]
    cs = np.zeros((128, 2, L), np.float32)
    cs[:, 0, :] = 1.0
    for p in range(128):
        d = p % 64
        if d < 16:
            cs[p, 0] = np.cos(ang[d % 8])
            cs[p, 1] = np.sin(ang[d % 8])
    return cbm.astype(ml_dtypes.bfloat16), cfm, cs


def pack_params(inp):
    pp = np.zeros((128, NPP), np.float32)
    p = np.arange(128)
    pp[:, PP_NWMIX:PP_NWMIX + 8] = inp["norm_mix_w"][0].reshape(8, 128).T
    pp[:, PP_NWMLP:PP_NWMLP + 8] = inp["norm_mlp_w"][0].reshape(8, 128).T
    pp[:, PP_NWMEM:PP_NWMEM + 8] = inp["mem_norm_w"][0].reshape(8, 128).T
    pp[:, PP_SSDNW:PP_SSDNW + 8] = inp["ssd_norm_w"][0].reshape(8, 128).T
    cw = inp["conv_w"][0]
    pp[:, PP_CONVW:PP_CONVW + 60] = cw.reshape(5, 12, 128).transpose(2, 1, 0).reshape(128, 60)
    pp[:, PP_CONVB:PP_CONVB + 12] = inp["conv_b"][0].reshape(12, 128).T
    pp[:, PP_QW] = inp["q_norm_w"][0][p % 64]
    pp[:, PP_KW] = inp["k_norm_w"][0][p % 64]
    pp[:, PP_XQW] = inp["xq_norm_w"][0]
    for c in range(4):
        pp[:, PP_SINK + c] = inp["attn_sink"][0][2 * c + p // 64]
    pp[:, PP_DTB:PP_DTB + 16] = inp["dt_bias_f"][0][None, :]
    pp[:, PP_DTB + 16:PP_DTB + 32] = inp["dt_bias_b"][0][None, :]
    pp[:, PP_ALOG:PP_ALOG + 16] = inp["a_log_f"][0][None, :]
    pp[:, PP_ALOG + 16:PP_ALOG + 32] = inp["a_log_b"][0][None, :]
    pp[:, PP_SSDD:PP_SSDD + 16] = inp["ssd_d"][0][None, :]
    pp[:, PP_XKW:PP_XKW + 128] = inp["xk_norm_w"][0][None, :]
    pp[:, PP_M0] = (p < 64)
    pp[:, PP_M1] = (p >= 64)
    return pp


_NC_CACHE = {}


def kernel(**inputs):
    inp = {k: np.asarray(v) for k, v in inputs.items()}
    if "nc" not in _NC_CACHE:
        _NC_CACHE["nc"] = build_program()
    nc = _NC_CACHE["nc"]
    cbm, cfm, cs = make_consts()
    pp = pack_params(inp)
    shared = {
        "w_in": np.ascontiguousarray(inp["w_in"][0]),
        "w_mem_kv": np.ascontiguousarray(inp["w_mem_kv"][0]),
        "w_out": np.ascontiguousarray(inp["w_out"][0]),
        "w_up": np.ascontiguousarray(inp["w_mlp_up"][0]),
        "w_down": np.ascontiguousarray(inp["w_mlp_down"][0]),
        "cbf": cbm, "cf32": cfm, "pp": pp, "cossin": cs,
    }
    in_maps = []
    for c in range(NCORES):
        m = dict(shared)
        m["x"] = np.ascontiguousarray(inp["x"][c * SEQ_PER_CORE:(c + 1) * SEQ_PER_CORE])
        m["mem"] = np.ascontiguousarray(inp["mem"][c * SEQ_PER_CORE:(c + 1) * SEQ_PER_CORE])
        in_maps.append(m)
    res = run_bass_kernel_spmd(nc, in_maps, core_ids=list(range(NCORES)))
    out = np.concatenate([np.asarray(r["out"]) for r in res.results], axis=0)
    return out.astype(np.float32)
```

```python
import numpy as np
import ml_dtypes
from contextlib import ExitStack
import concourse.bass as bass
import concourse.mybir as mybir
from concourse.bass_utils import run_bass_kernel_spmd

F32 = mybir.dt.float32
BF16 = mybir.dt.bfloat16
AF = mybir.ActivationFunctionType
ALU = mybir.AluOpType

NCORES = 8
SEQ_PER_CORE = 2
L = 2048
D = 1024
NT = 16
EPS = 1e-6
DEBUG = False
STOP = 99


class _Stop(Exception):
    pass
NSEQ_RUN = SEQ_PER_CORE

C_Q, C_K, C_V, C_Z, C_XBC, C_DT, C_QX = 0, 512, 640, 768, 1792, 3328, 3360

(CB_ID, CB_ONES, CB_BLK, CB_OZ0, CB_OZ1, CB_LSF, CB_LSB, CB_ROT, CB_MPREV, CB_MNEXT, CB_MF, CB_MB) = range(12)
NCB = 12
(CF_TUI, CF_TLI, CF_TUS, CF_TLS, CF_ONES, CF_ID) = range(6)
NCF = 6
PP_NWMIX, PP_NWMLP, PP_NWMEM, PP_SSDNW = 0, 8, 16, 24
PP_CONVW = 32
PP_CONVB = 92
PP_QW, PP_KW, PP_XQW = 104, 105, 106
PP_SINK = 107
PP_DTB = 111
PP_ALOG = 143
PP_SSDD = 175
PP_XKW = 191
PP_M0, PP_M1 = 320, 321
NPP = 322

ENG = ["pe", "act", "dve", "pool", "sp"]


class Res:
    __slots__ = ("w", "r")

    def __init__(self):
        self.w = None
        self.r = {}


def RL(n):
    return [Res() for _ in range(n)]


class Bld:
    def __init__(self):
        self.streams = {e: [] for e in ENG}
        self.cnt = {}
        self.known = {e: {} for e in ENG}
        self.psi = 0

    def need(self, eng, tok, skip_same=False):
        if tok is None:
            return
        k, v = tok
        if skip_same and k == eng:
            return
        if self.known[eng].get(k, 0) >= v:
            return
        self.known[eng][k] = v
        self.streams[eng].append((0, k, v))

    def op(self, eng, fn, reads=(), writes=(), inc=True):
        for r in reads:
            self.need(eng, r.w)
        for w in writes:
            self.need(eng, w.w, True)
            for k, v in w.r.items():
                self.need(eng, (k, v), True)
        c = self.cnt.get(eng, 0) + 1
        if inc:
            self.cnt[eng] = c
        self.streams[eng].append((1, fn, eng if inc else None, 1))
        for r in reads:
            r.r[eng] = c
        for w in writes:
            w.w = (eng, c)
            w.r = {}

    def dma(self, q, chan, fn, reads=(), writes=()):
        for r in reads:
            self.need(q, r.w)
        for w in writes:
            if not (w.w is not None and w.w[0] == chan):
                self.need(q, w.w)
            for k, v in w.r.items():
                self.need(q, (k, v))
        c = self.cnt.get(chan, 0) + 16
        self.cnt[chan] = c
        self.streams[q].append((1, fn, chan, 16))
        for r in reads:
            r.r[chan] = c
        for w in writes:
            w.w = (chan, c)
            w.r = {}

    def barrier(self):
        toks = list(self.cnt.items())
        for e in ENG:
            for t in toks:
                self.need(e, t, True)

    def mm(self, out, lhsT, rhs, start, stop, reads, writes, inc=False):
        self.op("pe", lambda e: e.matmul(out, lhsT=lhsT, rhs=rhs, start=start, stop=stop), reads, writes, inc)

    def tr(self, out, in_, ident, reads, writes, inc=False):
        self.op("pe", lambda e: e.transpose(out=out, in_=in_, identity=ident), reads, writes, inc)

    def act(self, out, in_, func, reads, writes, scale=1.0, bias=0.0, accum=None):
        if accum is None:
            self.op("act", lambda e: e.activation(out=out, in_=in_, func=func, bias=bias, scale=scale), reads, writes)
        else:
            self.op("act", lambda e: e.activation(out=out, in_=in_, func=func, bias=bias, scale=scale,
                                                  accum_out=accum), reads, writes)

    def tt(self, eng, out, in0, in1, op, reads, writes):
        self.op(eng, lambda e: e.tensor_tensor(out=out, in0=in0, in1=in1, op=op), reads, writes)

    def ts(self, eng, out, in0, s1, s2, op0, op1, reads, writes):
        if s2 is None:
            self.op(eng, lambda e: e.tensor_scalar(out=out, in0=in0, scalar1=s1, scalar2=None, op0=op0), reads, writes)
        else:
            self.op(eng, lambda e: e.tensor_scalar(out=out, in0=in0, scalar1=s1, scalar2=s2, op0=op0, op1=op1),
                    reads, writes)

    def stt(self, eng, out, in0, scalar, in1, op0, op1, reads, writes):
        self.op(eng, lambda e: e.scalar_tensor_tensor(out=out, in0=in0, scalar=scalar, in1=in1, op0=op0, op1=op1),
                reads, writes)

    def cp(self, eng, out, in_, reads, writes):
        if eng == "act":
            self.op("act", lambda e: e.activation(out=out, in_=in_, func=AF.Copy), reads, writes)
        else:
            self.op(eng, lambda e: e.tensor_copy(out=out, in_=in_), reads, writes)

    def ms(self, eng, out, val, reads, writes):
        self.op(eng, lambda e: e.memset(out, val), reads, writes)


def run_pipelined(gens):
    prev = None
    for g in gens:
        next(g)
        if prev is not None:
            for _ in prev:
                pass
        prev = g
    if prev is not None:
        for _ in prev:
            pass


def build_program():
    nc = bass.Bass("TRN2", target_bir_lowering=False)
    dx = nc.dram_tensor("x", [SEQ_PER_CORE, L, D], F32, kind="ExternalInput").ap()
    dmem = nc.dram_tensor("mem", [SEQ_PER_CORE, 256, D], F32, kind="ExternalInput").ap()
    dwin = nc.dram_tensor("w_in", [D, 3872], F32, kind="ExternalInput").ap()
    dwkv = nc.dram_tensor("w_mem_kv", [D, 1024], F32, kind="ExternalInput").ap()
    dwout = nc.dram_tensor("w_out", [2048, D], F32, kind="ExternalInput").ap()
    dwup = nc.dram_tensor("w_up", [D, 4096], F32, kind="ExternalInput").ap()
    dwdn = nc.dram_tensor("w_down", [4096, D], F32, kind="ExternalInput").ap()
    dcb = nc.dram_tensor("cbf", [128, NCB, 128], BF16, kind="ExternalInput").ap()
    dcf = nc.dram_tensor("cf32", [128, NCF, 128], F32, kind="ExternalInput").ap()
    dpp = nc.dram_tensor("pp", [128, NPP], F32, kind="ExternalInput").ap()
    dcs = nc.dram_tensor("cossin", [128, 2, L], F32, kind="ExternalInput").ap()
    dout = nc.dram_tensor("out", [SEQ_PER_CORE, L, D], F32, kind="ExternalOutput").ap()
    if DEBUG:
        ddbg = nc.dram_tensor("dbg", [SEQ_PER_CORE, 128, 16, L], BF16, kind="ExternalOutput").ap()

    win_v = dwin.rearrange("(k p) n -> p k n", p=128)
    wkv_v = dwkv.rearrange("(k p) n -> p k n", p=128)
    wout_v = dwout.rearrange("(k p) n -> p k n", p=128)
    wup_v = dwup.rearrange("(k p) n -> p k n", p=128)
    wdn_v = dwdn.rearrange("(k p) n -> p k n", p=128)

    B = Bld()
    es = ExitStack()
    ARENA_ELEMS = 106400
    arena = es.enter_context(nc.sbuf_tensor("arena", [128, ARENA_ELEMS], BF16))
    PF = [es.enter_context(nc.psum_tensor(f"pf{i}", [128, 512], F32)) for i in range(6)]
    PB = [es.enter_context(nc.psum_tensor(f"pb{i}", [128, 1024], BF16)) for i in range(2)]
    PFR = RL(6)
    PBR = RL(2)
    pst = {"f": 0, "b": 0, "n": 6}

    def psf():
        i = pst["f"] % pst["n"]
        pst["f"] = (i + 1) % pst["n"]
        return PF[i], PFR[i]

    def psb():
        i = pst["b"]
        pst["b"] = (i + 1) % 2
        return PB[i], PBR[i]

    def vb(off, n):
        assert off % 4 == 0 and off // 2 + n <= ARENA_ELEMS, (off, n)
        return arena[:, off // 2: off // 2 + n]

    def vf(off, n):
        assert off % 4 == 0 and off // 2 + 2 * n <= ARENA_ELEMS, (off, n)
        return arena[:, off // 2: off // 2 + 2 * n].bitcast(F32)

    o = 0
    O_CB = o; o += NCB * 256
    O_CF = o; o += NCF * 512
    O_PP = o; o += NPP * 4
    O_SM = o; o += 1024
    O_HT = o; o += 32768
    O_MX = o; o += 32768
    O_LO = o; o += 32768
    O_HI = o; o += 65536
    O_WK = o
    WK_SIZE = ARENA_ELEMS * 2 - O_WK
    assert WK_SIZE >= 33000, WK_SIZE

    cb = vb(O_CB, NCB * 128).rearrange("p (m n) -> p m n", m=NCB)
    cf = vf(O_CF, NCF * 128).rearrange("p (m n) -> p m n", m=NCF)
    pp = vf(O_PP, NPP)
    esink = vf(O_SM, 4)
    aneg = vf(O_SM + 16, 32)
    R_const = Res()

    hT = vb(O_HT, 8 * L).rearrange("p (k t) -> p k t", k=8)
    mxA = vb(O_MX, 4 * L).rearrange("p (k t) -> p k t", k=4)
    mxX = vb(O_MX + 16384, 4 * L).rearrange("p (k t) -> p k t", k=4)
    lo = vb(O_LO, NT * 1024).rearrange("p (c f) -> p c f", c=NT)

    ident = cb[:, CB_ID, :]

    B.dma("sp", "c0", lambda e: e.dma_start(out=cb, in_=dcb), (), (R_const,))
    B.dma("sp", "c0", lambda e: e.dma_start(out=cf, in_=dcf), (), (R_const,))
    B.dma("sp", "c0", lambda e: e.dma_start(out=pp, in_=dpp), (), (R_const,))
    B.act(esink, pp[:, PP_SINK:PP_SINK + 4], AF.Exp, (R_const,), (R_const,))
    B.act(aneg, pp[:, PP_ALOG:PP_ALOG + 32], AF.Exp, (R_const,), (R_const,))
    B.ts("dve", aneg, aneg, -1.0, None, ALU.mult, None, (R_const,), (R_const,))

    def wload(dst, src, reads_w, chan):
        B.dma("pool", chan, lambda e: e.dma_start(out=dst, in_=src), (), reads_w)

    def hbuild(src_fn, ntiles, dstT, dst_res, nwcol, wk_off, src_res=None):
        xt = [vf(wk_off + i * 4096, 1024) for i in range(2)]
        xn = [vb(wk_off + 8192 + i * 2048, 1024) for i in range(2)]
        junk = vb(wk_off + 12288, 1024)
        st = vf(wk_off + 14336, 3 * ntiles)
        Rxt, Rxn, Rj, Rst = RL(2), RL(2), Res(), Res()
        B.ms("dve", st, 0.0, (), (Rst,))
        def hb_iter(tt):
            b = tt % 2
            if src_res is None:
                src = src_fn(tt)
                B.dma("sp", f"xt{b}", lambda e, o_=xt[b], i_=src: e.dma_start(out=o_, in_=i_), (), (Rxt[b],))
                xin, rin = xt[b], Rxt[b]
            else:
                xin, rin = src_fn(tt), src_res[tt]
            B.act(junk, xin, AF.Square, (rin, Rst), (Rj, Rst), accum=st[:, 3 * tt:3 * tt + 1])
            B.act(st[:, 3 * tt + 1:3 * tt + 2], st[:, 3 * tt:3 * tt + 1], AF.Ln, (Rst,), (Rst,), scale=1.0 / D, bias=EPS)
            B.act(st[:, 3 * tt + 2:3 * tt + 3], st[:, 3 * tt + 1:3 * tt + 2], AF.Exp, (Rst,), (Rst,), scale=-0.5)
            B.ts("dve", xn[b], xin, st[:, 3 * tt + 2:3 * tt + 3], None, ALU.mult, None, (rin, Rst), (Rxn[b],))
            yield
            pb, pbr = psb()
            for k in range(8):
                B.tr(pb[:, k * 128:(k + 1) * 128], xn[b][:, k * 128:(k + 1) * 128], ident, (Rxn[b], R_const), (pbr,),
                     inc=(k == 7))
            B.tt("dve", dstT[:, :, tt * 128:(tt + 1) * 128], pb[:, :].rearrange("p (k t) -> p k t", k=8),
                 pp[:, nwcol:nwcol + 8].unsqueeze(2).to_broadcast([128, 8, 128]), ALU.mult,
                 (pbr, R_const), (dst_res[tt],))

        run_pipelined([hb_iter(tt) for tt in range(ntiles)])

    def rstd_from_ps(psB, psBr, n_feat, lnv, rstd, Rln, Rrs):
        B.act(lnv, psB, AF.Ln, (psBr,), (Rln,), scale=1.0 / n_feat, bias=EPS)
        B.act(rstd, lnv, AF.Exp, (Rln,), (Rrs,), scale=-0.5)

    for s in range(NSEQ_RUN):
      try:
          B.barrier()
          RhT = RL(NT)
          hbuild(lambda tt: dx[s, tt * 128:(tt + 1) * 128, :], NT, hT, RhT, PP_NWMIX, O_WK)
          RhT4 = [[RhT[4 * T + i] for i in range(4)] for T in range(4)]
          B.barrier()
          if STOP == 1:
              raise _Stop()

          Rlo = RL(NT)
          prevb = vb(O_HI, NT * 1024).rearrange("p (c f) -> p c f", c=NT)
          BT = vb(O_HI + 32768, 2 * L).rearrange("p (g t) -> p g t", g=2)
          CT = vb(O_HI + 40960, 2 * L).rearrange("p (g t) -> p g t", g=2)
          Btok = vb(O_HI + 49152, NT * 256).rearrange("p (c f) -> p c f", c=NT)
          Rprevb, RBT, RCT, RBtok = RL(NT), RL(NT), RL(NT), RL(NT)
          Wz = vb(O_MX, 8 * 1024).rearrange("p (k n) -> p k n", k=8)
          RWz = Res()
          dtall = vf(O_MX + 16384, NT * 32).rearrange("p (c f) -> p c f", c=NT)
          aall = vf(O_MX + 18432, NT * 32).rearrange("p (c f) -> p c f", c=NT)
          exall = vf(O_MX + 20480, NT * 64).rearrange("p (c f) -> p c f", c=NT)
          cdec = vf(O_MX + 24576, NT * 32).rearrange("p (c f) -> p c f", c=NT)
          Rst = RL(NT)
          DI = vb(O_MX + 26624, 16 * 128).rearrange("p (h n) -> p h n", h=16)
          RDI = Res()
          wb = [vb(O_WK + i * 8192, 8 * 512).rearrange("p (k n) -> p k n", k=8) for i in range(2)]
          Rwb = RL(2)
          pre = [vb(O_WK + 16384 + i * 4112, 2052) for i in range(2)]
          Rpre = RL(2)
          actT = [vb(O_WK + 24640 + i * 4096, 2048) for i in range(2)]
          RactT = RL(2)
          diag = vb(O_HI, 5 * 128 * 2).rearrange("p (b k n) -> p b k n", b=2, k=5)
          Rdiag = RL(2)
          tmp32 = vf(O_HI + 4096, 64)
          Rtmp = Res()

          wload(Wz, win_v[:, :, C_Z:C_Z + 1024], (RWz,), "wz")
          for h in range(16):
              B.ts("dve", DI[:, h, :], cf[:, CF_ID, :], pp[:, PP_SSDD + h:PP_SSDD + h + 1], None, ALU.mult, None,
                   (R_const,), (RDI,))

          wdt = vb(O_HI + 8192, 8 * 32).rearrange("p (k n) -> p k n", k=8)
          Rwdt = Res()
          wload(wdt, win_v[:, :, C_DT:C_DT + 32], (Rwdt,), "wdt")
          for c in range(NT):
              ps, psr = psf()
              for k in range(8):
                  B.mm(ps[:, 0:32], hT[:, k, c * 128:(c + 1) * 128], wdt[:, k, :], k == 0, k == 7, (RhT[c], Rwdt), (psr,),
                       inc=(k == 7))
              B.tt("dve", tmp32[:, 0:32], ps[:, 0:32], pp[:, PP_DTB:PP_DTB + 32], ALU.add, (psr, R_const), (Rtmp,))
              B.act(tmp32[:, 32:64], tmp32[:, 0:32], AF.Exp, (Rtmp,), (Rtmp,))
              B.act(dtall[:, c, :], tmp32[:, 32:64], AF.Ln, (Rtmp,), (Rst[c],), bias=1.0)
              B.tt("dve", aall[:, c, :], dtall[:, c, :], aneg, ALU.mult, (Rst[c], R_const), (Rst[c],))
              ps, psr = psf()
              B.mm(ps[:, 0:16], cf[:, CF_TUI, :], aall[:, c, 0:16], True, True, (R_const, Rst[c]), (psr,))
              B.mm(ps[:, 16:32], cf[:, CF_TLI, :], aall[:, c, 16:32], True, True, (R_const, Rst[c]), (psr,))
              B.mm(ps[:, 32:48], cf[:, CF_TUS, :], aall[:, c, 0:16], True, True, (R_const, Rst[c]), (psr,))
              B.mm(ps[:, 48:64], cf[:, CF_TLS, :], aall[:, c, 16:32], True, True, (R_const, Rst[c]), (psr,))
              B.mm(ps[:, 64:96], cf[:, CF_ONES, :], aall[:, c, :], True, True, (R_const, Rst[c]), (psr,), inc=True)
              B.act(exall[:, c, :], ps[:, 0:64], AF.Exp, (psr,), (Rst[c],))
              B.act(cdec[:, c, :], ps[:, 64:96], AF.Exp, (psr,), (Rst[c],))

          def xbc_iter(j):
              blk, jj = j // 4, j % 4
              wbi = blk % 2
              if jj == 0:
                  wload(wb[wbi], win_v[:, :, C_XBC + blk * 512:C_XBC + (blk + 1) * 512], (Rwb[wbi],), f"wb{wbi}")
              pb_ = j % 2
              if j < 2:
                  B.ms("dve", pre[pb_][:, 0:2], 0.0, (), (Rpre[pb_],))
                  B.ms("dve", pre[pb_][:, 2050:2052], 0.0, (), (Rpre[pb_],))
              for tap in range(5):
                  B.ts("dve", diag[:, pb_, tap, :], cf[:, CF_ID, :],
                       pp[:, PP_CONVW + j * 5 + tap:PP_CONVW + j * 5 + tap + 1], None, ALU.mult, None,
                       (R_const,), (Rdiag[pb_],))
              for T in range(4):
                  ps, psr = psf()
                  for k in range(8):
                      B.mm(ps[:, :], wb[wbi][:, k, jj * 128:(jj + 1) * 128], hT[:, k, T * 512:(T + 1) * 512], k == 0, k == 7,
                           (Rwb[wbi],) + tuple(RhT4[T]), (psr,), inc=(k == 7))
                  B.cp("act", pre[pb_][:, 2 + T * 512:2 + (T + 1) * 512], ps[:, :], (psr,), (Rpre[pb_],))
              yield
              if j < 8:
                  dstF, dres = actT[pb_], None
              elif j < 10:
                  dstF = BT[:, j - 8, :]
              else:
                  dstF = CT[:, j - 10, :]
              for T in range(4):
                  ps, psr = psf()
                  for tap in range(5):
                      B.mm(ps[:, :], diag[:, pb_, tap, :], pre[pb_][:, T * 512 + tap:T * 512 + tap + 512], tap == 0, tap == 4,
                           (Rdiag[pb_], Rpre[pb_]), (psr,), inc=(tap == 4))
                  if j < 8:
                      wr = (RactT[pb_],)
                  elif j < 10:
                      wr = tuple(RBT[4 * T:4 * T + 4])
                  else:
                      wr = tuple(RCT[4 * T:4 * T + 4])
                  B.act(dstF[:, T * 512:(T + 1) * 512], ps[:, :], AF.Silu, (psr, R_const), wr,
                        bias=pp[:, PP_CONVB + j:PP_CONVB + j + 1])
              if j < 10:
                  for q4 in range(4):
                      pb, pbr = psb()
                      for i in range(4):
                          c = q4 * 4 + i
                          rd = (RactT[pb_], R_const) if j < 8 else (RBT[c], R_const)
                          B.tr(pb[:, i * 128:(i + 1) * 128], dstF[:, c * 128:(c + 1) * 128], ident, rd, (pbr,), inc=(i == 3))
                      if j < 8:
                          B.cp("act", lo[:, q4 * 4:(q4 + 1) * 4, j * 128:(j + 1) * 128],
                               pb[:, 0:512].rearrange("p (c f) -> p c f", c=4), (pbr,), tuple(Rlo[q4 * 4:q4 * 4 + 4]))
                      else:
                          B.cp("act", Btok[:, q4 * 4:(q4 + 1) * 4, (j - 8) * 128:(j - 7) * 128],
                               pb[:, 0:512].rearrange("p (c f) -> p c f", c=4), (pbr,), tuple(RBtok[q4 * 4:q4 * 4 + 4]))
          run_pipelined([xbc_iter(j) for j in range(12)])
          B.barrier()
          if STOP == 3:
              raise _Stop()

          w0 = O_WK
          Hs = vf(w0, 1024); w0 += 4096
          xd = [vb(w0 + i * 2048, 1024) for i in range(2)]; w0 += 4096
          wsm = vf(w0, 64); w0 += 256
          Ebuf = vb(w0, 4096).rearrange("p (q n) -> p q n", q=8); w0 += 8192
          rhsb2 = vb(w0, 2048); rhsb = rhsb2.rearrange("p (h n) -> p h n", h=16); w0 += 4096
          xdt = [vb(w0 + i * 2048, 1024) for i in range(2)]; w0 += 4096
          cbm = vb(w0, 512).rearrange("p (g d n) -> p g d n", g=2, d=2); w0 += 1024
          prevf = vb(w0, 1024); w0 += 2048
          szb = vb(w0, 1024); w0 += 2048
          t1 = vf(w0, 1024); w0 += 4096
          t2 = vf(w0, 1024); w0 += 4096
          gst = vf(w0, 8); w0 += 32
          ynb = vb(w0, 1024); w0 += 2048
          assert w0 - O_WK <= WK_SIZE, (w0 - O_WK, WK_SIZE)
          RH, Rxd, Rws, RE, Rrhs, Rxdt, Rcbm, Rpf, Rsz, Rt1, Rt2, Rg, Ryn = (Res(), RL(2), Res(), RL(8), Res(), RL(2),
                                                                              Res(), Res(), Res(), Res(), Res(), Res(), Res())

          def bc16(ap16):
              return ap16.unsqueeze(2).to_broadcast([128, 16, 64])

          def v3(ap1024):
              return ap1024.rearrange("p (h d) -> p h d", h=16)

          def state_prep(c, d, wcol, ecol, banks=None):
              B.tt("dve", wsm[:, d * 16:(d + 1) * 16], dtall[:, c, wcol:wcol + 16], exall[:, c, ecol:ecol + 16], ALU.mult,
                   (Rst[c],), (Rws,))
              B.tt("dve", v3(xd[d]), v3(lo[:, c, :]), bc16(wsm[:, d * 16:(d + 1) * 16]), ALU.mult, (Rlo[c], Rws), (Rxd[d],))
              pss = []
              for g in range(2):
                  ps, psr = psf() if banks is None else (PF[banks[g]], PFR[banks[g]])
                  B.mm(ps[:, :], Btok[:, c, g * 128:(g + 1) * 128], xd[d][:, g * 512:(g + 1) * 512], True, True,
                       (RBtok[c], Rxd[d]), (psr,), inc=True)
                  pss.append((ps, psr))
              return pss

          def state_apply(c, dcol, pss):
              B.tt("dve", v3(Hs), v3(Hs), bc16(cdec[:, c, dcol:dcol + 16]), ALU.mult, (RH, Rst[c]), (RH,))
              for g in range(2):
                  B.tt("dve", Hs[:, g * 512:(g + 1) * 512], Hs[:, g * 512:(g + 1) * 512], pss[g][0][:, :], ALU.add,
                       (RH, pss[g][1]), (RH,))

          B.ms("dve", Hs, 0.0, (), (RH,))

          def p1_iter(c):
              pss = state_prep(c, 1, 16, 48) if c > 0 else None
              yield
              B.cp("act", prevb[:, c, :], Hs, (RH,), (Rprevb[c],))
              if c > 0:
                  state_apply(c, 16, pss)

          run_pipelined([p1_iter(c) for c in range(NT - 1, -1, -1)])

          E2 = [Ebuf, vb(O_HI + 57344, 4096).rearrange("p (q n) -> p q n", q=8)]
          RE2 = [RE, RL(8)]
          xdt2 = [xdt, [vb(O_WK + 4096 + 2048, 1024), vb(O_MX + 30720, 1024)]]
          Rxdt2 = [Rxdt, [Rxd[1], Res()]]
          B.ms("dve", Hs, 0.0, (), (RH,))

          s1rot = [0]
          Rdm = Res()

          def s1bank():
              i = 4 + s1rot[0] % 2
              s1rot[0] += 1
              return PF[i], PFR[i]

          def p2_s1(c):
              tsl = slice(c * 128, (c + 1) * 128)
              pi = c % 2
              Eb, REb, xdtb, Rxdtb = E2[pi], RE2[pi], xdt2[pi], Rxdt2[pi]
              for g in range(2):
                  ps, psr = s1bank()
                  B.mm(ps[:, 0:128], BT[:, g, tsl], CT[:, g, tsl], True, True, (RBT[c], RCT[c]), (psr,), inc=True)
                  B.tt("dve", cbm[:, g, :, :], ps[:, 0:128].unsqueeze(1).to_broadcast([128, 2, 128]),
                       cb[:, CB_MF:CB_MF + 2, :], ALU.mult, (psr, R_const), (Rcbm,))
              for d in range(2):
                  B.tt("pool", v3(xdtb[d]), v3(lo[:, c, :]), bc16(dtall[:, c, d * 16:(d + 1) * 16]), ALU.mult,
                       (Rlo[c], Rst[c]), (Rxdtb[d],))
              for d in range(2):
                  tri = cf[:, CF_TUI, :] if d == 0 else cf[:, CF_TLI, :]
                  lsm = cb[:, CB_LSF, :] if d == 0 else cb[:, CB_LSB, :]
                  B.tt("pool", rhsb, aall[:, c, d * 16:(d + 1) * 16].unsqueeze(2).to_broadcast([128, 16, 128]),
                       tri.unsqueeze(1).to_broadcast([128, 16, 128]), ALU.mult, (Rst[c], R_const), (Rrhs,))
                  for q in range(4):
                      ps, psr = s1bank()
                      B.mm(ps[:, :], lsm, rhsb2[:, q * 512:(q + 1) * 512], True, True, (R_const, Rrhs), (psr,), inc=True)
                      qi = d * 4 + q
                      B.act(Eb[:, qi, :], ps[:, :], AF.Exp, (psr,), (REb[qi],))
                  yield
                  for q in range(4):
                      qi = d * 4 + q
                      g = q // 2
                      B.tt("dve", Eb[:, qi, :].rearrange("p (h n) -> p h n", h=4),
                           Eb[:, qi, :].rearrange("p (h n) -> p h n", h=4),
                           cbm[:, g, d, :].unsqueeze(1).to_broadcast([128, 4, 128]), ALU.mult, (REb[qi], Rcbm), (REb[qi],))

          def p2_s2(c):
              tsl = slice(c * 128, (c + 1) * 128)
              pi = c % 2
              Eb, REb, xdtb, Rxdtb = E2[pi], RE2[pi], xdt2[pi], Rxdt2[pi]
              B.cp("act", prevf, Hs, (RH,), (Rpf,))
              for g in range(2):
                  ps, psr = PF[2 + g], PFR[2 + g]
                  B.mm(ps[:, :], CT[:, g, tsl], prevf[:, g * 512:(g + 1) * 512], True, True, (RCT[c], Rpf), (psr,), inc=True)
                  B.tt("dve", v3(t1)[:, g * 8:(g + 1) * 8, :], ps[:, :].rearrange("p (h d) -> p h d", h=8),
                       exall[:, c, g * 8:(g + 1) * 8].unsqueeze(2).to_broadcast([128, 8, 64]), ALU.mult,
                       (psr, Rst[c]), (Rt1,))
              yps = []
              for hf in range(2):
                  ps, psr = PF[hf], PFR[hf]
                  for h8 in range(8):
                      h = hf * 8 + h8
                      osl = ps[:, h8 * 64:(h8 + 1) * 64]
                      B.mm(osl, Eb[:, h // 4, (h % 4) * 128:(h % 4 + 1) * 128], xdtb[0][:, h * 64:(h + 1) * 64], True, False,
                           (REb[h // 4], Rxdtb[0]), (psr,))
                      B.mm(osl, Eb[:, 4 + h // 4, (h % 4) * 128:(h % 4 + 1) * 128], xdtb[1][:, h * 64:(h + 1) * 64], False,
                           False, (REb[4 + h // 4], Rxdtb[1]), (psr,))
                      B.mm(osl, DI[:, h, :], lo[:, c, h * 64:(h + 1) * 64], False, True, (RDI, Rlo[c]), (psr,), inc=(h8 == 7))
                  yps.append((ps, psr))
              for g in range(2):
                  ps, psr = PF[2 + g], PFR[2 + g]
                  B.mm(ps[:, :], CT[:, g, tsl], prevb[:, c, g * 512:(g + 1) * 512], True, True, (RCT[c], Rprevb[c]), (psr,),
                       inc=True)
                  B.tt("dve", v3(t2)[:, g * 8:(g + 1) * 8, :], ps[:, :].rearrange("p (h d) -> p h d", h=8),
                       exall[:, c, 16 + g * 8:16 + (g + 1) * 8].unsqueeze(2).to_broadcast([128, 8, 64]), ALU.mult,
                       (psr, Rst[c]), (Rt2,))
              yield
              for hf in range(2):
                  ps, psr = PF[2 + hf], PFR[2 + hf]
                  for k in range(8):
                      B.mm(ps[:, :], hT[:, k, tsl], Wz[:, k, hf * 512:(hf + 1) * 512], k == 0, k == 7, (RhT[c], RWz), (psr,),
                           inc=(k == 7))
                  B.act(szb[:, hf * 512:(hf + 1) * 512], ps[:, :], AF.Silu, (psr,), (Rsz,))
              B.act(gst[:, 7:8], cf[:, CF_ONES, 0:1], AF.Ln, (R_const,), (Rdm,))
              B.tt("dve", t1, t1, t2, ALU.add, (Rt1, Rt2), (Rt1,))
              for hf in range(2):
                  B.tt("dve", t1[:, hf * 512:(hf + 1) * 512], t1[:, hf * 512:(hf + 1) * 512], yps[hf][0][:, :], ALU.add,
                       (Rt1, yps[hf][1]), (Rt1,))
              yield
              B.tt("dve", t1, t1, szb, ALU.mult, (Rt1, Rsz), (Rt1,))
              B.ms("dve", gst[:, 0:6], 0.0, (), (Rg,))
              for g in range(2):
                  B.act(t2[:, g * 512:(g + 1) * 512], t1[:, g * 512:(g + 1) * 512], AF.Square, (Rt1, Rg), (Rt2, Rg),
                        accum=gst[:, g:g + 1])
              B.act(gst[:, 2:4], gst[:, 0:2], AF.Ln, (Rg,), (Rg,), scale=1.0 / 512, bias=EPS)
              B.act(gst[:, 4:6], gst[:, 2:4], AF.Exp, (Rg,), (Rg,), scale=-0.5)
              for g in range(2):
                  B.ts("dve", ynb[:, g * 512:(g + 1) * 512], t1[:, g * 512:(g + 1) * 512], gst[:, 4 + g:5 + g], None, ALU.mult,
                       None, (Rt1, Rg), (Ryn,))
              yield
              if c < NT - 1:
                  pss = state_prep(c, 0, 0, 32, banks=(2, 3))
                  state_apply(c, 0, pss)
              pb, pbr = psb()
              for k in range(8):
                  B.tr(pb[:, k * 128:(k + 1) * 128], ynb[:, k * 128:(k + 1) * 128], ident, (Ryn, R_const), (pbr,), inc=(k == 7))
              B.tt("dve", lo[:, c, :].rearrange("p (k t) -> p k t", k=8), pb[:, :].rearrange("p (k t) -> p k t", k=8),
                   pp[:, PP_SSDNW:PP_SSDNW + 8].unsqueeze(2).to_broadcast([128, 8, 128]), ALU.mult,
                   (pbr, R_const), (Rlo[c],))

          for _ in p2_s1(0):
              pass
          for c in range(NT):
              g2_ = p2_s2(c)
              g1_ = p2_s1(c + 1) if c + 1 < NT else None
              for seg in range(4):
                  next(g2_, None)
                  if g1_ is not None and seg < 3:
                      next(g1_, None)
          pst["f"] = 0
          B.barrier()
          if STOP == 5:
              raise _Stop()

          qT = vb(O_HI, 4 * L).rearrange("p (k t) -> p k t", k=4)
          kz = vb(O_HI + 16384, 4 * L).rearrange("p (g h t) -> p g h t", g=2, h=2)
          vz = vb(O_HI + 32768, NT * 512).rearrange("p (c f) -> p c f", c=NT)
          cs = vf(O_HI + 49152, 2 * L).rearrange("p (a t) -> p a t", a=2)
          RqT, Rkd, Rvz, Rcs = RL(NT), RL(NT), RL(NT), Res()
          B.dma("sp", "c0", lambda e: e.dma_start(out=cs, in_=dcs), (), (Rcs,))
          wv = vb(O_WK, 8 * 512).rearrange("p (k n) -> p k n", k=8)
          wk = vb(O_WK + 8192, 8 * 256).rearrange("p (k n) -> p k n", k=8)
          wq = vb(O_WK + 12288, 8 * 512).rearrange("p (k n) -> p k n", k=8)
          Rwv, Rwk, Rwq = Res(), Res(), Res()
          def qkset(w0):
              d_ = {}
              d_["sq"] = vb(w0, 512); w0 += 1024
              d_["lnv"] = vf(w0, 512); w0 += 2048
              d_["qn"] = vb(w0, 512); w0 += 1024
              d_["ta"] = vf(w0, 512); w0 += 2048
              d_["tb"] = vf(w0, 512); w0 += 2048
              for nm in ("Rsq", "Rln", "Rqn", "Rta", "Rtb"):
                  d_[nm] = Res()
              return d_
          qks = [qkset(O_WK + 20480), qkset(O_WK)]
          w0 = O_WK + 28672
          Pt = [vb(w0 + i * 1536, 768) for i in range(2)]; w0 += 3072
          dpl = vf(w0, 512); w0 += 2048
          ktmp2 = [vb(w0 + i * 1024, 512) for i in range(2)]; w0 += 2048
          Rkt2 = RL(2)
          assert w0 - O_WK <= WK_SIZE
          RPt, Rdp = RL(2), Res()
          qkc = [0]

          B.ms("dve", vb(O_WK, 4096), 0.0, (), (Rwv,))
          for g in range(2):
              for hf in range(2):
                  c0 = (g * 2 + hf) * 128 + hf * 64
                  wload(wv[:, :, c0:c0 + 64], win_v[:, :, C_V + g * 64:C_V + (g + 1) * 64], (Rwv,), "wv")
              for hf in range(2):
                  wload(wk[:, :, g * 128 + hf * 64:g * 128 + (hf + 1) * 64], win_v[:, :, C_K + g * 64:C_K + (g + 1) * 64],
                        (Rwk,), "wk")
          wload(wq, win_v[:, :, C_Q:C_Q + 512], (Rwq,), "wq")
          for tt in range(NT):
              ps, psr = psf()
              for k in range(8):
                  B.mm(ps[:, :], hT[:, k, tt * 128:(tt + 1) * 128], wv[:, k, :], k == 0, k == 7, (RhT[tt], Rwv), (psr,),
                       inc=(k == 7))
              B.cp("act", vz[:, tt, :], ps[:, :], (psr,), (Rvz[tt],))

          def qk_chunk(wtile, wres, col0, dst_ap, dres4, pcol, T, post=None):
              S_ = qks[qkc[0] % 2]
              qkc[0] += 1
              sq, lnv, qn, ta, tb = S_["sq"], S_["lnv"], S_["qn"], S_["ta"], S_["tb"]
              Rsq, Rln, Rqn, Rta, Rtb = S_["Rsq"], S_["Rln"], S_["Rqn"], S_["Rta"], S_["Rtb"]
              tsl = slice(T * 512, (T + 1) * 512)
              psA, psAr = psf()
              for k in range(8):
                  B.mm(psA[:, :], wtile[:, k, col0:col0 + 128], hT[:, k, tsl], k == 0, k == 7, (wres,) + tuple(RhT4[T]),
                       (psAr,), inc=(k == 7))
              B.act(sq, psA[:, :], AF.Square, (psAr,), (Rsq,))
              psB, psBr = psf()
              B.mm(psB[:, :], cb[:, CB_BLK, :], sq, True, True, (R_const, Rsq), (psBr,), inc=True)
              rstd_from_ps(psB[:, :], psBr, 64, lnv, lnv, Rln, Rln)
              B.stt("dve", qn, psA[:, :], pp[:, pcol:pcol + 1], lnv, ALU.mult, ALU.mult, (psAr, R_const, Rln), (Rqn,))
              yield
              psR, psRr = psf()
              B.mm(psR[:, :], cb[:, CB_ROT, :], qn, True, True, (R_const, Rqn), (psRr,), inc=True)
              B.tt("dve", ta, psR[:, :], cs[:, 1, tsl], ALU.mult, (psRr, Rcs), (Rta,))
              B.tt("dve", tb, qn, cs[:, 0, tsl], ALU.mult, (Rqn, Rcs), (Rtb,))
              B.tt("dve", dst_ap, ta, tb, ALU.add, (Rta, Rtb), tuple(dres4))
              if post is not None:
                  post()

          B.barrier()
          def kpost(g, T):
              def f():
                  for hf in range(2):
                      B.ts("dve", kz[:, g, hf, T * 512:(T + 1) * 512], ktmp2[g], pp[:, PP_M0 + hf:PP_M0 + hf + 1], None,
                           ALU.mult, None, (Rkt2[g], R_const), tuple(Rkd[4 * T:4 * T + 4]))
              return f
          gens = []
          for T in range(4):
              for g in range(2):
                  gens.append(qk_chunk(wk, Rwk, g * 128, ktmp2[g], (Rkt2[g],), PP_KW, T, post=kpost(g, T)))
              for c4 in range(4):
                  gens.append(qk_chunk(wq, Rwq, c4 * 128, qT[:, c4, T * 512:(T + 1) * 512], RqT[4 * T:4 * T + 4], PP_QW, T))
          run_pipelined(gens)

          if STOP == 5.5:
              B.barrier()
              raise _Stop()
          pst["n"] = 4
          pst["f"] = 0
          psN, psNr = PF[4], PFR[4]
          psD, psDr = PF[5], PFR[5]

          def att_iter(n, c4):
              qsl = slice(n * 128, (n + 1) * 128)
              js = [j for j in (n - 1, n, n + 1) if 0 <= j < NT]
              g = c4 // 2
              pi = c4 % 2
              psS0, psS0r = psf()
              psS1, psS1r = psf()
              for ji, j in enumerate(js):
                  for hf in range(2):
                      slot = ji * 2 + hf
                      pS, pSr = (psS0, psS0r) if slot < 4 else (psS1, psS1r)
                      so = (slot % 4) * 128
                      B.mm(pS[:, so:so + 128], kz[:, g, hf, j * 128:(j + 1) * 128],
                           qT[:, c4, qsl], True, True, (Rkd[j], RqT[n]), (pSr,),
                           inc=(slot == 3 or slot == len(js) * 2 - 1))
              n0 = min(4, len(js) * 2)
              B.act(Pt[pi][:, 0:n0 * 128], psS0[:, 0:n0 * 128], AF.Exp, (psS0r,), (RPt[pi],), scale=0.125)
              if len(js) * 2 > 4:
                  B.act(Pt[pi][:, 512:768], psS1[:, 0:256], AF.Exp, (psS1r,), (RPt[pi],), scale=0.125)
              for ji, j in enumerate(js):
                  if j != n:
                      mk = cb[:, CB_MPREV, :] if j < n else cb[:, CB_MNEXT, :]
                      pv = Pt[pi][:, ji * 256:(ji + 1) * 256].rearrange("p (h q) -> p h q", h=2)
                      B.tt("dve", pv, pv, mk.unsqueeze(1).to_broadcast([128, 2, 128]), ALU.mult, (RPt[pi], R_const),
                           (RPt[pi],))
              yield
              nmm = len(js) * 2
              i = 0
              for ji, j in enumerate(js):
                  for hf in range(2):
                      B.mm(psN[:, c4 * 128:(c4 + 1) * 128], vz[:, j, (g * 2 + hf) * 128:(g * 2 + hf + 1) * 128],
                           Pt[pi][:, (ji * 2 + hf) * 128:(ji * 2 + hf + 1) * 128], i == 0, i == nmm - 1,
                           (Rvz[j], RPt[pi]), (psNr,))
                      i += 1
              i = 0
              for ji, j in enumerate(js):
                  for hf in range(2):
                      B.mm(psD[:, c4 * 128:(c4 + 1) * 128], cb[:, CB_OZ0 + hf, :],
                           Pt[pi][:, (ji * 2 + hf) * 128:(ji * 2 + hf + 1) * 128], i == 0, i == nmm - 1,
                           (R_const, RPt[pi]), (psDr,), inc=(i == nmm - 1))
                      i += 1
              if c4 == 3:
                  B.tt("dve", dpl.rearrange("p (c q) -> p c q", c=4), psD[:, :].rearrange("p (c q) -> p c q", c=4),
                       esink.unsqueeze(2).to_broadcast([128, 4, 128]), ALU.add, (psDr, R_const), (Rdp,))
                  B.op("dve", lambda e: e.reciprocal(out=dpl, in_=dpl), (Rdp,), (Rdp,))
                  B.tt("dve", mxA[:, :, qsl], psN[:, :].rearrange("p (c q) -> p c q", c=4),
                       dpl.rearrange("p (c q) -> p c q", c=4), ALU.mult, (psNr, Rdp), ())

          run_pipelined([att_iter(n, c4) for n in range(NT) for c4 in range(4)])
          pst["n"] = 6
          pst["f"] = 0
          B.barrier()
          if STOP == 6:
              raise _Stop()

          memT = vb(O_HI, 8 * 256).rearrange("p (k t) -> p k t", k=8)
          kmT = vb(O_HI + 4096, 4 * 256).rearrange("p (h t) -> p h t", h=4)
          vm = vb(O_HI + 6144, 2 * 512).rearrange("p (m f) -> p m f", m=2)
          kmn = vb(O_HI + 8192, 512)
          kst = vf(O_HI + 9216, 16)
          wkv = vb(O_HI + 16384, 8 * 1024).rearrange("p (k n) -> p k n", k=8)
          wqx = vb(O_HI + 32768, 8 * 512).rearrange("p (k n) -> p k n", k=8)
          RmemT, RkmT, Rvm, Rkmn, Rkst, Rwkv, Rwqx = RL(2), Res(), RL(2), Res(), Res(), Res(), Res()
          wload(wkv, wkv_v, (Rwkv,), "wkv")
          wload(wqx, win_v[:, :, C_QX:C_QX + 512], (Rwqx,), "wqx")
          hbuild(lambda tt: dmem[s, tt * 128:(tt + 1) * 128, :], 2, memT, RmemT, PP_NWMEM, O_WK)
          def xset(w0):
              d_ = {}
              d_["sq"] = vb(w0, 512); w0 += 1024
              d_["lnv"] = vf(w0, 512); w0 += 2048
              d_["qn"] = vb(w0, 512); w0 += 1024
              d_["rD"] = vf(w0, 512); w0 += 2048
              d_["Px"] = [vb(w0 + i * 1024, 512) for i in range(2)]; w0 += 2048
              for nm in ("Rsq", "Rln", "Rqn", "RrD"):
                  d_[nm] = Res()
              d_["RPx"] = RL(2)
              return d_
          xs_ = [xset(O_WK + 16384), xset(O_WK + 16384 + 8192)]
          tk = vf(O_WK + 32768, 512)
          Rtk = Res()
          assert 32768 + 2048 <= WK_SIZE
          for mt in range(2):
              msl = slice(mt * 128, (mt + 1) * 128)
              psK, psKr = psf()
              for k in range(8):
                  B.mm(psK[:, :], memT[:, k, msl], wkv[:, k, 0:512], k == 0, k == 7, (RmemT[mt], Rwkv), (psKr,), inc=(k == 7))
              B.ms("dve", kst, 0.0, (), (Rkst,))
              for h in range(4):
                  B.act(tk[:, h * 128:(h + 1) * 128], psK[:, h * 128:(h + 1) * 128], AF.Square, (psKr, Rkst), (Rtk, Rkst),
                        accum=kst[:, h:h + 1])
              B.act(kst[:, 4:8], kst[:, 0:4], AF.Ln, (Rkst,), (Rkst,), scale=1.0 / 128, bias=EPS)
              B.act(kst[:, 8:12], kst[:, 4:8], AF.Exp, (Rkst,), (Rkst,), scale=-0.5)
              B.tt("dve", tk.rearrange("p (h d) -> p h d", h=4), psK[:, :].rearrange("p (h d) -> p h d", h=4),
                   kst[:, 8:12].unsqueeze(2).to_broadcast([128, 4, 128]), ALU.mult, (psKr, Rkst), (Rtk,))
              B.tt("dve", kmn.rearrange("p (h d) -> p h d", h=4), tk.rearrange("p (h d) -> p h d", h=4),
                   pp[:, PP_XKW:PP_XKW + 128].unsqueeze(1).to_broadcast([128, 4, 128]), ALU.mult, (Rtk, R_const), (Rkmn,))
              pb, pbr = psb()
              for h in range(4):
                  B.tr(pb[:, h * 128:(h + 1) * 128], kmn[:, h * 128:(h + 1) * 128], ident, (Rkmn, R_const), (pbr,), inc=(h == 3))
              B.cp("act", kmT[:, :, msl], pb[:, 0:512].rearrange("p (h t) -> p h t", h=4), (pbr,), (RkmT,))
              psV, psVr = psf()
              for k in range(8):
                  B.mm(psV[:, :], memT[:, k, msl], wkv[:, k, 512:1024], k == 0, k == 7, (RmemT[mt], Rwkv), (psVr,), inc=(k == 7))
              B.cp("act", vm[:, mt, :], psV[:, :], (psVr,), (Rvm[mt],))
          def xat_iter(T, h, xc):
              tsl = slice(T * 512, (T + 1) * 512)
              S_ = xs_[xc % 2]
              sq, lnv, qn, rD, Px = S_["sq"], S_["lnv"], S_["qn"], S_["rD"], S_["Px"]
              Rsq, Rln, Rqn, RrD, RPx = S_["Rsq"], S_["Rln"], S_["Rqn"], S_["RrD"], S_["RPx"]
              psA, psAr = psf()
              for k in range(8):
                  B.mm(psA[:, :], wqx[:, k, h * 128:(h + 1) * 128], hT[:, k, tsl], k == 0, k == 7, (Rwqx,) + tuple(RhT4[T]),
                       (psAr,), inc=(k == 7))
              B.act(sq, psA[:, :], AF.Square, (psAr,), (Rsq,))
              psB, psBr = psf()
              B.mm(psB[:, :], cb[:, CB_ONES, :], sq, True, True, (R_const, Rsq), (psBr,), inc=True)
              rstd_from_ps(psB[:, :], psBr, 128, lnv, lnv, Rln, Rln)
              B.stt("dve", qn, psA[:, :], pp[:, PP_XQW:PP_XQW + 1], lnv, ALU.mult, ALU.mult, (psAr, R_const, Rln), (Rqn,))
              yield
              for mt in range(2):
                  psS, psSr = psf()
                  B.mm(psS[:, :], kmT[:, h, mt * 128:(mt + 1) * 128], qn, True, True, (RkmT, Rqn), (psSr,), inc=True)
                  B.act(Px[mt], psS[:, :], AF.Exp, (psSr,), (RPx[mt],), scale=128 ** -0.5)
              psN, psNr = psf()
              psD, psDr = psf()
              for mt in range(2):
                  B.mm(psN[:, :], vm[:, mt, h * 128:(h + 1) * 128], Px[mt], mt == 0, mt == 1, (Rvm[mt], RPx[mt]), (psNr,))
              for mt in range(2):
                  B.mm(psD[:, :], cb[:, CB_ONES, :], Px[mt], mt == 0, mt == 1, (R_const, RPx[mt]), (psDr,), inc=(mt == 1))
              B.op("dve", lambda e, o_=rD, i_=psD[:, :]: e.reciprocal(out=o_, in_=i_), (psDr,), (RrD,))
              B.tt("dve", mxX[:, h, tsl], psN[:, :], rD, ALU.mult, (psNr, RrD), ())

          run_pipelined([xat_iter(T, h, T * 4 + h) for T in range(4) for h in range(4)])
          B.barrier()
          if STOP == 7:
              raise _Stop()

          if DEBUG:
              B.dma("sp", "dbg", lambda e: e.dma_start(out=ddbg[s, :, 0:4, :], in_=mxA), (), ())
              B.dma("sp", "dbg", lambda e: e.dma_start(out=ddbg[s, :, 12:16, :], in_=mxX), (), ())
              for c in range(NT):
                  B.dma("sp", "dbg", lambda e, c=c: e.dma_start(out=ddbg[s, :, 4:12, c * 128:(c + 1) * 128],
                                                               in_=lo[:, c, :].rearrange("p (k t) -> p k t", k=8)), (), ())
              B.barrier()

          x1 = vf(O_HI, NT * 1024).rearrange("p (c f) -> p c f", c=NT)
          Rx1 = RL(NT)
          wo = vb(O_HT, 16 * 1024).rearrange("p (k n) -> p k n", k=16)
          Rwo = Res()
          for kh in range(2):
              wload(wo[:, kh * 8:(kh + 1) * 8, :], wout_v[:, kh * 8:(kh + 1) * 8, :], (Rwo,), "wo")
          xt = [vf(O_WK + i * 4096, 1024) for i in range(2)]
          Rxt = RL(2)
          for tt in range(NT):
              tsl = slice(tt * 128, (tt + 1) * 128)
              b = tt % 2
              B.dma("sp", f"xt{b}", lambda e, o_=xt[b], i_=dx[s, tsl, :]: e.dma_start(out=o_, in_=i_), (), (Rxt[b],))
              for hf in range(2):
                  ps, psr = psf()
                  for kc in range(16):
                      if kc < 4:
                          lhs = mxA[:, kc, tsl]
                      elif kc < 12:
                          lhs = lo[:, tt, (kc - 4) * 128:(kc - 3) * 128]
                      else:
                          lhs = mxX[:, kc - 12, tsl]
                      B.mm(ps[:, :], lhs, wo[:, kc, hf * 512:(hf + 1) * 512], kc == 0, kc == 15, (Rwo,), (psr,), inc=(kc == 15))
                  B.tt("dve", x1[:, tt, hf * 512:(hf + 1) * 512], ps[:, :], xt[b][:, hf * 512:(hf + 1) * 512], ALU.add,
                       (psr, Rxt[b]), (Rx1[tt],))
          B.barrier()
          if STOP == 8:
              raise _Stop()

          Rh2 = RL(NT)
          hbuild(lambda tt: x1[:, tt, :], NT, hT, Rh2, PP_NWMLP, O_WK, src_res=Rx1)
          Rh24 = [[Rh2[4 * T + i] for i in range(4)] for T in range(4)]
          if STOP == 8.5:
              B.barrier()
              raise _Stop()
          wu = [vb(O_MX + i * 32768, 8 * 1024).rearrange("p (k n) -> p k n", k=8) for i in range(2)]
          wd = [vb(O_MX + 16384 + i * 32768, 8 * 1024).rearrange("p (k n) -> p k n", k=8) for i in range(2)]
          Rwu, Rwd = RL(2), RL(2)
          uT = [vb(O_WK + 16384 + i * 8192, 8 * 512).rearrange("p (k t) -> p k t", k=8) for i in range(2)]
          RuT = RL(2)
          rl = [vb(O_WK + 32768 + i * 1024, 512) for i in range(2)]
          Rrl = RL(2)
          assert 32768 + 2048 <= WK_SIZE
          ui = 0

          def mlp_wload(fb_):
              wi_ = fb_ % 2
              wload(wu[wi_], wup_v[:, :, fb_ * 1024:(fb_ + 1) * 1024], (Rwu[wi_],), f"wu{wi_}")
              wload(wd[wi_], wdn_v[:, fb_ * 8:(fb_ + 1) * 8, :], (Rwd[wi_],), f"wd{wi_}")
          mlp_wload(0)
          mlp_wload(1)
          def mlp_iter(fb, T, u):
              wi = fb % 2
              tsl = slice(T * 512, (T + 1) * 512)
              for fc in range(8):
                  ps, psr = psf()
                  for k in range(8):
                      B.mm(ps[:, :], wu[wi][:, k, fc * 128:(fc + 1) * 128], hT[:, k, tsl], k == 0, k == 7,
                           (Rwu[wi],) + tuple(Rh24[T]), (psr,), inc=(k == 7))
                  r = fc % 2
                  B.act(rl[r], ps[:, :], AF.Relu, (psr,), (Rrl[r],))
                  B.tt("pool", uT[u][:, fc, :], rl[r], rl[r], ALU.mult, (Rrl[r],), (RuT[u],))
              yield
              for ti in range(4):
                  tt = T * 4 + ti
                  for hf in range(2):
                      ps, psr = psf()
                      for fc in range(8):
                          B.mm(ps[:, :], uT[u][:, fc, ti * 128:(ti + 1) * 128], wd[wi][:, fc, hf * 512:(hf + 1) * 512],
                               fc == 0, fc == 7, (RuT[u], Rwd[wi]), (psr,), inc=(fc == 7))
                      B.tt("dve", x1[:, tt, hf * 512:(hf + 1) * 512], x1[:, tt, hf * 512:(hf + 1) * 512], ps[:, :], ALU.add,
                           (Rx1[tt], psr), (Rx1[tt],))
                  if fb == 3:
                      B.dma("sp", "out", lambda e, o_=dout[s, tt * 128:(tt + 1) * 128, :], i_=x1[:, tt, :]:
                            e.dma_start(out=o_, in_=i_), (Rx1[tt],), ())
              if T == 3 and fb + 2 < 4:
                  mlp_wload(fb + 2)

          run_pipelined([mlp_iter(fb, T, (fb * 4 + T) % 2) for fb in range(4) for T in range(4)])
      except _Stop:
        pass
    B.barrier()
    for k, v in list(B.cnt.items()):
        B.need("sp", (k, v))

    keys = list(B.cnt.keys())
    sems = {k: es.enter_context(nc.semaphore(f"s_{k}")) for k in keys}

    def run(e, stream):
        for it in stream:
            if it[0] == 0:
                e.wait_ge(sems[it[1]], it[2])
            else:
                ins = it[1](e)
                if it[2] is not None:
                    ins.then_inc(sems[it[2]], it[3])

    with nc.Block() as block:
        @block.tensor
        def _(e):
            run(e, B.streams["pe"])

        @block.scalar
        def _(e):
            run(e, B.streams["act"])

        @block.vector
        def _(e):
            run(e, B.streams["dve"])

        @block.gpsimd
        def _(e):
            run(e, B.streams["pool"])

        @block.sync
        def _(e):
            run(e, B.streams["sp"])
    es.close()
    return nc


def make_consts():
    j = np.arange(128)[:, None]
    l = np.arange(128)[None, :]
    cbm = np.zeros((128, NCB, 128), np.float32)
    cbm[:, CB_ID] = (j == l)
    cbm[:, CB_ONES] = 1.0
    cbm[:, CB_BLK] = (j // 64 == l // 64)
    cbm[:, CB_OZ0] = (l < 64)
    cbm[:, CB_OZ1] = (l >= 64)
    cbm[:, CB_LSF] = (j > l)
    cbm[:, CB_LSB] = (j < l)
    rot = np.zeros((128, 128), np.float32)
    for hb in (0, 64):
        for d in range(8):
            rot[hb + d + 8, hb + d] = -1.0
            rot[hb + d, hb + d + 8] = 1.0
    cbm[:, CB_ROT] = rot
    cbm[:, CB_MPREV] = (j >= l)
    cbm[:, CB_MNEXT] = (j <= l)
    cbm[:, CB_MF] = (l >= j)
    cbm[:, CB_MB] = (l <= j)
    cfm = np.zeros((128, NCF, 128), np.float32)
    cfm[:, CF_TUI] = (j <= l)
    cfm[:, CF_TLI] = (j >= l)
    cfm[:, CF_TUS] = (j > l)
    cfm[:, CF_TLS] = (j < l)
    cfm[:, CF_ONES] = 1.0
    cfm[:, CF_ID] = (j == l)
    inv = 500000.0 ** (-np.arange(0, 16, 2, dtype=np.float32) / 16)
    t = np.arange(L, dtype=np.float32)
    ang = t[None, :] * inv[:, None]
    cs = np.zeros((128, 2, L), np.float32)
    cs[:, 0, :] = 1.0
    for p in range(128):
        d = p % 64
        if d < 16:
            cs[p, 0] = np.cos(ang[d % 8])
            cs[p, 1] = np.sin(ang[d % 8])
    return cbm.astype(ml_dtypes.bfloat16), cfm, cs


def pack_params(inp):
    pp = np.zeros((128, NPP), np.float32)
    p = np.arange(128)
    pp[:, PP_NWMIX:PP_NWMIX + 8] = inp["norm_mix_w"][0].reshape(8, 128).T
    pp[:, PP_NWMLP:PP_NWMLP + 8] = inp["norm_mlp_w"][0].reshape(8, 128).T
    pp[:, PP_NWMEM:PP_NWMEM + 8] = inp["mem_norm_w"][0].reshape(8, 128).T
    pp[:, PP_SSDNW:PP_SSDNW + 8] = inp["ssd_norm_w"][0].reshape(8, 128).T
    cw = inp["conv_w"][0]
    pp[:, PP_CONVW:PP_CONVW + 60] = cw.reshape(5, 12, 128).transpose(2, 1, 0).reshape(128, 60)
    pp[:, PP_CONVB:PP_CONVB + 12] = inp["conv_b"][0].reshape(12, 128).T
    pp[:, PP_QW] = inp["q_norm_w"][0][p % 64]
    pp[:, PP_KW] = inp["k_norm_w"][0][p % 64]
    pp[:, PP_XQW] = inp["xq_norm_w"][0]
    for c in range(4):
        pp[:, PP_SINK + c] = inp["attn_sink"][0][2 * c + p // 64]
    pp[:, PP_DTB:PP_DTB + 16] = inp["dt_bias_f"][0][None, :]
    pp[:, PP_DTB + 16:PP_DTB + 32] = inp["dt_bias_b"][0][None, :]
    pp[:, PP_ALOG:PP_ALOG + 16] = inp["a_log_f"][0][None, :]
    pp[:, PP_ALOG + 16:PP_ALOG + 32] = inp["a_log_b"][0][None, :]
    pp[:, PP_SSDD:PP_SSDD + 16] = inp["ssd_d"][0][None, :]
    pp[:, PP_XKW:PP_XKW + 128] = inp["xk_norm_w"][0][None, :]
    pp[:, PP_M0] = (p < 64)
    pp[:, PP_M1] = (p >= 64)
    return pp


_NC_CACHE = {}


def kernel(**inputs):
    inp = {k: np.asarray(v) for k, v in inputs.items()}
    if "nc" not in _NC_CACHE:
        _NC_CACHE["nc"] = build_program()
    nc = _NC_CACHE["nc"]
    cbm, cfm, cs = make_consts()
    pp = pack_params(inp)
    shared = {
        "w_in": np.ascontiguousarray(inp["w_in"][0]),
        "w_mem_kv": np.ascontiguousarray(inp["w_mem_kv"][0]),
        "w_out": np.ascontiguousarray(inp["w_out"][0]),
        "w_up": np.ascontiguousarray(inp["w_mlp_up"][0]),
        "w_down": np.ascontiguousarray(inp["w_mlp_down"][0]),
        "cbf": cbm, "cf32": cfm, "pp": pp, "cossin": cs,
    }
    in_maps = []
    for c in range(NCORES):
        m = dict(shared)
        m["x"] = np.ascontiguousarray(inp["x"][c * SEQ_PER_CORE:(c + 1) * SEQ_PER_CORE])
        m["mem"] = np.ascontiguousarray(inp["mem"][c * SEQ_PER_CORE:(c + 1) * SEQ_PER_CORE])
        in_maps.append(m)
    res = run_bass_kernel_spmd(nc, in_maps, core_ids=list(range(NCORES)))
    out = np.concatenate([np.asarray(r["out"]) for r in res.results], axis=0)
    return out.astype(np.float32)
```

```python
import numpy as np
import ml_dtypes
from contextlib import ExitStack
import concourse.bass as bass
import concourse.mybir as mybir
from concourse.bass_utils import run_bass_kernel_spmd

F32 = mybir.dt.float32
BF16 = mybir.dt.bfloat16
AF = mybir.ActivationFunctionType
ALU = mybir.AluOpType

NCORES = 8
SEQ_PER_CORE = 2
L = 2048
D = 1024
NT = 16
EPS = 1e-6
DEBUG = False
STOP = 99


class _Stop(Exception):
    pass
NSEQ_RUN = SEQ_PER_CORE

C_Q, C_K, C_V, C_Z, C_XBC, C_DT, C_QX = 0, 512, 640, 768, 1792, 3328, 3360

(CB_ID, CB_ONES, CB_BLK, CB_OZ0, CB_OZ1, CB_LSF, CB_LSB, CB_ROT, CB_MPREV, CB_MNEXT, CB_MF, CB_MB) = range(12)
NCB = 12
(CF_TUI, CF_TLI, CF_TUS, CF_TLS, CF_ONES, CF_ID) = range(6)
NCF = 6
PP_NWMIX, PP_NWMLP, PP_NWMEM, PP_SSDNW = 0, 8, 16, 24
PP_CONVW = 32
PP_CONVB = 92
PP_QW, PP_KW, PP_XQW = 104, 105, 106
PP_SINK = 107
PP_DTB = 111
PP_ALOG = 143
PP_SSDD = 175
PP_XKW = 191
PP_M0, PP_M1 = 320, 321
NPP = 322

ENG = ["pe", "act", "dve", "pool", "sp"]


class Res:
    __slots__ = ("w", "r")

    def __init__(self):
        self.w = None
        self.r = {}


def RL(n):
    return [Res() for _ in range(n)]


class Bld:
    def __init__(self):
        self.streams = {e: [] for e in ENG}
        self.cnt = {}
        self.known = {e: {} for e in ENG}
        self.psi = 0

    def need(self, eng, tok, skip_same=False):
        if tok is None:
            return
        k, v = tok
        if skip_same and k == eng:
            return
        if self.known[eng].get(k, 0) >= v:
            return
        self.known[eng][k] = v
        self.streams[eng].append((0, k, v))

    def op(self, eng, fn, reads=(), writes=(), inc=True):
        for r in reads:
            self.need(eng, r.w)
        for w in writes:
            self.need(eng, w.w, True)
            for k, v in w.r.items():
                self.need(eng, (k, v), True)
        c = self.cnt.get(eng, 0) + 1
        if inc:
            self.cnt[eng] = c
        self.streams[eng].append((1, fn, eng if inc else None, 1))
        for r in reads:
            r.r[eng] = c
        for w in writes:
            w.w = (eng, c)
            w.r = {}

    def dma(self, q, chan, fn, reads=(), writes=()):
        for r in reads:
            self.need(q, r.w)
        for w in writes:
            if not (w.w is not None and w.w[0] == chan):
                self.need(q, w.w)
            for k, v in w.r.items():
                self.need(q, (k, v))
        c = self.cnt.get(chan, 0) + 16
        self.cnt[chan] = c
        self.streams[q].append((1, fn, chan, 16))
        for r in reads:
            r.r[chan] = c
        for w in writes:
            w.w = (chan, c)
            w.r = {}

    def barrier(self):
        toks = list(self.cnt.items())
        for e in ENG:
            for t in toks:
                self.need(e, t, True)

    def mm(self, out, lhsT, rhs, start, stop, reads, writes, inc=False):
        self.op("pe", lambda e: e.matmul(out, lhsT=lhsT, rhs=rhs, start=start, stop=stop), reads, writes, inc)

    def tr(self, out, in_, ident, reads, writes, inc=False):
        self.op("pe", lambda e: e.transpose(out=out, in_=in_, identity=ident), reads, writes, inc)

    def act(self, out, in_, func, reads, writes, scale=1.0, bias=0.0, accum=None):
        if accum is None:
            self.op("act", lambda e: e.activation(out=out, in_=in_, func=func, bias=bias, scale=scale), reads, writes)
        else:
            self.op("act", lambda e: e.activation(out=out, in_=in_, func=func, bias=bias, scale=scale,
                                                  accum_out=accum), reads, writes)

    def tt(self, eng, out, in0, in1, op, reads, writes):
        self.op(eng, lambda e: e.tensor_tensor(out=out, in0=in0, in1=in1, op=op), reads, writes)

    def ts(self, eng, out, in0, s1, s2, op0, op1, reads, writes):
        if s2 is None:
            self.op(eng, lambda e: e.tensor_scalar(out=out, in0=in0, scalar1=s1, scalar2=None, op0=op0), reads, writes)
        else:
            self.op(eng, lambda e: e.tensor_scalar(out=out, in0=in0, scalar1=s1, scalar2=s2, op0=op0, op1=op1),
                    reads, writes)

    def stt(self, eng, out, in0, scalar, in1, op0, op1, reads, writes):
        self.op(eng, lambda e: e.scalar_tensor_tensor(out=out, in0=in0, scalar=scalar, in1=in1, op0=op0, op1=op1),
                reads, writes)

    def cp(self, eng, out, in_, reads, writes):
        if eng == "act":
            self.op("act", lambda e: e.activation(out=out, in_=in_, func=AF.Copy), reads, writes)
        else:
            self.op(eng, lambda e: e.tensor_copy(out=out, in_=in_), reads, writes)

    def ms(self, eng, out, val, reads, writes):
        self.op(eng, lambda e: e.memset(out, val), reads, writes)


def run_pipelined(gens):
    prev = None
    for g in gens:
        next(g)
        if prev is not None:
            for _ in prev:
                pass
        prev = g
    if prev is not None:
        for _ in prev:
            pass


def build_program():
    nc = bass.Bass("TRN2", target_bir_lowering=False)
    dx = nc.dram_tensor("x", [SEQ_PER_CORE, L, D], F32, kind="ExternalInput").ap()
    dmem = nc.dram_tensor("mem", [SEQ_PER_CORE, 256, D], F32, kind="ExternalInput").ap()
    dwin = nc.dram_tensor("w_in", [D, 3872], F32, kind="ExternalInput").ap()
    dwkv = nc.dram_tensor("w_mem_kv", [D, 1024], F32, kind="ExternalInput").ap()
    dwout = nc.dram_tensor("w_out", [2048, D], F32, kind="ExternalInput").ap()
    dwup = nc.dram_tensor("w_up", [D, 4096], F32, kind="ExternalInput").ap()
    dwdn = nc.dram_tensor("w_down", [4096, D], F32, kind="ExternalInput").ap()
    dcb = nc.dram_tensor("cbf", [128, NCB, 128], BF16, kind="ExternalInput").ap()
    dcf = nc.dram_tensor("cf32", [128, NCF, 128], F32, kind="ExternalInput").ap()
    dpp = nc.dram_tensor("pp", [128, NPP], F32, kind="ExternalInput").ap()
    dcs = nc.dram_tensor("cossin", [128, 2, L], F32, kind="ExternalInput").ap()
    dout = nc.dram_tensor("out", [SEQ_PER_CORE, L, D], F32, kind="ExternalOutput").ap()
    if DEBUG:
        ddbg = nc.dram_tensor("dbg", [SEQ_PER_CORE, 128, 16, L], BF16, kind="ExternalOutput").ap()

    win_v = dwin.rearrange("(k p) n -> p k n", p=128)
    wkv_v = dwkv.rearrange("(k p) n -> p k n", p=128)
    wout_v = dwout.rearrange("(k p) n -> p k n", p=128)
    wup_v = dwup.rearrange("(k p) n -> p k n", p=128)
    wdn_v = dwdn.rearrange("(k p) n -> p k n", p=128)

    B = Bld()
    es = ExitStack()
    ARENA_ELEMS = 106400
    arena = es.enter_context(nc.sbuf_tensor("arena", [128, ARENA_ELEMS], BF16))
    PF = [es.enter_context(nc.psum_tensor(f"pf{i}", [128, 512], F32)) for i in range(6)]
    PB = [es.enter_context(nc.psum_tensor(f"pb{i}", [128, 1024], BF16)) for i in range(2)]
    PFR = RL(6)
    PBR = RL(2)
    pst = {"f": 0, "b": 0, "n": 6}

    def psf():
        i = pst["f"] % pst["n"]
        pst["f"] = (i + 1) % pst["n"]
        return PF[i], PFR[i]

    def psb():
        i = pst["b"]
        pst["b"] = (i + 1) % 2
        return PB[i], PBR[i]

    def vb(off, n):
        assert off % 4 == 0 and off // 2 + n <= ARENA_ELEMS, (off, n)
        return arena[:, off // 2: off // 2 + n]

    def vf(off, n):
        assert off % 4 == 0 and off // 2 + 2 * n <= ARENA_ELEMS, (off, n)
        return arena[:, off // 2: off // 2 + 2 * n].bitcast(F32)

    o = 0
    O_CB = o; o += NCB * 256
    O_CF = o; o += NCF * 512
    O_PP = o; o += NPP * 4
    O_SM = o; o += 1024
    O_HT = o; o += 32768
    O_MX = o; o += 32768
    O_LO = o; o += 32768
    O_HI = o; o += 65536
    O_WK = o
    WK_SIZE = ARENA_ELEMS * 2 - O_WK
    assert WK_SIZE >= 33000, WK_SIZE

    cb = vb(O_CB, NCB * 128).rearrange("p (m n) -> p m n", m=NCB)
    cf = vf(O_CF, NCF * 128).rearrange("p (m n) -> p m n", m=NCF)
    pp = vf(O_PP, NPP)
    esink = vf(O_SM, 4)
    aneg = vf(O_SM + 16, 32)
    R_const = Res()

    hT = vb(O_HT, 8 * L).rearrange("p (k t) -> p k t", k=8)
    mxA = vb(O_MX, 4 * L).rearrange("p (k t) -> p k t", k=4)
    mxX = vb(O_MX + 16384, 4 * L).rearrange("p (k t) -> p k t", k=4)
    lo = vb(O_LO, NT * 1024).rearrange("p (c f) -> p c f", c=NT)

    ident = cb[:, CB_ID, :]

    B.dma("sp", "c0", lambda e: e.dma_start(out=cb, in_=dcb), (), (R_const,))
    B.dma("sp", "c0", lambda e: e.dma_start(out=cf, in_=dcf), (), (R_const,))
    B.dma("sp", "c0", lambda e: e.dma_start(out=pp, in_=dpp), (), (R_const,))
    B.act(esink, pp[:, PP_SINK:PP_SINK + 4], AF.Exp, (R_const,), (R_const,))
    B.act(aneg, pp[:, PP_ALOG:PP_ALOG + 32], AF.Exp, (R_const,), (R_const,))
    B.ts("dve", aneg, aneg, -1.0, None, ALU.mult, None, (R_const,), (R_const,))

    def wload(dst, src, reads_w, chan):
        B.dma("pool", chan, lambda e: e.dma_start(out=dst, in_=src), (), reads_w)

    def hbuild(src_fn, ntiles, dstT, dst_res, nwcol, wk_off, src_res=None, dst_fn=None, run=True):
        xt = [vf(wk_off + i * 4096, 1024) for i in range(2)]
        xn = [vb(wk_off + 8192 + i * 2048, 1024) for i in range(2)]
        junk = vb(wk_off + 12288, 1024)
        st = vf(wk_off + 14336, 3 * ntiles)
        Rxt, Rxn, Rj, Rst = RL(2), RL(2), Res(), Res()
        B.ms("dve", st, 0.0, (), (Rst,))
        def hb_iter(tt):
            b = tt % 2
            if src_res is None:
                src = src_fn(tt)
                B.dma("sp", f"xt{b}", lambda e, o_=xt[b], i_=src: e.dma_start(out=o_, in_=i_), (), (Rxt[b],))
                xin, rin = xt[b], Rxt[b]
            else:
                xin, rin = src_fn(tt), src_res[tt]
            B.act(junk, xin, AF.Square, (rin, Rst), (Rj, Rst), accum=st[:, 3 * tt:3 * tt + 1])
            B.act(st[:, 3 * tt + 1:3 * tt + 2], st[:, 3 * tt:3 * tt + 1], AF.Ln, (Rst,), (Rst,), scale=1.0 / D, bias=EPS)
            B.act(st[:, 3 * tt + 2:3 * tt + 3], st[:, 3 * tt + 1:3 * tt + 2], AF.Exp, (Rst,), (Rst,), scale=-0.5)
            B.ts("dve", xn[b], xin, st[:, 3 * tt + 2:3 * tt + 3], None, ALU.mult, None, (rin, Rst), (Rxn[b],))
            yield
            pb, pbr = psb()
            for k in range(8):
                B.tr(pb[:, k * 128:(k + 1) * 128], xn[b][:, k * 128:(k + 1) * 128], ident, (Rxn[b], R_const), (pbr,),
                     inc=(k == 7))
            dst_ap = dstT[:, :, tt * 128:(tt + 1) * 128] if dst_fn is None else dst_fn(tt)
            B.tt("dve", dst_ap, pb[:, :].rearrange("p (k t) -> p k t", k=8),
                 pp[:, nwcol:nwcol + 8].unsqueeze(2).to_broadcast([128, 8, 128]), ALU.mult,
                 (pbr, R_const), (dst_res[tt],))

        gens_ = [hb_iter(tt) for tt in range(ntiles)]
        if not run:
            return gens_
        run_pipelined(gens_)

    def rstd_from_ps(psB, psBr, n_feat, lnv, rstd, Rln, Rrs):
        B.act(lnv, psB, AF.Ln, (psBr,), (Rln,), scale=1.0 / n_feat, bias=EPS)
        B.act(rstd, lnv, AF.Exp, (Rln,), (Rrs,), scale=-0.5)

    for s in range(NSEQ_RUN):
      try:
          B.barrier()
          RhT = RL(NT)
          hbuild(lambda tt: dx[s, tt * 128:(tt + 1) * 128, :], NT, hT, RhT, PP_NWMIX, O_WK)
          RhT4 = [[RhT[4 * T + i] for i in range(4)] for T in range(4)]
          B.barrier()
          if STOP == 1:
              raise _Stop()

          Rlo = RL(NT)
          prevb = vb(O_HI, NT * 1024).rearrange("p (c f) -> p c f", c=NT)
          BT = vb(O_HI + 32768, 2 * L).rearrange("p (g t) -> p g t", g=2)
          CT = vb(O_HI + 40960, 2 * L).rearrange("p (g t) -> p g t", g=2)
          Btok = vb(O_HI + 49152, NT * 256).rearrange("p (c f) -> p c f", c=NT)
          Rprevb, RBT, RCT, RBtok = RL(NT), RL(NT), RL(NT), RL(NT)
          Wz = vb(O_MX, 8 * 1024).rearrange("p (k n) -> p k n", k=8)
          RWz = Res()
          dtall = vf(O_MX + 16384, NT * 32).rearrange("p (c f) -> p c f", c=NT)
          aall = vf(O_MX + 18432, NT * 32).rearrange("p (c f) -> p c f", c=NT)
          exall = vf(O_MX + 20480, NT * 64).rearrange("p (c f) -> p c f", c=NT)
          cdec = vf(O_MX + 24576, NT * 32).rearrange("p (c f) -> p c f", c=NT)
          Rst = RL(NT)
          DI = vb(O_MX + 26624, 16 * 128).rearrange("p (h n) -> p h n", h=16)
          RDI = Res()
          wb = [vb(O_WK + i * 8192, 8 * 512).rearrange("p (k n) -> p k n", k=8) for i in range(2)]
          Rwb = RL(2)
          pre = [vb(O_WK + 16384 + i * 4112, 2052) for i in range(2)]
          Rpre = RL(2)
          actT = [vb(O_WK + 24640 + i * 4096, 2048) for i in range(2)]
          RactT = RL(2)
          diag = vb(O_HI, 5 * 128 * 2).rearrange("p (b k n) -> p b k n", b=2, k=5)
          Rdiag = RL(2)
          tmp32 = vf(O_HI + 4096, 64)
          Rtmp = Res()

          wload(Wz, win_v[:, :, C_Z:C_Z + 1024], (RWz,), "wz")
          for h in range(16):
              B.ts("dve", DI[:, h, :], cf[:, CF_ID, :], pp[:, PP_SSDD + h:PP_SSDD + h + 1], None, ALU.mult, None,
                   (R_const,), (RDI,))

          wdt = vb(O_HI + 8192, 8 * 32).rearrange("p (k n) -> p k n", k=8)
          Rwdt = Res()
          wload(wdt, win_v[:, :, C_DT:C_DT + 32], (Rwdt,), "wdt")
          for c in range(NT):
              ps, psr = psf()
              for k in range(8):
                  B.mm(ps[:, 0:32], hT[:, k, c * 128:(c + 1) * 128], wdt[:, k, :], k == 0, k == 7, (RhT[c], Rwdt), (psr,),
                       inc=(k == 7))
              B.tt("dve", tmp32[:, 0:32], ps[:, 0:32], pp[:, PP_DTB:PP_DTB + 32], ALU.add, (psr, R_const), (Rtmp,))
              B.act(tmp32[:, 32:64], tmp32[:, 0:32], AF.Exp, (Rtmp,), (Rtmp,))
              B.act(dtall[:, c, :], tmp32[:, 32:64], AF.Ln, (Rtmp,), (Rst[c],), bias=1.0)
              B.tt("dve", aall[:, c, :], dtall[:, c, :], aneg, ALU.mult, (Rst[c], R_const), (Rst[c],))
              ps, psr = psf()
              B.mm(ps[:, 0:16], cf[:, CF_TUI, :], aall[:, c, 0:16], True, True, (R_const, Rst[c]), (psr,))
              B.mm(ps[:, 16:32], cf[:, CF_TLI, :], aall[:, c, 16:32], True, True, (R_const, Rst[c]), (psr,))
              B.mm(ps[:, 32:48], cf[:, CF_TUS, :], aall[:, c, 0:16], True, True, (R_const, Rst[c]), (psr,))
              B.mm(ps[:, 48:64], cf[:, CF_TLS, :], aall[:, c, 16:32], True, True, (R_const, Rst[c]), (psr,))
              B.mm(ps[:, 64:96], cf[:, CF_ONES, :], aall[:, c, :], True, True, (R_const, Rst[c]), (psr,), inc=True)
              B.act(exall[:, c, :], ps[:, 0:64], AF.Exp, (psr,), (Rst[c],))
              B.act(cdec[:, c, :], ps[:, 64:96], AF.Exp, (psr,), (Rst[c],))

          def xbc_iter(j):
              blk, jj = j // 4, j % 4
              wbi = blk % 2
              if jj == 0:
                  wload(wb[wbi], win_v[:, :, C_XBC + blk * 512:C_XBC + (blk + 1) * 512], (Rwb[wbi],), f"wb{wbi}")
              pb_ = j % 2
              if j < 2:
                  B.ms("dve", pre[pb_][:, 0:2], 0.0, (), (Rpre[pb_],))
                  B.ms("dve", pre[pb_][:, 2050:2052], 0.0, (), (Rpre[pb_],))
              for tap in range(5):
                  B.ts("dve", diag[:, pb_, tap, :], cf[:, CF_ID, :],
                       pp[:, PP_CONVW + j * 5 + tap:PP_CONVW + j * 5 + tap + 1], None, ALU.mult, None,
                       (R_const,), (Rdiag[pb_],))
              for T in range(4):
                  ps, psr = psf()
                  for k in range(8):
                      B.mm(ps[:, :], wb[wbi][:, k, jj * 128:(jj + 1) * 128], hT[:, k, T * 512:(T + 1) * 512], k == 0, k == 7,
                           (Rwb[wbi],) + tuple(RhT4[T]), (psr,), inc=(k == 7))
                  B.cp("act", pre[pb_][:, 2 + T * 512:2 + (T + 1) * 512], ps[:, :], (psr,), (Rpre[pb_],))
              yield
              if j < 8:
                  dstF, dres = actT[pb_], None
              elif j < 10:
                  dstF = BT[:, j - 8, :]
              else:
                  dstF = CT[:, j - 10, :]
              for T in range(4):
                  ps, psr = psf()
                  for tap in range(5):
                      B.mm(ps[:, :], diag[:, pb_, tap, :], pre[pb_][:, T * 512 + tap:T * 512 + tap + 512], tap == 0, tap == 4,
                           (Rdiag[pb_], Rpre[pb_]), (psr,), inc=(tap == 4))
                  if j < 8:
                      wr = (RactT[pb_],)
                  elif j < 10:
                      wr = tuple(RBT[4 * T:4 * T + 4])
                  else:
                      wr = tuple(RCT[4 * T:4 * T + 4])
                  B.act(dstF[:, T * 512:(T + 1) * 512], ps[:, :], AF.Silu, (psr, R_const), wr,
                        bias=pp[:, PP_CONVB + j:PP_CONVB + j + 1])
              if j < 10:
                  for q4 in range(4):
                      pb, pbr = psb()
                      for i in range(4):
                          c = q4 * 4 + i
                          rd = (RactT[pb_], R_const) if j < 8 else (RBT[c], R_const)
                          B.tr(pb[:, i * 128:(i + 1) * 128], dstF[:, c * 128:(c + 1) * 128], ident, rd, (pbr,), inc=(i == 3))
                      if j < 8:
                          B.cp("act", lo[:, q4 * 4:(q4 + 1) * 4, j * 128:(j + 1) * 128],
                               pb[:, 0:512].rearrange("p (c f) -> p c f", c=4), (pbr,), tuple(Rlo[q4 * 4:q4 * 4 + 4]))
                      else:
                          B.cp("act", Btok[:, q4 * 4:(q4 + 1) * 4, (j - 8) * 128:(j - 7) * 128],
                               pb[:, 0:512].rearrange("p (c f) -> p c f", c=4), (pbr,), tuple(RBtok[q4 * 4:q4 * 4 + 4]))
          run_pipelined([xbc_iter(j) for j in range(12)])
          B.barrier()
          if STOP == 3:
              raise _Stop()

          w0 = O_WK
          Hs = vf(w0, 1024); w0 += 4096
          xd = [vb(w0 + i * 2048, 1024) for i in range(2)]; w0 += 4096
          wsm = vf(w0, 64); w0 += 256
          Ebuf = vb(w0, 4096).rearrange("p (q n) -> p q n", q=8); w0 += 8192
          rhsb2 = vb(w0, 2048); rhsb = rhsb2.rearrange("p (h n) -> p h n", h=16); w0 += 4096
          xdt = [vb(w0 + i * 2048, 1024) for i in range(2)]; w0 += 4096
          cbm = vb(w0, 512).rearrange("p (g d n) -> p g d n", g=2, d=2); w0 += 1024
          prevf = vb(w0, 1024); w0 += 2048
          szb = vb(w0, 1024); w0 += 2048
          t1 = vf(w0, 1024); w0 += 4096
          t2 = vf(w0, 1024); w0 += 4096
          gst = vf(w0, 8); w0 += 32
          ynb = vb(w0, 1024); w0 += 2048
          assert w0 - O_WK <= WK_SIZE, (w0 - O_WK, WK_SIZE)
          RH, Rxd, Rws, RE, Rrhs, Rxdt, Rcbm, Rpf, Rsz, Rt1, Rt2, Rg, Ryn = (Res(), RL(2), Res(), RL(8), Res(), RL(2),
                                                                              Res(), Res(), Res(), Res(), Res(), Res(), Res())

          def bc16(ap16):
              return ap16.unsqueeze(2).to_broadcast([128, 16, 64])

          def v3(ap1024):
              return ap1024.rearrange("p (h d) -> p h d", h=16)

          def state_prep(c, d, wcol, ecol, banks=None):
              B.tt("dve", wsm[:, d * 16:(d + 1) * 16], dtall[:, c, wcol:wcol + 16], exall[:, c, ecol:ecol + 16], ALU.mult,
                   (Rst[c],), (Rws,))
              B.tt("dve", v3(xd[d]), v3(lo[:, c, :]), bc16(wsm[:, d * 16:(d + 1) * 16]), ALU.mult, (Rlo[c], Rws), (Rxd[d],))
              pss = []
              for g in range(2):
                  ps, psr = psf() if banks is None else (PF[banks[g]], PFR[banks[g]])
                  B.mm(ps[:, :], Btok[:, c, g * 128:(g + 1) * 128], xd[d][:, g * 512:(g + 1) * 512], True, True,
                       (RBtok[c], Rxd[d]), (psr,), inc=True)
                  pss.append((ps, psr))
              return pss

          def state_apply(c, dcol, pss):
              B.tt("dve", v3(Hs), v3(Hs), bc16(cdec[:, c, dcol:dcol + 16]), ALU.mult, (RH, Rst[c]), (RH,))
              for g in range(2):
                  B.tt("dve", Hs[:, g * 512:(g + 1) * 512], Hs[:, g * 512:(g + 1) * 512], pss[g][0][:, :], ALU.add,
                       (RH, pss[g][1]), (RH,))

          B.ms("dve", Hs, 0.0, (), (RH,))

          def p1_iter(c):
              pss = state_prep(c, 1, 16, 48) if c > 0 else None
              yield
              B.cp("act", prevb[:, c, :], Hs, (RH,), (Rprevb[c],))
              if c > 0:
                  state_apply(c, 16, pss)

          run_pipelined([p1_iter(c) for c in range(NT - 1, -1, -1)])

          E2 = [Ebuf, vb(O_HI + 57344, 4096).rearrange("p (q n) -> p q n", q=8)]
          RE2 = [RE, RL(8)]
          xdt2 = [xdt, [vb(O_WK + 4096 + 2048, 1024), vb(O_MX + 30720, 1024)]]
          Rxdt2 = [Rxdt, [Rxd[1], Res()]]
          B.ms("dve", Hs, 0.0, (), (RH,))

          s1rot = [0]
          Rdm = Res()

          def s1bank():
              i = 4 + s1rot[0] % 2
              s1rot[0] += 1
              return PF[i], PFR[i]

          def p2_s1(c):
              tsl = slice(c * 128, (c + 1) * 128)
              pi = c % 2
              Eb, REb, xdtb, Rxdtb = E2[pi], RE2[pi], xdt2[pi], Rxdt2[pi]
              for g in range(2):
                  ps, psr = s1bank()
                  B.mm(ps[:, 0:128], BT[:, g, tsl], CT[:, g, tsl], True, True, (RBT[c], RCT[c]), (psr,), inc=True)
                  B.tt("dve", cbm[:, g, :, :], ps[:, 0:128].unsqueeze(1).to_broadcast([128, 2, 128]),
                       cb[:, CB_MF:CB_MF + 2, :], ALU.mult, (psr, R_const), (Rcbm,))
              for d in range(2):
                  B.tt("pool", v3(xdtb[d]), v3(lo[:, c, :]), bc16(dtall[:, c, d * 16:(d + 1) * 16]), ALU.mult,
                       (Rlo[c], Rst[c]), (Rxdtb[d],))
              for d in range(2):
                  tri = cf[:, CF_TUI, :] if d == 0 else cf[:, CF_TLI, :]
                  lsm = cb[:, CB_LSF, :] if d == 0 else cb[:, CB_LSB, :]
                  B.tt("pool", rhsb, aall[:, c, d * 16:(d + 1) * 16].unsqueeze(2).to_broadcast([128, 16, 128]),
                       tri.unsqueeze(1).to_broadcast([128, 16, 128]), ALU.mult, (Rst[c], R_const), (Rrhs,))
                  for q in range(4):
                      ps, psr = s1bank()
                      B.mm(ps[:, :], lsm, rhsb2[:, q * 512:(q + 1) * 512], True, True, (R_const, Rrhs), (psr,), inc=True)
                      qi = d * 4 + q
                      B.act(Eb[:, qi, :], ps[:, :], AF.Exp, (psr,), (REb[qi],))
                  yield
                  for q in range(4):
                      qi = d * 4 + q
                      g = q // 2
                      B.tt("dve", Eb[:, qi, :].rearrange("p (h n) -> p h n", h=4),
                           Eb[:, qi, :].rearrange("p (h n) -> p h n", h=4),
                           cbm[:, g, d, :].unsqueeze(1).to_broadcast([128, 4, 128]), ALU.mult, (REb[qi], Rcbm), (REb[qi],))

          def p2_s2(c):
              tsl = slice(c * 128, (c + 1) * 128)
              pi = c % 2
              Eb, REb, xdtb, Rxdtb = E2[pi], RE2[pi], xdt2[pi], Rxdt2[pi]
              B.cp("act", prevf, Hs, (RH,), (Rpf,))
              if c < NT - 1:
                  state_apply(c, 0, [(PF[2], PFR[2]), (PF[3], PFR[3])])
              for g in range(2):
                  ps, psr = PF[2 + g], PFR[2 + g]
                  B.mm(ps[:, :], CT[:, g, tsl], prevf[:, g * 512:(g + 1) * 512], True, True, (RCT[c], Rpf), (psr,), inc=True)
                  B.tt("dve", v3(t1)[:, g * 8:(g + 1) * 8, :], ps[:, :].rearrange("p (h d) -> p h d", h=8),
                       exall[:, c, g * 8:(g + 1) * 8].unsqueeze(2).to_broadcast([128, 8, 64]), ALU.mult,
                       (psr, Rst[c]), (Rt1,))
              yps = []
              for hf in range(2):
                  ps, psr = PF[hf], PFR[hf]
                  for h8 in range(8):
                      h = hf * 8 + h8
                      osl = ps[:, h8 * 64:(h8 + 1) * 64]
                      B.mm(osl, Eb[:, h // 4, (h % 4) * 128:(h % 4 + 1) * 128], xdtb[0][:, h * 64:(h + 1) * 64], True, False,
                           (REb[h // 4], Rxdtb[0]), (psr,))
                      B.mm(osl, Eb[:, 4 + h // 4, (h % 4) * 128:(h % 4 + 1) * 128], xdtb[1][:, h * 64:(h + 1) * 64], False,
                           False, (REb[4 + h // 4], Rxdtb[1]), (psr,))
                      B.mm(osl, DI[:, h, :], lo[:, c, h * 64:(h + 1) * 64], False, True, (RDI, Rlo[c]), (psr,), inc=(h8 == 7))
                  yps.append((ps, psr))
              for g in range(2):
                  ps, psr = PF[2 + g], PFR[2 + g]
                  B.mm(ps[:, :], CT[:, g, tsl], prevb[:, c, g * 512:(g + 1) * 512], True, True, (RCT[c], Rprevb[c]), (psr,),
                       inc=True)
                  B.tt("dve", v3(t2)[:, g * 8:(g + 1) * 8, :], ps[:, :].rearrange("p (h d) -> p h d", h=8),
                       exall[:, c, 16 + g * 8:16 + (g + 1) * 8].unsqueeze(2).to_broadcast([128, 8, 64]), ALU.mult,
                       (psr, Rst[c]), (Rt2,))
              yield
              for hf in range(2):
                  ps, psr = PF[2 + hf], PFR[2 + hf]
                  for k in range(8):
                      B.mm(ps[:, :], hT[:, k, tsl], Wz[:, k, hf * 512:(hf + 1) * 512], k == 0, k == 7, (RhT[c], RWz), (psr,),
                           inc=(k == 7))
                  B.act(szb[:, hf * 512:(hf + 1) * 512], ps[:, :], AF.Silu, (psr,), (Rsz,))
              B.act(gst[:, 7:8], cf[:, CF_ONES, 0:1], AF.Ln, (R_const,), (Rdm,))
              B.tt("dve", t1, t1, t2, ALU.add, (Rt1, Rt2), (Rt1,))
              for hf in range(2):
                  B.tt("dve", t1[:, hf * 512:(hf + 1) * 512], t1[:, hf * 512:(hf + 1) * 512], yps[hf][0][:, :], ALU.add,
                       (Rt1, yps[hf][1]), (Rt1,))
              yield
              B.tt("dve", t1, t1, szb, ALU.mult, (Rt1, Rsz), (Rt1,))
              B.ms("dve", gst[:, 0:6], 0.0, (), (Rg,))
              for g in range(2):
                  B.act(t2[:, g * 512:(g + 1) * 512], t1[:, g * 512:(g + 1) * 512], AF.Square, (Rt1, Rg), (Rt2, Rg),
                        accum=gst[:, g:g + 1])
              B.act(gst[:, 2:4], gst[:, 0:2], AF.Ln, (Rg,), (Rg,), scale=1.0 / 512, bias=EPS)
              B.act(gst[:, 4:6], gst[:, 2:4], AF.Exp, (Rg,), (Rg,), scale=-0.5)
              for g in range(2):
                  B.ts("dve", ynb[:, g * 512:(g + 1) * 512], t1[:, g * 512:(g + 1) * 512], gst[:, 4 + g:5 + g], None, ALU.mult,
                       None, (Rt1, Rg), (Ryn,))
              yield
              pb, pbr = psb()
              for k in range(8):
                  B.tr(pb[:, k * 128:(k + 1) * 128], ynb[:, k * 128:(k + 1) * 128], ident, (Ryn, R_const), (pbr,), inc=(k == 7))
              B.tt("dve", lo[:, c, :].rearrange("p (k t) -> p k t", k=8), pb[:, :].rearrange("p (k t) -> p k t", k=8),
                   pp[:, PP_SSDNW:PP_SSDNW + 8].unsqueeze(2).to_broadcast([128, 8, 128]), ALU.mult,
                   (pbr, R_const), (Rlo[c],))
              if c + 1 < NT - 1:
                  state_prep(c + 1, 0, 0, 32, banks=(2, 3))

          for _ in p2_s1(0):
              pass
          state_prep(0, 0, 0, 32, banks=(2, 3))
          for c in range(NT):
              g2_ = p2_s2(c)
              g1_ = p2_s1(c + 1) if c + 1 < NT else None
              for seg in range(4):
                  next(g2_, None)
                  if g1_ is not None and seg < 3:
                      next(g1_, None)
          pst["f"] = 0
          B.barrier()
          if STOP == 5:
              raise _Stop()

          qT = vb(O_HI, 4 * L).rearrange("p (k t) -> p k t", k=4)
          kz = vb(O_HI + 16384, 4 * L).rearrange("p (g h t) -> p g h t", g=2, h=2)
          vz = vb(O_HI + 32768, NT * 512).rearrange("p (c f) -> p c f", c=NT)
          cs = vf(O_HI + 49152, 2 * L).rearrange("p (a t) -> p a t", a=2)
          RqT, Rkd, Rvz, Rcs = RL(NT), RL(NT), RL(NT), Res()
          B.dma("sp", "c0", lambda e: e.dma_start(out=cs, in_=dcs), (), (Rcs,))
          wv = vb(O_WK, 8 * 512).rearrange("p (k n) -> p k n", k=8)
          wk = vb(O_WK + 8192, 8 * 256).rearrange("p (k n) -> p k n", k=8)
          wq = vb(O_WK + 12288, 8 * 512).rearrange("p (k n) -> p k n", k=8)
          Rwv, Rwk, Rwq = Res(), Res(), Res()
          def qkset(w0):
              d_ = {}
              d_["sq"] = vb(w0, 512); w0 += 1024
              d_["lnv"] = vf(w0, 512); w0 += 2048
              d_["qn"] = vb(w0, 512); w0 += 1024
              d_["ta"] = vf(w0, 512); w0 += 2048
              d_["tb"] = vf(w0, 512); w0 += 2048
              for nm in ("Rsq", "Rln", "Rqn", "Rta", "Rtb"):
                  d_[nm] = Res()
              return d_
          qks = [qkset(O_WK + 20480), qkset(O_WK)]
          w0 = O_WK + 28672
          Pt = [vb(w0 + i * 1536, 768) for i in range(2)]; w0 += 3072
          dpl = vf(w0, 512); w0 += 2048
          ktmp2 = [vb(w0 + i * 1024, 512) for i in range(2)]; w0 += 2048
          Rkt2 = RL(2)
          assert w0 - O_WK <= WK_SIZE
          RPt, Rdp = RL(2), Res()
          qkc = [0]

          B.ms("dve", vb(O_WK, 4096), 0.0, (), (Rwv,))
          for g in range(2):
              for hf in range(2):
                  c0 = (g * 2 + hf) * 128 + hf * 64
                  wload(wv[:, :, c0:c0 + 64], win_v[:, :, C_V + g * 64:C_V + (g + 1) * 64], (Rwv,), "wv")
              for hf in range(2):
                  wload(wk[:, :, g * 128 + hf * 64:g * 128 + (hf + 1) * 64], win_v[:, :, C_K + g * 64:C_K + (g + 1) * 64],
                        (Rwk,), "wk")
          wload(wq, win_v[:, :, C_Q:C_Q + 512], (Rwq,), "wq")
          for tt in range(NT):
              ps, psr = psf()
              for k in range(8):
                  B.mm(ps[:, :], hT[:, k, tt * 128:(tt + 1) * 128], wv[:, k, :], k == 0, k == 7, (RhT[tt], Rwv), (psr,),
                       inc=(k == 7))
              B.cp("act", vz[:, tt, :], ps[:, :], (psr,), (Rvz[tt],))

          def qk_chunk(wtile, wres, col0, dst_ap, dres4, pcol, T, post=None):
              S_ = qks[qkc[0] % 2]
              qkc[0] += 1
              sq, lnv, qn, ta, tb = S_["sq"], S_["lnv"], S_["qn"], S_["ta"], S_["tb"]
              Rsq, Rln, Rqn, Rta, Rtb = S_["Rsq"], S_["Rln"], S_["Rqn"], S_["Rta"], S_["Rtb"]
              tsl = slice(T * 512, (T + 1) * 512)
              psA, psAr = psf()
              for k in range(8):
                  B.mm(psA[:, :], wtile[:, k, col0:col0 + 128], hT[:, k, tsl], k == 0, k == 7, (wres,) + tuple(RhT4[T]),
                       (psAr,), inc=(k == 7))
              B.act(sq, psA[:, :], AF.Square, (psAr,), (Rsq,))
              psB, psBr = psf()
              B.mm(psB[:, :], cb[:, CB_BLK, :], sq, True, True, (R_const, Rsq), (psBr,), inc=True)
              rstd_from_ps(psB[:, :], psBr, 64, lnv, lnv, Rln, Rln)
              B.stt("dve", qn, psA[:, :], pp[:, pcol:pcol + 1], lnv, ALU.mult, ALU.mult, (psAr, R_const, Rln), (Rqn,))
              yield
              psR, psRr = psf()
              B.mm(psR[:, :], cb[:, CB_ROT, :], qn, True, True, (R_const, Rqn), (psRr,), inc=True)
              B.tt("dve", ta, psR[:, :], cs[:, 1, tsl], ALU.mult, (psRr, Rcs), (Rta,))
              B.tt("dve", tb, qn, cs[:, 0, tsl], ALU.mult, (Rqn, Rcs), (Rtb,))
              B.tt("dve", dst_ap, ta, tb, ALU.add, (Rta, Rtb), tuple(dres4))
              if post is not None:
                  post()

          B.barrier()
          def kpost(g, T):
              def f():
                  for hf in range(2):
                      B.ts("dve", kz[:, g, hf, T * 512:(T + 1) * 512], ktmp2[g], pp[:, PP_M0 + hf:PP_M0 + hf + 1], None,
                           ALU.mult, None, (Rkt2[g], R_const), tuple(Rkd[4 * T:4 * T + 4]))
              return f
          gens = []
          for T in range(4):
              for g in range(2):
                  gens.append(qk_chunk(wk, Rwk, g * 128, ktmp2[g], (Rkt2[g],), PP_KW, T, post=kpost(g, T)))
              for c4 in range(4):
                  gens.append(qk_chunk(wq, Rwq, c4 * 128, qT[:, c4, T * 512:(T + 1) * 512], RqT[4 * T:4 * T + 4], PP_QW, T))
          run_pipelined(gens)

          if STOP == 5.5:
              B.barrier()
              raise _Stop()
          pst["n"] = 4
          pst["f"] = 0
          psN, psNr = PF[4], PFR[4]
          psD, psDr = PF[5], PFR[5]

          def att_iter(n, c4):
              qsl = slice(n * 128, (n + 1) * 128)
              js = [j for j in (n - 1, n, n + 1) if 0 <= j < NT]
              g = c4 // 2
              pi = c4 % 2
              psS0, psS0r = psf()
              psS1, psS1r = psf()
              for ji, j in enumerate(js):
                  for hf in range(2):
                      slot = ji * 2 + hf
                      pS, pSr = (psS0, psS0r) if slot < 4 else (psS1, psS1r)
                      so = (slot % 4) * 128
                      B.mm(pS[:, so:so + 128], kz[:, g, hf, j * 128:(j + 1) * 128],
                           qT[:, c4, qsl], True, True, (Rkd[j], RqT[n]), (pSr,),
                           inc=(slot == 3 or slot == len(js) * 2 - 1))
              n0 = min(4, len(js) * 2)
              B.act(Pt[pi][:, 0:n0 * 128], psS0[:, 0:n0 * 128], AF.Exp, (psS0r,), (RPt[pi],), scale=0.125)
              if len(js) * 2 > 4:
                  B.act(Pt[pi][:, 512:768], psS1[:, 0:256], AF.Exp, (psS1r,), (RPt[pi],), scale=0.125)
              for ji, j in enumerate(js):
                  if j != n:
                      mk = cb[:, CB_MPREV, :] if j < n else cb[:, CB_MNEXT, :]
                      pv = Pt[pi][:, ji * 256:(ji + 1) * 256].rearrange("p (h q) -> p h q", h=2)
                      B.tt("dve", pv, pv, mk.unsqueeze(1).to_broadcast([128, 2, 128]), ALU.mult, (RPt[pi], R_const),
                           (RPt[pi],))
              yield
              nmm = len(js) * 2
              i = 0
              for ji, j in enumerate(js):
                  for hf in range(2):
                      B.mm(psN[:, c4 * 128:(c4 + 1) * 128], vz[:, j, (g * 2 + hf) * 128:(g * 2 + hf + 1) * 128],
                           Pt[pi][:, (ji * 2 + hf) * 128:(ji * 2 + hf + 1) * 128], i == 0, i == nmm - 1,
                           (Rvz[j], RPt[pi]), (psNr,))
                      i += 1
              i = 0
              for ji, j in enumerate(js):
                  for hf in range(2):
                      B.mm(psD[:, c4 * 128:(c4 + 1) * 128], cb[:, CB_OZ0 + hf, :],
                           Pt[pi][:, (ji * 2 + hf) * 128:(ji * 2 + hf + 1) * 128], i == 0, i == nmm - 1,
                           (R_const, RPt[pi]), (psDr,), inc=(i == nmm - 1))
                      i += 1
              if c4 == 3:
                  B.tt("dve", dpl.rearrange("p (c q) -> p c q", c=4), psD[:, :].rearrange("p (c q) -> p c q", c=4),
                       esink.unsqueeze(2).to_broadcast([128, 4, 128]), ALU.add, (psDr, R_const), (Rdp,))
                  B.op("dve", lambda e: e.reciprocal(out=dpl, in_=dpl), (Rdp,), (Rdp,))
                  B.tt("dve", mxA[:, :, qsl], psN[:, :].rearrange("p (c q) -> p c q", c=4),
                       dpl.rearrange("p (c q) -> p c q", c=4), ALU.mult, (psNr, Rdp), ())

          run_pipelined([att_iter(n, c4) for n in range(NT) for c4 in range(4)])
          pst["n"] = 6
          pst["f"] = 0
          B.barrier()
          if STOP == 6:
              raise _Stop()

          memT = vb(O_HI, 8 * 256).rearrange("p (k t) -> p k t", k=8)
          kmT = vb(O_HI + 4096, 4 * 256).rearrange("p (h t) -> p h t", h=4)
          vm = vb(O_HI + 6144, 2 * 512).rearrange("p (m f) -> p m f", m=2)
          kmn = vb(O_HI + 8192, 512)
          kst = vf(O_HI + 9216, 16)
          wkv = vb(O_HI + 16384, 8 * 1024).rearrange("p (k n) -> p k n", k=8)
          wqx = vb(O_HI + 32768, 8 * 512).rearrange("p (k n) -> p k n", k=8)
          RmemT, RkmT, Rvm, Rkmn, Rkst, Rwkv, Rwqx = RL(2), Res(), RL(2), Res(), Res(), Res(), Res()
          wload(wkv, wkv_v, (Rwkv,), "wkv")
          wload(wqx, win_v[:, :, C_QX:C_QX + 512], (Rwqx,), "wqx")
          hbuild(lambda tt: dmem[s, tt * 128:(tt + 1) * 128, :], 2, memT, RmemT, PP_NWMEM, O_WK)
          def xset(w0):
              d_ = {}
              d_["sq"] = vb(w0, 512); w0 += 1024
              d_["lnv"] = vf(w0, 512); w0 += 2048
              d_["qn"] = vb(w0, 512); w0 += 1024
              d_["rD"] = vf(w0, 512); w0 += 2048
              d_["Px"] = [vb(w0 + i * 1024, 512) for i in range(2)]; w0 += 2048
              for nm in ("Rsq", "Rln", "Rqn", "RrD"):
                  d_[nm] = Res()
              d_["RPx"] = RL(2)
              return d_
          xs_ = [xset(O_WK + 16384), xset(O_WK + 16384 + 8192)]
          tk = vf(O_WK + 32768, 512)
          Rtk = Res()
          assert 32768 + 2048 <= WK_SIZE
          for mt in range(2):
              msl = slice(mt * 128, (mt + 1) * 128)
              psK, psKr = psf()
              for k in range(8):
                  B.mm(psK[:, :], memT[:, k, msl], wkv[:, k, 0:512], k == 0, k == 7, (RmemT[mt], Rwkv), (psKr,), inc=(k == 7))
              B.ms("dve", kst, 0.0, (), (Rkst,))
              for h in range(4):
                  B.act(tk[:, h * 128:(h + 1) * 128], psK[:, h * 128:(h + 1) * 128], AF.Square, (psKr, Rkst), (Rtk, Rkst),
                        accum=kst[:, h:h + 1])
              B.act(kst[:, 4:8], kst[:, 0:4], AF.Ln, (Rkst,), (Rkst,), scale=1.0 / 128, bias=EPS)
              B.act(kst[:, 8:12], kst[:, 4:8], AF.Exp, (Rkst,), (Rkst,), scale=-0.5)
              B.tt("dve", tk.rearrange("p (h d) -> p h d", h=4), psK[:, :].rearrange("p (h d) -> p h d", h=4),
                   kst[:, 8:12].unsqueeze(2).to_broadcast([128, 4, 128]), ALU.mult, (psKr, Rkst), (Rtk,))
              B.tt("dve", kmn.rearrange("p (h d) -> p h d", h=4), tk.rearrange("p (h d) -> p h d", h=4),
                   pp[:, PP_XKW:PP_XKW + 128].unsqueeze(1).to_broadcast([128, 4, 128]), ALU.mult, (Rtk, R_const), (Rkmn,))
              pb, pbr = psb()
              for h in range(4):
                  B.tr(pb[:, h * 128:(h + 1) * 128], kmn[:, h * 128:(h + 1) * 128], ident, (Rkmn, R_const), (pbr,), inc=(h == 3))
              B.cp("act", kmT[:, :, msl], pb[:, 0:512].rearrange("p (h t) -> p h t", h=4), (pbr,), (RkmT,))
              psV, psVr = psf()
              for k in range(8):
                  B.mm(psV[:, :], memT[:, k, msl], wkv[:, k, 512:1024], k == 0, k == 7, (RmemT[mt], Rwkv), (psVr,), inc=(k == 7))
              B.cp("act", vm[:, mt, :], psV[:, :], (psVr,), (Rvm[mt],))
          def xat_iter(T, h, xc):
              tsl = slice(T * 512, (T + 1) * 512)
              S_ = xs_[xc % 2]
              sq, lnv, qn, rD, Px = S_["sq"], S_["lnv"], S_["qn"], S_["rD"], S_["Px"]
              Rsq, Rln, Rqn, RrD, RPx = S_["Rsq"], S_["Rln"], S_["Rqn"], S_["RrD"], S_["RPx"]
              psA, psAr = psf()
              for k in range(8):
                  B.mm(psA[:, :], wqx[:, k, h * 128:(h + 1) * 128], hT[:, k, tsl], k == 0, k == 7, (Rwqx,) + tuple(RhT4[T]),
                       (psAr,), inc=(k == 7))
              B.act(sq, psA[:, :], AF.Square, (psAr,), (Rsq,))
              psB, psBr = psf()
              B.mm(psB[:, :], cb[:, CB_ONES, :], sq, True, True, (R_const, Rsq), (psBr,), inc=True)
              rstd_from_ps(psB[:, :], psBr, 128, lnv, lnv, Rln, Rln)
              B.stt("dve", qn, psA[:, :], pp[:, PP_XQW:PP_XQW + 1], lnv, ALU.mult, ALU.mult, (psAr, R_const, Rln), (Rqn,))
              yield
              for mt in range(2):
                  psS, psSr = psf()
                  B.mm(psS[:, :], kmT[:, h, mt * 128:(mt + 1) * 128], qn, True, True, (RkmT, Rqn), (psSr,), inc=True)
                  B.act(Px[mt], psS[:, :], AF.Exp, (psSr,), (RPx[mt],), scale=128 ** -0.5)
              psN, psNr = psf()
              psD, psDr = psf()
              for mt in range(2):
                  B.mm(psN[:, :], vm[:, mt, h * 128:(h + 1) * 128], Px[mt], mt == 0, mt == 1, (Rvm[mt], RPx[mt]), (psNr,))
              for mt in range(2):
                  B.mm(psD[:, :], cb[:, CB_ONES, :], Px[mt], mt == 0, mt == 1, (R_const, RPx[mt]), (psDr,), inc=(mt == 1))
              B.op("dve", lambda e, o_=rD, i_=psD[:, :]: e.reciprocal(out=o_, in_=i_), (psDr,), (RrD,))
              B.tt("dve", mxX[:, h, tsl], psN[:, :], rD, ALU.mult, (psNr, RrD), ())

          run_pipelined([xat_iter(T, h, T * 4 + h) for T in range(4) for h in range(4)])
          B.barrier()
          if STOP == 7:
              raise _Stop()

          if DEBUG:
              B.dma("sp", "dbg", lambda e: e.dma_start(out=ddbg[s, :, 0:4, :], in_=mxA), (), ())
              B.dma("sp", "dbg", lambda e: e.dma_start(out=ddbg[s, :, 12:16, :], in_=mxX), (), ())
              for c in range(NT):
                  B.dma("sp", "dbg", lambda e, c=c: e.dma_start(out=ddbg[s, :, 4:12, c * 128:(c + 1) * 128],
                                                               in_=lo[:, c, :].rearrange("p (k t) -> p k t", k=8)), (), ())
              B.barrier()

          x1 = vf(O_HI, NT * 1024).rearrange("p (c f) -> p c f", c=NT)
          Rx1 = RL(NT)
          wo = vb(O_HT, 16 * 1024).rearrange("p (k n) -> p k n", k=16)
          Rwo = Res()
          for kh in range(2):
              wload(wo[:, kh * 8:(kh + 1) * 8, :], wout_v[:, kh * 8:(kh + 1) * 8, :], (Rwo,), "wo")
          xt = [vf(O_WK + 24576 + i * 4096, 1024) for i in range(2)]
          Rxt = RL(2)
          Rmx = Res()
          Rh2 = RL(NT)
          hgens = hbuild(lambda tt: x1[:, tt, :], NT, None, Rh2, PP_NWMLP, O_WK, src_res=Rx1,
                         dst_fn=lambda tt: lo[:, tt, :].rearrange("p (k t) -> p k t", k=8), run=False)
          prevg = None
          for tt in range(NT):
              tsl = slice(tt * 128, (tt + 1) * 128)
              b = tt % 2
              B.dma("sp", f"xt{b}", lambda e, o_=xt[b], i_=dx[s, tsl, :]: e.dma_start(out=o_, in_=i_), (), (Rxt[b],))
              for hf in range(2):
                  ps, psr = psf()
                  for kc in range(16):
                      if kc < 4:
                          lhs = mxA[:, kc, tsl]
                      elif kc < 12:
                          lhs = lo[:, tt, (kc - 4) * 128:(kc - 3) * 128]
                      else:
                          lhs = mxX[:, kc - 12, tsl]
                      B.mm(ps[:, :], lhs, wo[:, kc, hf * 512:(hf + 1) * 512], kc == 0, kc == 15, (Rwo, Rmx, Rh2[tt]), (psr,),
                           inc=(kc == 15))
                  B.tt("dve", x1[:, tt, hf * 512:(hf + 1) * 512], ps[:, :], xt[b][:, hf * 512:(hf + 1) * 512], ALU.add,
                       (psr, Rxt[b]), (Rx1[tt],))
              next(hgens[tt])
              if prevg is not None:
                  for _ in prevg:
                      pass
              prevg = hgens[tt]
          for _ in prevg:
              pass
          if STOP == 8:
              B.barrier()
              raise _Stop()
          Rh24 = [[Rh2[4 * T + i] for i in range(4)] for T in range(4)]
          if STOP == 8.5:
              B.barrier()
              raise _Stop()
          wbase = [O_MX, O_HT]
          wu = [vb(wbase[i], 8 * 1024).rearrange("p (k n) -> p k n", k=8) for i in range(2)]
          wd = [vb(wbase[i] + 16384, 8 * 1024).rearrange("p (k n) -> p k n", k=8) for i in range(2)]
          Rwu, Rwd = [Rmx, Rwo], [Res(), Res()]
          Rwd[0].r, Rwd[1].r = Rmx.r, Rwo.r
          uT = [vb(O_WK + 16384 + i * 8192, 8 * 512).rearrange("p (k t) -> p k t", k=8) for i in range(2)]
          RuT = RL(2)
          rl = [vb(O_WK + 32768 + i * 1024, 512) for i in range(2)]
          Rrl = RL(2)
          assert 32768 + 2048 <= WK_SIZE
          ui = 0

          def mlp_wload(fb_):
              wi_ = fb_ % 2
              wload(wu[wi_], wup_v[:, :, fb_ * 1024:(fb_ + 1) * 1024], (Rwu[wi_],), f"wu{wi_}")
              wload(wd[wi_], wdn_v[:, fb_ * 8:(fb_ + 1) * 8, :], (Rwd[wi_],), f"wd{wi_}")
          mlp_wload(0)
          mlp_wload(1)
          def mlp_iter(fb, T, u):
              wi = fb % 2
              tsl = slice(T * 512, (T + 1) * 512)
              for fc in range(8):
                  ps, psr = psf()
                  for k in range(8):
                      B.mm(ps[:, :], wu[wi][:, k, fc * 128:(fc + 1) * 128], lo[:, 4 * T:4 * T + 4, k * 128:(k + 1) * 128], k == 0,
                           k == 7, (Rwu[wi],) + tuple(Rh24[T]), (psr,), inc=(k == 7))
                  r = fc % 2
                  B.act(rl[r], ps[:, :], AF.Relu, (psr,), (Rrl[r],))
                  B.tt("pool", uT[u][:, fc, :], rl[r], rl[r], ALU.mult, (Rrl[r],), (RuT[u],) if u == 0 else (RuT[u], Rxt[0], Rxt[1]))
              yield
              for ti in range(4):
                  tt = T * 4 + ti
                  for hf in range(2):
                      ps, psr = psf()
                      for fc in range(8):
                          B.mm(ps[:, :], uT[u][:, fc, ti * 128:(ti + 1) * 128], wd[wi][:, fc, hf * 512:(hf + 1) * 512],
                               fc == 0, fc == 7, (RuT[u], Rwd[wi]), (psr,), inc=(fc == 7))
                      B.tt("dve", x1[:, tt, hf * 512:(hf + 1) * 512], x1[:, tt, hf * 512:(hf + 1) * 512], ps[:, :], ALU.add,
                           (Rx1[tt], psr), (Rx1[tt],))
                  if fb == 3:
                      B.dma("sp", "out", lambda e, o_=dout[s, tt * 128:(tt + 1) * 128, :], i_=x1[:, tt, :]:
                            e.dma_start(out=o_, in_=i_), (Rx1[tt],), ())
              if T == 3 and fb + 2 < 4:
                  mlp_wload(fb + 2)

          run_pipelined([mlp_iter(fb, T, (fb * 4 + T) % 2) for fb in range(4) for T in range(4)])
      except _Stop:
        pass
    B.barrier()
    for k, v in list(B.cnt.items()):
        B.need("sp", (k, v))

    keys = list(B.cnt.keys())
    sems = {k: es.enter_context(nc.semaphore(f"s_{k}")) for k in keys}

    def run(e, stream):
        for it in stream:
            if it[0] == 0:
                e.wait_ge(sems[it[1]], it[2])
            else:
                ins = it[1](e)
                if it[2] is not None:
                    ins.then_inc(sems[it[2]], it[3])

    with nc.Block() as block:
        @block.tensor
        def _(e):
            run(e, B.streams["pe"])

        @block.scalar
        def _(e):
            run(e, B.streams["act"])

        @block.vector
        def _(e):
            run(e, B.streams["dve"])

        @block.gpsimd
        def _(e):
            run(e, B.streams["pool"])

        @block.sync
        def _(e):
            run(e, B.streams["sp"])
    es.close()
    return nc


def make_consts():
    j = np.arange(128)[:, None]
    l = np.arange(128)[None, :]
    cbm = np.zeros((128, NCB, 128), np.float32)
    cbm[:, CB_ID] = (j == l)
    cbm[:, CB_ONES] = 1.0
    cbm[:, CB_BLK] = (j // 64 == l // 64)
    cbm[:, CB_OZ0] = (l < 64)
    cbm[:, CB_OZ1] = (l >= 64)
    cbm[:, CB_LSF] = (j > l)
    cbm[:, CB_LSB] = (j < l)
    rot = np.zeros((128, 128), np.float32)
    for hb in (0, 64):
        for d in range(8):
            rot[hb + d + 8, hb + d] = -1.0
            rot[hb + d, hb + d + 8] = 1.0
    cbm[:, CB_ROT] = rot
    cbm[:, CB_MPREV] = (j >= l)
    cbm[:, CB_MNEXT] = (j <= l)
    cbm[:, CB_MF] = (l >= j)
    cbm[:, CB_MB] = (l <= j)
    cfm = np.zeros((128, NCF, 128), np.float32)
    cfm[:, CF_TUI] = (j <= l)
    cfm[:, CF_TLI] = (j >= l)
    cfm[:, CF_TUS] = (j > l)
    cfm[:, CF_TLS] = (j < l)
    cfm[:, CF_ONES] = 1.0
    cfm[:, CF_ID] = (j == l)
    inv = 500000.0 ** (-np.arange(0, 16, 2, dtype=np.float32) / 16)
    t = np.arange(L, dtype=np.float32)
    ang = t[None, :] * inv[:, None]
    cs = np.zeros((128, 2, L), np.float32)
    cs[:, 0, :] = 1.0
    for p in range(128):
        d = p % 64
        if d < 16:
            cs[p, 0] = np.cos(ang[d % 8])
            cs[p, 1] = np.sin(ang[d % 8])
    return cbm.astype(ml_dtypes.bfloat16), cfm, cs


def pack_params(inp):
    pp = np.zeros((128, NPP), np.float32)
    p = np.arange(128)
    pp[:, PP_NWMIX:PP_NWMIX + 8] = inp["norm_mix_w"][0].reshape(8, 128).T
    pp[:, PP_NWMLP:PP_NWMLP + 8] = inp["norm_mlp_w"][0].reshape(8, 128).T
    pp[:, PP_NWMEM:PP_NWMEM + 8] = inp["mem_norm_w"][0].reshape(8, 128).T
    pp[:, PP_SSDNW:PP_SSDNW + 8] = inp["ssd_norm_w"][0].reshape(8, 128).T
    cw = inp["conv_w"][0]
    pp[:, PP_CONVW:PP_CONVW + 60] = cw.reshape(5, 12, 128).transpose(2, 1, 0).reshape(128, 60)
    pp[:, PP_CONVB:PP_CONVB + 12] = inp["conv_b"][0].reshape(12, 128).T
    pp[:, PP_QW] = inp["q_norm_w"][0][p % 64]
    pp[:, PP_KW] = inp["k_norm_w"][0][p % 64]
    pp[:, PP_XQW] = inp["xq_norm_w"][0]
    for c in range(4):
        pp[:, PP_SINK + c] = inp["attn_sink"][0][2 * c + p // 64]
    pp[:, PP_DTB:PP_DTB + 16] = inp["dt_bias_f"][0][None, :]
    pp[:, PP_DTB + 16:PP_DTB + 32] = inp["dt_bias_b"][0][None, :]
    pp[:, PP_ALOG:PP_ALOG + 16] = inp["a_log_f"][0][None, :]
    pp[:, PP_ALOG + 16:PP_ALOG + 32] = inp["a_log_b"][0][None, :]
    pp[:, PP_SSDD:PP_SSDD + 16] = inp["ssd_d"][0][None, :]
    pp[:, PP_XKW:PP_XKW + 128] = inp["xk_norm_w"][0][None, :]
    pp[:, PP_M0] = (p < 64)
    pp[:, PP_M1] = (p >= 64)
    return pp


_NC_CACHE = {}


def kernel(**inputs):
    inp = {k: np.asarray(v) for k, v in inputs.items()}
    if "nc" not in _NC_CACHE:
        _NC_CACHE["nc"] = build_program()
    nc = _NC_CACHE["nc"]
    cbm, cfm, cs = make_consts()
    pp = pack_params(inp)
    shared = {
        "w_in": np.ascontiguousarray(inp["w_in"][0]),
        "w_mem_kv": np.ascontiguousarray(inp["w_mem_kv"][0]),
        "w_out": np.ascontiguousarray(inp["w_out"][0]),
        "w_up": np.ascontiguousarray(inp["w_mlp_up"][0]),
        "w_down": np.ascontiguousarray(inp["w_mlp_down"][0]),
        "cbf": cbm, "cf32": cfm, "pp": pp, "cossin": cs,
    }
    in_maps = []
    for c in range(NCORES):
        m = dict(shared)
        m["x"] = np.ascontiguousarray(inp["x"][c * SEQ_PER_CORE:(c + 1) * SEQ_PER_CORE])
        m["mem"] = np.ascontiguousarray(inp["mem"][c * SEQ_PER_CORE:(c + 1) * SEQ_PER_CORE])
        in_maps.append(m)
    res = run_bass_kernel_spmd(nc, in_maps, core_ids=list(range(NCORES)))
    out = np.concatenate([np.asarray(r["out"]) for r in res.results], axis=0)
    return out.astype(np.float32)
```

```python
import numpy as np
import ml_dtypes
from contextlib import ExitStack
import concourse.bass as bass
import concourse.mybir as mybir
from concourse.bass_utils import run_bass_kernel_spmd

F32 = mybir.dt.float32
BF16 = mybir.dt.bfloat16
AF = mybir.ActivationFunctionType
ALU = mybir.AluOpType

NCORES = 8
SEQ_PER_CORE = 2
L = 2048
D = 1024
NT = 16
EPS = 1e-6
DEBUG = False
STOP = 99


class _Stop(Exception):
    pass
NSEQ_RUN = SEQ_PER_CORE

C_Q, C_K, C_V, C_Z, C_XBC, C_DT, C_QX = 0, 512, 640, 768, 1792, 3328, 3360

(CB_ID, CB_ONES, CB_BLK, CB_OZ0, CB_OZ1, CB_LSF, CB_LSB, CB_ROT, CB_MPREV, CB_MNEXT, CB_MF, CB_MB) = range(12)
NCB = 12
(CF_TUI, CF_TLI, CF_TUS, CF_TLS, CF_ONES, CF_ID) = range(6)
NCF = 6
PP_NWMIX, PP_NWMLP, PP_NWMEM, PP_SSDNW = 0, 8, 16, 24
PP_CONVW = 32
PP_CONVB = 92
PP_QW, PP_KW, PP_XQW = 104, 105, 106
PP_SINK = 107
PP_DTB = 111
PP_ALOG = 143
PP_SSDD = 175
PP_XKW = 191
PP_M0, PP_M1 = 320, 321
NPP = 322

ENG = ["pe", "act", "dve", "pool", "sp"]


class Res:
    __slots__ = ("w", "r")

    def __init__(self):
        self.w = None
        self.r = {}


def RL(n):
    return [Res() for _ in range(n)]


class Bld:
    def __init__(self):
        self.streams = {e: [] for e in ENG}
        self.cnt = {}
        self.known = {e: {} for e in ENG}
        self.psi = 0

    def need(self, eng, tok, skip_same=False):
        if tok is None:
            return
        k, v = tok
        if skip_same and k == eng:
            return
        if self.known[eng].get(k, 0) >= v:
            return
        self.known[eng][k] = v
        self.streams[eng].append((0, k, v))

    def op(self, eng, fn, reads=(), writes=(), inc=True):
        for r in reads:
            self.need(eng, r.w)
        for w in writes:
            self.need(eng, w.w, True)
            for k, v in w.r.items():
                self.need(eng, (k, v), True)
        c = self.cnt.get(eng, 0) + 1
        if inc:
            self.cnt[eng] = c
        self.streams[eng].append((1, fn, eng if inc else None, 1))
        for r in reads:
            r.r[eng] = c
        for w in writes:
            w.w = (eng, c)
            w.r = {}

    def dma(self, q, chan, fn, reads=(), writes=()):
        for r in reads:
            self.need(q, r.w)
        for w in writes:
            if not (w.w is not None and w.w[0] == chan):
                self.need(q, w.w)
            for k, v in w.r.items():
                self.need(q, (k, v))
        c = self.cnt.get(chan, 0) + 16
        self.cnt[chan] = c
        self.streams[q].append((1, fn, chan, 16))
        for r in reads:
            r.r[chan] = c
        for w in writes:
            w.w = (chan, c)
            w.r = {}

    def barrier(self):
        toks = list(self.cnt.items())
        for e in ENG:
            for t in toks:
                self.need(e, t, True)

    def mm(self, out, lhsT, rhs, start, stop, reads, writes, inc=False):
        self.op("pe", lambda e: e.matmul(out, lhsT=lhsT, rhs=rhs, start=start, stop=stop), reads, writes, inc)

    def tr(self, out, in_, ident, reads, writes, inc=False):
        self.op("pe", lambda e: e.transpose(out=out, in_=in_, identity=ident), reads, writes, inc)

    def act(self, out, in_, func, reads, writes, scale=1.0, bias=0.0, accum=None):
        if accum is None:
            self.op("act", lambda e: e.activation(out=out, in_=in_, func=func, bias=bias, scale=scale), reads, writes)
        else:
            self.op("act", lambda e: e.activation(out=out, in_=in_, func=func, bias=bias, scale=scale,
                                                  accum_out=accum), reads, writes)

    def tt(self, eng, out, in0, in1, op, reads, writes):
        self.op(eng, lambda e: e.tensor_tensor(out=out, in0=in0, in1=in1, op=op), reads, writes)

    def ts(self, eng, out, in0, s1, s2, op0, op1, reads, writes):
        if s2 is None:
            self.op(eng, lambda e: e.tensor_scalar(out=out, in0=in0, scalar1=s1, scalar2=None, op0=op0), reads, writes)
        else:
            self.op(eng, lambda e: e.tensor_scalar(out=out, in0=in0, scalar1=s1, scalar2=s2, op0=op0, op1=op1),
                    reads, writes)

    def stt(self, eng, out, in0, scalar, in1, op0, op1, reads, writes):
        self.op(eng, lambda e: e.scalar_tensor_tensor(out=out, in0=in0, scalar=scalar, in1=in1, op0=op0, op1=op1),
                reads, writes)

    def cp(self, eng, out, in_, reads, writes):
        if eng == "act":
            self.op("act", lambda e: e.activation(out=out, in_=in_, func=AF.Copy), reads, writes)
        else:
            self.op(eng, lambda e: e.tensor_copy(out=out, in_=in_), reads, writes)

    def ms(self, eng, out, val, reads, writes):
        self.op(eng, lambda e: e.memset(out, val), reads, writes)


def run_pipelined(gens):
    prev = None
    for g in gens:
        next(g)
        if prev is not None:
            for _ in prev:
                pass
        prev = g
    if prev is not None:
        for _ in prev:
            pass


def build_program():
    nc = bass.Bass("TRN2", target_bir_lowering=False)
    dx = nc.dram_tensor("x", [SEQ_PER_CORE, L, D], F32, kind="ExternalInput").ap()
    dmem = nc.dram_tensor("mem", [SEQ_PER_CORE, 256, D], F32, kind="ExternalInput").ap()
    dwin = nc.dram_tensor("w_in", [D, 3872], F32, kind="ExternalInput").ap()
    dwkv = nc.dram_tensor("w_mem_kv", [D, 1024], F32, kind="ExternalInput").ap()
    dwout = nc.dram_tensor("w_out", [2048, D], F32, kind="ExternalInput").ap()
    dwup = nc.dram_tensor("w_up", [D, 4096], F32, kind="ExternalInput").ap()
    dwdn = nc.dram_tensor("w_down", [4096, D], F32, kind="ExternalInput").ap()
    dcb = nc.dram_tensor("cbf", [128, NCB, 128], BF16, kind="ExternalInput").ap()
    dcf = nc.dram_tensor("cf32", [128, NCF, 128], F32, kind="ExternalInput").ap()
    dpp = nc.dram_tensor("pp", [128, NPP], F32, kind="ExternalInput").ap()
    dcs = nc.dram_tensor("cossin", [128, 2, L], F32, kind="ExternalInput").ap()
    dout = nc.dram_tensor("out", [SEQ_PER_CORE, L, D], F32, kind="ExternalOutput").ap()
    if DEBUG:
        ddbg = nc.dram_tensor("dbg", [SEQ_PER_CORE, 128, 16, L], BF16, kind="ExternalOutput").ap()

    win_v = dwin.rearrange("(k p) n -> p k n", p=128)
    wkv_v = dwkv.rearrange("(k p) n -> p k n", p=128)
    wout_v = dwout.rearrange("(k p) n -> p k n", p=128)
    wup_v = dwup.rearrange("(k p) n -> p k n", p=128)
    wdn_v = dwdn.rearrange("(k p) n -> p k n", p=128)

    B = Bld()
    es = ExitStack()
    ARENA_ELEMS = 106400
    arena = es.enter_context(nc.sbuf_tensor("arena", [128, ARENA_ELEMS], BF16))
    PF = [es.enter_context(nc.psum_tensor(f"pf{i}", [128, 512], F32)) for i in range(6)]
    PB = [es.enter_context(nc.psum_tensor(f"pb{i}", [128, 1024], BF16)) for i in range(2)]
    PFR = RL(6)
    PBR = RL(2)
    pst = {"f": 0, "b": 0, "n": 6}

    def psf():
        i = pst["f"] % pst["n"]
        pst["f"] = (i + 1) % pst["n"]
        return PF[i], PFR[i]

    def psb():
        i = pst["b"]
        pst["b"] = (i + 1) % 2
        return PB[i], PBR[i]

    def vb(off, n):
        assert off % 4 == 0 and off // 2 + n <= ARENA_ELEMS, (off, n)
        return arena[:, off // 2: off // 2 + n]

    def vf(off, n):
        assert off % 4 == 0 and off // 2 + 2 * n <= ARENA_ELEMS, (off, n)
        return arena[:, off // 2: off // 2 + 2 * n].bitcast(F32)

    o = 0
    O_CB = o; o += NCB * 256
    O_CF = o; o += NCF * 512
    O_PP = o; o += NPP * 4
    O_SM = o; o += 1024
    O_HT = o; o += 32768
    O_MX = o; o += 32768
    O_LO = o; o += 32768
    O_HI = o; o += 65536
    O_WK = o
    WK_SIZE = ARENA_ELEMS * 2 - O_WK
    assert WK_SIZE >= 33000, WK_SIZE

    cb = vb(O_CB, NCB * 128).rearrange("p (m n) -> p m n", m=NCB)
    cf = vf(O_CF, NCF * 128).rearrange("p (m n) -> p m n", m=NCF)
    pp = vf(O_PP, NPP)
    esink = vf(O_SM, 4)
    aneg = vf(O_SM + 16, 32)
    R_const = Res()

    hT = vb(O_HT, 8 * L).rearrange("p (k t) -> p k t", k=8)
    mxA = vb(O_MX, 4 * L).rearrange("p (k t) -> p k t", k=4)
    mxX = vb(O_MX + 16384, 4 * L).rearrange("p (k t) -> p k t", k=4)
    lo = vb(O_LO, NT * 1024).rearrange("p (c f) -> p c f", c=NT)

    ident = cb[:, CB_ID, :]

    B.dma("sp", "c0", lambda e: e.dma_start(out=cb, in_=dcb), (), (R_const,))
    B.dma("sp", "c0", lambda e: e.dma_start(out=cf, in_=dcf), (), (R_const,))
    B.dma("sp", "c0", lambda e: e.dma_start(out=pp, in_=dpp), (), (R_const,))
    B.act(esink, pp[:, PP_SINK:PP_SINK + 4], AF.Exp, (R_const,), (R_const,))
    B.act(aneg, pp[:, PP_ALOG:PP_ALOG + 32], AF.Exp, (R_const,), (R_const,))
    B.ts("dve", aneg, aneg, -1.0, None, ALU.mult, None, (R_const,), (R_const,))

    def wload(dst, src, reads_w, chan):
        B.dma("pool", chan, lambda e: e.dma_start(out=dst, in_=src), (), reads_w)

    def hbuild(src_fn, ntiles, dstT, dst_res, nwcol, wk_off, src_res=None, dst_fn=None, run=True):
        xt = [vf(wk_off + i * 4096, 1024) for i in range(2)]
        xn = [vb(wk_off + 8192 + i * 2048, 1024) for i in range(2)]
        junk = vb(wk_off + 12288, 1024)
        st = vf(wk_off + 14336, 3 * ntiles)
        Rxt, Rxn, Rj, Rst = RL(2), RL(2), Res(), Res()
        B.ms("dve", st, 0.0, (), (Rst,))
        def hb_iter(tt):
            b = tt % 2
            if src_res is None:
                src = src_fn(tt)
                B.dma("sp", f"xt{b}", lambda e, o_=xt[b], i_=src: e.dma_start(out=o_, in_=i_), (), (Rxt[b],))
                xin, rin = xt[b], Rxt[b]
            else:
                xin, rin = src_fn(tt), src_res[tt]
            B.act(junk, xin, AF.Square, (rin, Rst), (Rj, Rst), accum=st[:, 3 * tt:3 * tt + 1])
            B.act(st[:, 3 * tt + 1:3 * tt + 2], st[:, 3 * tt:3 * tt + 1], AF.Ln, (Rst,), (Rst,), scale=1.0 / D, bias=EPS)
            B.act(st[:, 3 * tt + 2:3 * tt + 3], st[:, 3 * tt + 1:3 * tt + 2], AF.Exp, (Rst,), (Rst,), scale=-0.5)
            B.ts("dve", xn[b], xin, st[:, 3 * tt + 2:3 * tt + 3], None, ALU.mult, None, (rin, Rst), (Rxn[b],))
            yield
            pb, pbr = psb()
            for k in range(8):
                B.tr(pb[:, k * 128:(k + 1) * 128], xn[b][:, k * 128:(k + 1) * 128], ident, (Rxn[b], R_const), (pbr,),
                     inc=(k == 7))
            dst_ap = dstT[:, :, tt * 128:(tt + 1) * 128] if dst_fn is None else dst_fn(tt)
            B.tt("dve", dst_ap, pb[:, :].rearrange("p (k t) -> p k t", k=8),
                 pp[:, nwcol:nwcol + 8].unsqueeze(2).to_broadcast([128, 8, 128]), ALU.mult,
                 (pbr, R_const), (dst_res[tt],))

        gens_ = [hb_iter(tt) for tt in range(ntiles)]
        if not run:
            return gens_
        run_pipelined(gens_)

    def rstd_from_ps(psB, psBr, n_feat, lnv, rstd, Rln, Rrs):
        B.act(lnv, psB, AF.Ln, (psBr,), (Rln,), scale=1.0 / n_feat, bias=EPS)
        B.act(rstd, lnv, AF.Exp, (Rln,), (Rrs,), scale=-0.5)

    for s in range(NSEQ_RUN):
      try:
          B.barrier()
          RhT = RL(NT)
          hbuild(lambda tt: dx[s, tt * 128:(tt + 1) * 128, :], NT, hT, RhT, PP_NWMIX, O_WK)
          RhT4 = [[RhT[4 * T + i] for i in range(4)] for T in range(4)]
          B.barrier()
          if STOP == 1:
              raise _Stop()

          Rlo = RL(NT)
          prevb = vb(O_HI, NT * 1024).rearrange("p (c f) -> p c f", c=NT)
          BT = vb(O_HI + 32768, 2 * L).rearrange("p (g t) -> p g t", g=2)
          CT = vb(O_HI + 40960, 2 * L).rearrange("p (g t) -> p g t", g=2)
          Btok = vb(O_HI + 49152, NT * 256).rearrange("p (c f) -> p c f", c=NT)
          Rprevb, RBT, RCT, RBtok = RL(NT), RL(NT), RL(NT), RL(NT)
          Wz = vb(O_MX, 8 * 1024).rearrange("p (k n) -> p k n", k=8)
          RWz = Res()
          dtall = vf(O_MX + 16384, NT * 32).rearrange("p (c f) -> p c f", c=NT)
          aall = vf(O_MX + 18432, NT * 32).rearrange("p (c f) -> p c f", c=NT)
          exall = vf(O_MX + 20480, NT * 64).rearrange("p (c f) -> p c f", c=NT)
          cdec = vf(O_MX + 24576, NT * 32).rearrange("p (c f) -> p c f", c=NT)
          Rst = RL(NT)
          DI = vb(O_MX + 26624, 16 * 128).rearrange("p (h n) -> p h n", h=16)
          RDI = Res()
          wb = [vb(O_WK + i * 8192, 8 * 512).rearrange("p (k n) -> p k n", k=8) for i in range(2)]
          Rwb = RL(2)
          pre = [vb(O_WK + 16384 + i * 4112, 2052) for i in range(2)]
          Rpre = RL(2)
          actT = [vb(O_WK + 24640 + i * 4096, 2048) for i in range(2)]
          RactT = RL(2)
          diag = vb(O_HI, 5 * 128 * 2).rearrange("p (b k n) -> p b k n", b=2, k=5)
          Rdiag = RL(2)
          tmp32 = vf(O_HI + 4096, 64)
          Rtmp = Res()

          wload(Wz, win_v[:, :, C_Z:C_Z + 1024], (RWz,), "wz")
          for h in range(16):
              B.ts("dve", DI[:, h, :], cf[:, CF_ID, :], pp[:, PP_SSDD + h:PP_SSDD + h + 1], None, ALU.mult, None,
                   (R_const,), (RDI,))

          wdt = vb(O_HI + 8192, 8 * 32).rearrange("p (k n) -> p k n", k=8)
          Rwdt = Res()
          wload(wdt, win_v[:, :, C_DT:C_DT + 32], (Rwdt,), "wdt")
          for c in range(NT):
              ps, psr = psf()
              for k in range(8):
                  B.mm(ps[:, 0:32], hT[:, k, c * 128:(c + 1) * 128], wdt[:, k, :], k == 0, k == 7, (RhT[c], Rwdt), (psr,),
                       inc=(k == 7))
              B.tt("dve", tmp32[:, 0:32], ps[:, 0:32], pp[:, PP_DTB:PP_DTB + 32], ALU.add, (psr, R_const), (Rtmp,))
              B.act(tmp32[:, 32:64], tmp32[:, 0:32], AF.Exp, (Rtmp,), (Rtmp,))
              B.act(dtall[:, c, :], tmp32[:, 32:64], AF.Ln, (Rtmp,), (Rst[c],), bias=1.0)
              B.tt("dve", aall[:, c, :], dtall[:, c, :], aneg, ALU.mult, (Rst[c], R_const), (Rst[c],))
              ps, psr = psf()
              B.mm(ps[:, 0:16], cf[:, CF_TUI, :], aall[:, c, 0:16], True, True, (R_const, Rst[c]), (psr,))
              B.mm(ps[:, 16:32], cf[:, CF_TLI, :], aall[:, c, 16:32], True, True, (R_const, Rst[c]), (psr,))
              B.mm(ps[:, 32:48], cf[:, CF_TUS, :], aall[:, c, 0:16], True, True, (R_const, Rst[c]), (psr,))
              B.mm(ps[:, 48:64], cf[:, CF_TLS, :], aall[:, c, 16:32], True, True, (R_const, Rst[c]), (psr,))
              B.mm(ps[:, 64:96], cf[:, CF_ONES, :], aall[:, c, :], True, True, (R_const, Rst[c]), (psr,), inc=True)
              B.act(exall[:, c, :], ps[:, 0:64], AF.Exp, (psr,), (Rst[c],))
              B.act(cdec[:, c, :], ps[:, 64:96], AF.Exp, (psr,), (Rst[c],))

          def xbc_iter(j):
              blk, jj = j // 4, j % 4
              wbi = blk % 2
              if jj == 0:
                  wload(wb[wbi], win_v[:, :, C_XBC + blk * 512:C_XBC + (blk + 1) * 512], (Rwb[wbi],), f"wb{wbi}")
              pb_ = j % 2
              if j < 2:
                  B.ms("dve", pre[pb_][:, 0:2], 0.0, (), (Rpre[pb_],))
                  B.ms("dve", pre[pb_][:, 2050:2052], 0.0, (), (Rpre[pb_],))
              for tap in range(5):
                  B.ts("dve", diag[:, pb_, tap, :], cf[:, CF_ID, :],
                       pp[:, PP_CONVW + j * 5 + tap:PP_CONVW + j * 5 + tap + 1], None, ALU.mult, None,
                       (R_const,), (Rdiag[pb_],))
              for T in range(4):
                  ps, psr = psf()
                  for k in range(8):
                      B.mm(ps[:, :], wb[wbi][:, k, jj * 128:(jj + 1) * 128], hT[:, k, T * 512:(T + 1) * 512], k == 0, k == 7,
                           (Rwb[wbi],) + tuple(RhT4[T]), (psr,), inc=(k == 7))
                  B.cp("act", pre[pb_][:, 2 + T * 512:2 + (T + 1) * 512], ps[:, :], (psr,), (Rpre[pb_],))
              yield
              if j < 8:
                  dstF, dres = actT[pb_], None
              elif j < 10:
                  dstF = BT[:, j - 8, :]
              else:
                  dstF = CT[:, j - 10, :]
              for T in range(4):
                  ps, psr = psf()
                  for tap in range(5):
                      B.mm(ps[:, :], diag[:, pb_, tap, :], pre[pb_][:, T * 512 + tap:T * 512 + tap + 512], tap == 0, tap == 4,
                           (Rdiag[pb_], Rpre[pb_]), (psr,), inc=(tap == 4))
                  if j < 8:
                      wr = (RactT[pb_],)
                  elif j < 10:
                      wr = tuple(RBT[4 * T:4 * T + 4])
                  else:
                      wr = tuple(RCT[4 * T:4 * T + 4])
                  B.act(dstF[:, T * 512:(T + 1) * 512], ps[:, :], AF.Silu, (psr, R_const), wr,
                        bias=pp[:, PP_CONVB + j:PP_CONVB + j + 1])
              if j < 10:
                  for q4 in range(4):
                      pb, pbr = psb()
                      for i in range(4):
                          c = q4 * 4 + i
                          rd = (RactT[pb_], R_const) if j < 8 else (RBT[c], R_const)
                          B.tr(pb[:, i * 128:(i + 1) * 128], dstF[:, c * 128:(c + 1) * 128], ident, rd, (pbr,), inc=(i == 3))
                      if j < 8:
                          B.cp("act", lo[:, q4 * 4:(q4 + 1) * 4, j * 128:(j + 1) * 128],
                               pb[:, 0:512].rearrange("p (c f) -> p c f", c=4), (pbr,), tuple(Rlo[q4 * 4:q4 * 4 + 4]))
                      else:
                          B.cp("act", Btok[:, q4 * 4:(q4 + 1) * 4, (j - 8) * 128:(j - 7) * 128],
                               pb[:, 0:512].rearrange("p (c f) -> p c f", c=4), (pbr,), tuple(RBtok[q4 * 4:q4 * 4 + 4]))
          run_pipelined([xbc_iter(j) for j in range(12)])
          B.barrier()
          if STOP == 3:
              raise _Stop()

          w0 = O_WK
          Hs = vf(w0, 1024); w0 += 4096
          xd = [vb(w0 + i * 2048, 1024) for i in range(2)]; w0 += 4096
          wsm = vf(w0, 64); w0 += 256
          Ebuf = vb(w0, 4096).rearrange("p (q n) -> p q n", q=8); w0 += 8192
          rhsb2 = vb(w0, 2048); rhsb = rhsb2.rearrange("p (h n) -> p h n", h=16); w0 += 4096
          xdt = [vb(w0 + i * 2048, 1024) for i in range(2)]; w0 += 4096
          cbm = vb(w0, 512).rearrange("p (g d n) -> p g d n", g=2, d=2); w0 += 1024
          prevf = vb(w0, 1024); w0 += 2048
          szb = vb(w0, 1024); w0 += 2048
          t1 = vf(w0, 1024); w0 += 4096
          t2 = vf(w0, 1024); w0 += 4096
          gst = vf(w0, 8); w0 += 32
          ynb = vb(w0, 1024); w0 += 2048
          assert w0 - O_WK <= WK_SIZE, (w0 - O_WK, WK_SIZE)
          RH, Rxd, Rws, RE, Rrhs, Rxdt, Rcbm, Rpf, Rsz, Rt1, Rt2, Rg, Ryn = (Res(), RL(2), Res(), RL(8), Res(), RL(2),
                                                                              Res(), Res(), Res(), Res(), Res(), Res(), Res())

          def bc16(ap16):
              return ap16.unsqueeze(2).to_broadcast([128, 16, 64])

          def v3(ap1024):
              return ap1024.rearrange("p (h d) -> p h d", h=16)

          def state_prep(c, d, wcol, ecol, banks=None):
              B.tt("dve", wsm[:, d * 16:(d + 1) * 16], dtall[:, c, wcol:wcol + 16], exall[:, c, ecol:ecol + 16], ALU.mult,
                   (Rst[c],), (Rws,))
              B.tt("dve", v3(xd[d]), v3(lo[:, c, :]), bc16(wsm[:, d * 16:(d + 1) * 16]), ALU.mult, (Rlo[c], Rws), (Rxd[d],))
              pss = []
              for g in range(2):
                  ps, psr = psf() if banks is None else (PF[banks[g]], PFR[banks[g]])
                  B.mm(ps[:, :], Btok[:, c, g * 128:(g + 1) * 128], xd[d][:, g * 512:(g + 1) * 512], True, True,
                       (RBtok[c], Rxd[d]), (psr,), inc=True)
                  pss.append((ps, psr))
              return pss

          def state_apply(c, dcol, pss):
              B.tt("dve", v3(Hs), v3(Hs), bc16(cdec[:, c, dcol:dcol + 16]), ALU.mult, (RH, Rst[c]), (RH,))
              for g in range(2):
                  B.tt("dve", Hs[:, g * 512:(g + 1) * 512], Hs[:, g * 512:(g + 1) * 512], pss[g][0][:, :], ALU.add,
                       (RH, pss[g][1]), (RH,))

          B.ms("dve", Hs, 0.0, (), (RH,))

          def p1_iter(c):
              pss = state_prep(c, 1, 16, 48) if c > 0 else None
              yield
              B.cp("act", prevb[:, c, :], Hs, (RH,), (Rprevb[c],))
              if c > 0:
                  state_apply(c, 16, pss)

          run_pipelined([p1_iter(c) for c in range(NT - 1, -1, -1)])

          E2 = [Ebuf, vb(O_HI + 57344, 4096).rearrange("p (q n) -> p q n", q=8)]
          RE2 = [RE, RL(8)]
          xdt2 = [xdt, [vb(O_WK + 4096 + 2048, 1024), vb(O_MX + 30720, 1024)]]
          Rxdt2 = [Rxdt, [Rxd[1], Res()]]
          B.ms("dve", Hs, 0.0, (), (RH,))

          s1rot = [0]
          Rdm = Res()

          def s1bank():
              i = 4 + s1rot[0] % 2
              s1rot[0] += 1
              return PF[i], PFR[i]

          def p2_s1(c):
              tsl = slice(c * 128, (c + 1) * 128)
              pi = c % 2
              Eb, REb, xdtb, Rxdtb = E2[pi], RE2[pi], xdt2[pi], Rxdt2[pi]
              for g in range(2):
                  ps, psr = s1bank()
                  B.mm(ps[:, 0:128], BT[:, g, tsl], CT[:, g, tsl], True, True, (RBT[c], RCT[c]), (psr,), inc=True)
                  B.tt("dve", cbm[:, g, :, :], ps[:, 0:128].unsqueeze(1).to_broadcast([128, 2, 128]),
                       cb[:, CB_MF:CB_MF + 2, :], ALU.mult, (psr, R_const), (Rcbm,))
              for d in range(2):
                  B.tt("pool", v3(xdtb[d]), v3(lo[:, c, :]), bc16(dtall[:, c, d * 16:(d + 1) * 16]), ALU.mult,
                       (Rlo[c], Rst[c]), (Rxdtb[d],))
              for d in range(2):
                  tri = cf[:, CF_TUI, :] if d == 0 else cf[:, CF_TLI, :]
                  lsm = cb[:, CB_LSF, :] if d == 0 else cb[:, CB_LSB, :]
                  B.tt("pool", rhsb, aall[:, c, d * 16:(d + 1) * 16].unsqueeze(2).to_broadcast([128, 16, 128]),
                       tri.unsqueeze(1).to_broadcast([128, 16, 128]), ALU.mult, (Rst[c], R_const), (Rrhs,))
                  for q in range(4):
                      ps, psr = s1bank()
                      B.mm(ps[:, :], lsm, rhsb2[:, q * 512:(q + 1) * 512], True, True, (R_const, Rrhs), (psr,), inc=True)
                      qi = d * 4 + q
                      B.act(Eb[:, qi, :], ps[:, :], AF.Exp, (psr,), (REb[qi],))
                  yield
                  for q in range(4):
                      qi = d * 4 + q
                      g = q // 2
                      B.tt("dve", Eb[:, qi, :].rearrange("p (h n) -> p h n", h=4),
                           Eb[:, qi, :].rearrange("p (h n) -> p h n", h=4),
                           cbm[:, g, d, :].unsqueeze(1).to_broadcast([128, 4, 128]), ALU.mult, (REb[qi], Rcbm), (REb[qi],))

          def p2_s2(c):
              tsl = slice(c * 128, (c + 1) * 128)
              pi = c % 2
              Eb, REb, xdtb, Rxdtb = E2[pi], RE2[pi], xdt2[pi], Rxdt2[pi]
              B.cp("act", prevf, Hs, (RH,), (Rpf,))
              if c < NT - 1:
                  state_apply(c, 0, [(PF[2], PFR[2]), (PF[3], PFR[3])])
              for g in range(2):
                  ps, psr = PF[2 + g], PFR[2 + g]
                  B.mm(ps[:, :], CT[:, g, tsl], prevf[:, g * 512:(g + 1) * 512], True, True, (RCT[c], Rpf), (psr,), inc=True)
                  B.tt("dve", v3(t1)[:, g * 8:(g + 1) * 8, :], ps[:, :].rearrange("p (h d) -> p h d", h=8),
                       exall[:, c, g * 8:(g + 1) * 8].unsqueeze(2).to_broadcast([128, 8, 64]), ALU.mult,
                       (psr, Rst[c]), (Rt1,))
              yps = []
              for hf in range(2):
                  ps, psr = PF[hf], PFR[hf]
                  for h8 in range(8):
                      h = hf * 8 + h8
                      osl = ps[:, h8 * 64:(h8 + 1) * 64]
                      B.mm(osl, Eb[:, h // 4, (h % 4) * 128:(h % 4 + 1) * 128], xdtb[0][:, h * 64:(h + 1) * 64], True, False,
                           (REb[h // 4], Rxdtb[0]), (psr,))
                      B.mm(osl, Eb[:, 4 + h // 4, (h % 4) * 128:(h % 4 + 1) * 128], xdtb[1][:, h * 64:(h + 1) * 64], False,
                           False, (REb[4 + h // 4], Rxdtb[1]), (psr,))
                      B.mm(osl, DI[:, h, :], lo[:, c, h * 64:(h + 1) * 64], False, True, (RDI, Rlo[c]), (psr,), inc=(h8 == 7))
                  yps.append((ps, psr))
              for g in range(2):
                  ps, psr = PF[2 + g], PFR[2 + g]
                  B.mm(ps[:, :], CT[:, g, tsl], prevb[:, c, g * 512:(g + 1) * 512], True, True, (RCT[c], Rprevb[c]), (psr,),
                       inc=True)
                  B.tt("dve", v3(t2)[:, g * 8:(g + 1) * 8, :], ps[:, :].rearrange("p (h d) -> p h d", h=8),
                       exall[:, c, 16 + g * 8:16 + (g + 1) * 8].unsqueeze(2).to_broadcast([128, 8, 64]), ALU.mult,
                       (psr, Rst[c]), (Rt2,))
              yield
              for hf in range(2):
                  ps, psr = PF[2 + hf], PFR[2 + hf]
                  for k in range(8):
                      B.mm(ps[:, :], hT[:, k, tsl], Wz[:, k, hf * 512:(hf + 1) * 512], k == 0, k == 7, (RhT[c], RWz), (psr,),
                           inc=(k == 7))
                  B.act(szb[:, hf * 512:(hf + 1) * 512], ps[:, :], AF.Silu, (psr,), (Rsz,))
              B.act(gst[:, 7:8], cf[:, CF_ONES, 0:1], AF.Ln, (R_const,), (Rdm,))
              B.tt("dve", t1, t1, t2, ALU.add, (Rt1, Rt2), (Rt1,))
              for hf in range(2):
                  B.tt("dve", t1[:, hf * 512:(hf + 1) * 512], t1[:, hf * 512:(hf + 1) * 512], yps[hf][0][:, :], ALU.add,
                       (Rt1, yps[hf][1]), (Rt1,))
              yield
              B.tt("dve", t1, t1, szb, ALU.mult, (Rt1, Rsz), (Rt1,))
              B.ms("dve", gst[:, 0:6], 0.0, (), (Rg,))
              for g in range(2):
                  B.act(t2[:, g * 512:(g + 1) * 512], t1[:, g * 512:(g + 1) * 512], AF.Square, (Rt1, Rg), (Rt2, Rg),
                        accum=gst[:, g:g + 1])
              B.act(gst[:, 2:4], gst[:, 0:2], AF.Ln, (Rg,), (Rg,), scale=1.0 / 512, bias=EPS)
              B.act(gst[:, 4:6], gst[:, 2:4], AF.Exp, (Rg,), (Rg,), scale=-0.5)
              for g in range(2):
                  B.ts("dve", ynb[:, g * 512:(g + 1) * 512], t1[:, g * 512:(g + 1) * 512], gst[:, 4 + g:5 + g], None, ALU.mult,
                       None, (Rt1, Rg), (Ryn,))
              yield
              pb, pbr = psb()
              for k in range(8):
                  B.tr(pb[:, k * 128:(k + 1) * 128], ynb[:, k * 128:(k + 1) * 128], ident, (Ryn, R_const), (pbr,), inc=(k == 7))
              B.tt("dve", lo[:, c, :].rearrange("p (k t) -> p k t", k=8), pb[:, :].rearrange("p (k t) -> p k t", k=8),
                   pp[:, PP_SSDNW:PP_SSDNW + 8].unsqueeze(2).to_broadcast([128, 8, 128]), ALU.mult,
                   (pbr, R_const), (Rlo[c],))
              if c + 1 < NT - 1:
                  state_prep(c + 1, 0, 0, 32, banks=(2, 3))

          for _ in p2_s1(0):
              pass
          state_prep(0, 0, 0, 32, banks=(2, 3))
          for c in range(NT):
              g2_ = p2_s2(c)
              g1_ = p2_s1(c + 1) if c + 1 < NT else None
              for seg in range(4):
                  next(g2_, None)
                  if g1_ is not None and seg < 3:
                      next(g1_, None)
          pst["f"] = 0
          B.barrier()
          if STOP == 5:
              raise _Stop()

          qT = vb(O_HI, 4 * L).rearrange("p (k t) -> p k t", k=4)
          kz = vb(O_HI + 16384, 4 * L).rearrange("p (g h t) -> p g h t", g=2, h=2)
          vz = vb(O_HI + 32768, NT * 512).rearrange("p (c f) -> p c f", c=NT)
          cs = vf(O_HI + 49152, 2 * L).rearrange("p (a t) -> p a t", a=2)
          RqT, Rkd, Rvz, Rcs = RL(NT), RL(NT), RL(NT), Res()
          B.dma("sp", "c0", lambda e: e.dma_start(out=cs, in_=dcs), (), (Rcs,))
          wv = vb(O_WK, 8 * 512).rearrange("p (k n) -> p k n", k=8)
          wk = vb(O_WK + 8192, 8 * 256).rearrange("p (k n) -> p k n", k=8)
          wq = vb(O_WK + 12288, 8 * 512).rearrange("p (k n) -> p k n", k=8)
          Rwv, Rwk, Rwq = Res(), Res(), Res()
          def qkset(w0):
              d_ = {}
              d_["sq"] = vb(w0, 512); w0 += 1024
              d_["lnv"] = vf(w0, 512); w0 += 2048
              d_["qn"] = vb(w0, 512); w0 += 1024
              d_["ta"] = vf(w0, 512); w0 += 2048
              d_["tb"] = vf(w0, 512); w0 += 2048
              for nm in ("Rsq", "Rln", "Rqn", "Rta", "Rtb"):
                  d_[nm] = Res()
              return d_
          qks = [qkset(O_WK + 20480), qkset(O_WK)]
          w0 = O_WK + 28672
          Pt = [vb(w0 + i * 1536, 768) for i in range(2)]; w0 += 3072
          dpl = vf(w0, 512); w0 += 2048
          ktmp2 = [vb(w0 + i * 1024, 512) for i in range(2)]; w0 += 2048
          Rkt2 = RL(2)
          assert w0 - O_WK <= WK_SIZE
          RPt, Rdp = RL(2), Res()
          qkc = [0]

          B.ms("dve", vb(O_WK, 4096), 0.0, (), (Rwv,))
          for g in range(2):
              for hf in range(2):
                  c0 = (g * 2 + hf) * 128 + hf * 64
                  wload(wv[:, :, c0:c0 + 64], win_v[:, :, C_V + g * 64:C_V + (g + 1) * 64], (Rwv,), "wv")
              for hf in range(2):
                  wload(wk[:, :, g * 128 + hf * 64:g * 128 + (hf + 1) * 64], win_v[:, :, C_K + g * 64:C_K + (g + 1) * 64],
                        (Rwk,), "wk")
          wload(wq, win_v[:, :, C_Q:C_Q + 512], (Rwq,), "wq")
          for tt in range(NT):
              ps, psr = psf()
              for k in range(8):
                  B.mm(ps[:, :], hT[:, k, tt * 128:(tt + 1) * 128], wv[:, k, :], k == 0, k == 7, (RhT[tt], Rwv), (psr,),
                       inc=(k == 7))
              B.cp("act", vz[:, tt, :], ps[:, :], (psr,), (Rvz[tt],))

          def qk_chunk(wtile, wres, col0, dst_ap, dres4, pcol, T, post=None):
              S_ = qks[qkc[0] % 2]
              qkc[0] += 1
              sq, lnv, qn, ta, tb = S_["sq"], S_["lnv"], S_["qn"], S_["ta"], S_["tb"]
              Rsq, Rln, Rqn, Rta, Rtb = S_["Rsq"], S_["Rln"], S_["Rqn"], S_["Rta"], S_["Rtb"]
              tsl = slice(T * 512, (T + 1) * 512)
              psA, psAr = psf()
              for k in range(8):
                  B.mm(psA[:, :], wtile[:, k, col0:col0 + 128], hT[:, k, tsl], k == 0, k == 7, (wres,) + tuple(RhT4[T]),
                       (psAr,), inc=(k == 7))
              B.act(sq, psA[:, :], AF.Square, (psAr,), (Rsq,))
              psB, psBr = psf()
              B.mm(psB[:, :], cb[:, CB_BLK, :], sq, True, True, (R_const, Rsq), (psBr,), inc=True)
              rstd_from_ps(psB[:, :], psBr, 64, lnv, lnv, Rln, Rln)
              B.stt("dve", qn, psA[:, :], pp[:, pcol:pcol + 1], lnv, ALU.mult, ALU.mult, (psAr, R_const, Rln), (Rqn,))
              yield
              psR, psRr = psf()
              B.mm(psR[:, :], cb[:, CB_ROT, :], qn, True, True, (R_const, Rqn), (psRr,), inc=True)
              B.tt("dve", ta, psR[:, :], cs[:, 1, tsl], ALU.mult, (psRr, Rcs), (Rta,))
              B.tt("dve", tb, qn, cs[:, 0, tsl], ALU.mult, (Rqn, Rcs), (Rtb,))
              B.tt("dve", dst_ap, ta, tb, ALU.add, (Rta, Rtb), tuple(dres4))
              if post is not None:
                  post()

          B.barrier()
          def kpost(g, T):
              def f():
                  for hf in range(2):
                      B.ts("dve", kz[:, g, hf, T * 512:(T + 1) * 512], ktmp2[g], pp[:, PP_M0 + hf:PP_M0 + hf + 1], None,
                           ALU.mult, None, (Rkt2[g], R_const), tuple(Rkd[4 * T:4 * T + 4]))
              return f
          gens = []
          for T in range(4):
              for g in range(2):
                  gens.append(qk_chunk(wk, Rwk, g * 128, ktmp2[g], (Rkt2[g],), PP_KW, T, post=kpost(g, T)))
              for c4 in range(4):
                  gens.append(qk_chunk(wq, Rwq, c4 * 128, qT[:, c4, T * 512:(T + 1) * 512], RqT[4 * T:4 * T + 4], PP_QW, T))
          run_pipelined(gens)

          if STOP == 5.5:
              B.barrier()
              raise _Stop()
          pst["n"] = 4
          pst["f"] = 0
          psN, psNr = PF[4], PFR[4]
          psD, psDr = PF[5], PFR[5]

          def att_iter(n, c4):
              qsl = slice(n * 128, (n + 1) * 128)
              js = [j for j in (n - 1, n, n + 1) if 0 <= j < NT]
              g = c4 // 2
              pi = c4 % 2
              psS0, psS0r = psf()
              psS1, psS1r = psf()
              for ji, j in enumerate(js):
                  for hf in range(2):
                      slot = ji * 2 + hf
                      pS, pSr = (psS0, psS0r) if slot < 4 else (psS1, psS1r)
                      so = (slot % 4) * 128
                      B.mm(pS[:, so:so + 128], kz[:, g, hf, j * 128:(j + 1) * 128],
                           qT[:, c4, qsl], True, True, (Rkd[j], RqT[n]), (pSr,),
                           inc=(slot == 3 or slot == len(js) * 2 - 1))
              n0 = min(4, len(js) * 2)
              B.act(Pt[pi][:, 0:n0 * 128], psS0[:, 0:n0 * 128], AF.Exp, (psS0r,), (RPt[pi],), scale=0.125)
              if len(js) * 2 > 4:
                  B.act(Pt[pi][:, 512:768], psS1[:, 0:256], AF.Exp, (psS1r,), (RPt[pi],), scale=0.125)
              for ji, j in enumerate(js):
                  if j != n:
                      mk = cb[:, CB_MPREV, :] if j < n else cb[:, CB_MNEXT, :]
                      pv = Pt[pi][:, ji * 256:(ji + 1) * 256].rearrange("p (h q) -> p h q", h=2)
                      B.tt("dve", pv, pv, mk.unsqueeze(1).to_broadcast([128, 2, 128]), ALU.mult, (RPt[pi], R_const),
                           (RPt[pi],))
              yield
              nmm = len(js) * 2
              i = 0
              for ji, j in enumerate(js):
                  for hf in range(2):
                      B.mm(psN[:, c4 * 128:(c4 + 1) * 128], vz[:, j, (g * 2 + hf) * 128:(g * 2 + hf + 1) * 128],
                           Pt[pi][:, (ji * 2 + hf) * 128:(ji * 2 + hf + 1) * 128], i == 0, i == nmm - 1,
                           (Rvz[j], RPt[pi]), (psNr,))
                      i += 1
              i = 0
              for ji, j in enumerate(js):
                  for hf in range(2):
                      B.mm(psD[:, c4 * 128:(c4 + 1) * 128], cb[:, CB_OZ0 + hf, :],
                           Pt[pi][:, (ji * 2 + hf) * 128:(ji * 2 + hf + 1) * 128], i == 0, i == nmm - 1,
                           (R_const, RPt[pi]), (psDr,), inc=(i == nmm - 1))
                      i += 1
              if c4 == 3:
                  for cc in range(4):
                      B.act(dpl[:, cc * 128:(cc + 1) * 128], psD[:, cc * 128:(cc + 1) * 128], AF.Ln, (psDr, R_const), (Rdp,),
                            bias=esink[:, cc:cc + 1])
                  B.act(dpl, dpl, AF.Exp, (Rdp,), (Rdp,), scale=-1.0)
                  B.tt("dve", mxA[:, :, qsl], psN[:, :].rearrange("p (c q) -> p c q", c=4),
                       dpl.rearrange("p (c q) -> p c q", c=4), ALU.mult, (psNr, Rdp), ())

          run_pipelined([att_iter(n, c4) for n in range(NT) for c4 in range(4)])
          pst["n"] = 6
          pst["f"] = 0
          B.barrier()
          if STOP == 6:
              raise _Stop()

          memT = vb(O_HI, 8 * 256).rearrange("p (k t) -> p k t", k=8)
          kmT = vb(O_HI + 4096, 4 * 256).rearrange("p (h t) -> p h t", h=4)
          vm = vb(O_HI + 6144, 2 * 512).rearrange("p (m f) -> p m f", m=2)
          kmn = vb(O_HI + 8192, 512)
          kst = vf(O_HI + 9216, 16)
          wkv = vb(O_HI + 16384, 8 * 1024).rearrange("p (k n) -> p k n", k=8)
          wqx = vb(O_HI + 32768, 8 * 512).rearrange("p (k n) -> p k n", k=8)
          RmemT, RkmT, Rvm, Rkmn, Rkst, Rwkv, Rwqx = RL(2), Res(), RL(2), Res(), Res(), Res(), Res()
          wload(wkv, wkv_v, (Rwkv,), "wkv")
          wload(wqx, win_v[:, :, C_QX:C_QX + 512], (Rwqx,), "wqx")
          hbuild(lambda tt: dmem[s, tt * 128:(tt + 1) * 128, :], 2, memT, RmemT, PP_NWMEM, O_WK)
          def xset(w0):
              d_ = {}
              d_["sq"] = vb(w0, 512); w0 += 1024
              d_["lnv"] = vf(w0, 512); w0 += 2048
              d_["qn"] = vb(w0, 512); w0 += 1024
              d_["rD"] = vf(w0, 512); w0 += 2048
              d_["Px"] = [vb(w0 + i * 1024, 512) for i in range(2)]; w0 += 2048
              for nm in ("Rsq", "Rln", "Rqn", "RrD"):
                  d_[nm] = Res()
              d_["RPx"] = RL(2)
              return d_
          xs_ = [xset(O_WK + 16384), xset(O_WK + 16384 + 8192)]
          tk = vf(O_WK + 32768, 512)
          Rtk = Res()
          assert 32768 + 2048 <= WK_SIZE
          for mt in range(2):
              msl = slice(mt * 128, (mt + 1) * 128)
              psK, psKr = psf()
              for k in range(8):
                  B.mm(psK[:, :], memT[:, k, msl], wkv[:, k, 0:512], k == 0, k == 7, (RmemT[mt], Rwkv), (psKr,), inc=(k == 7))
              B.ms("dve", kst, 0.0, (), (Rkst,))
              for h in range(4):
                  B.act(tk[:, h * 128:(h + 1) * 128], psK[:, h * 128:(h + 1) * 128], AF.Square, (psKr, Rkst), (Rtk, Rkst),
                        accum=kst[:, h:h + 1])
              B.act(kst[:, 4:8], kst[:, 0:4], AF.Ln, (Rkst,), (Rkst,), scale=1.0 / 128, bias=EPS)
              B.act(kst[:, 8:12], kst[:, 4:8], AF.Exp, (Rkst,), (Rkst,), scale=-0.5)
              B.tt("dve", tk.rearrange("p (h d) -> p h d", h=4), psK[:, :].rearrange("p (h d) -> p h d", h=4),
                   kst[:, 8:12].unsqueeze(2).to_broadcast([128, 4, 128]), ALU.mult, (psKr, Rkst), (Rtk,))
              B.tt("dve", kmn.rearrange("p (h d) -> p h d", h=4), tk.rearrange("p (h d) -> p h d", h=4),
                   pp[:, PP_XKW:PP_XKW + 128].unsqueeze(1).to_broadcast([128, 4, 128]), ALU.mult, (Rtk, R_const), (Rkmn,))
              pb, pbr = psb()
              for h in range(4):
                  B.tr(pb[:, h * 128:(h + 1) * 128], kmn[:, h * 128:(h + 1) * 128], ident, (Rkmn, R_const), (pbr,), inc=(h == 3))
              B.cp("act", kmT[:, :, msl], pb[:, 0:512].rearrange("p (h t) -> p h t", h=4), (pbr,), (RkmT,))
              psV, psVr = psf()
              for k in range(8):
                  B.mm(psV[:, :], memT[:, k, msl], wkv[:, k, 512:1024], k == 0, k == 7, (RmemT[mt], Rwkv), (psVr,), inc=(k == 7))
              B.cp("act", vm[:, mt, :], psV[:, :], (psVr,), (Rvm[mt],))
          def xat_iter(T, h, xc):
              tsl = slice(T * 512, (T + 1) * 512)
              S_ = xs_[xc % 2]
              sq, lnv, qn, rD, Px = S_["sq"], S_["lnv"], S_["qn"], S_["rD"], S_["Px"]
              Rsq, Rln, Rqn, RrD, RPx = S_["Rsq"], S_["Rln"], S_["Rqn"], S_["RrD"], S_["RPx"]
              psA, psAr = psf()
              for k in range(8):
                  B.mm(psA[:, :], wqx[:, k, h * 128:(h + 1) * 128], hT[:, k, tsl], k == 0, k == 7, (Rwqx,) + tuple(RhT4[T]),
                       (psAr,), inc=(k == 7))
              B.act(sq, psA[:, :], AF.Square, (psAr,), (Rsq,))
              psB, psBr = psf()
              B.mm(psB[:, :], cb[:, CB_ONES, :], sq, True, True, (R_const, Rsq), (psBr,), inc=True)
              rstd_from_ps(psB[:, :], psBr, 128, lnv, lnv, Rln, Rln)
              B.stt("dve", qn, psA[:, :], pp[:, PP_XQW:PP_XQW + 1], lnv, ALU.mult, ALU.mult, (psAr, R_const, Rln), (Rqn,))
              yield
              for mt in range(2):
                  psS, psSr = psf()
                  B.mm(psS[:, :], kmT[:, h, mt * 128:(mt + 1) * 128], qn, True, True, (RkmT, Rqn), (psSr,), inc=True)
                  B.act(Px[mt], psS[:, :], AF.Exp, (psSr,), (RPx[mt],), scale=128 ** -0.5)
              psN, psNr = psf()
              psD, psDr = psf()
              for mt in range(2):
                  B.mm(psN[:, :], vm[:, mt, h * 128:(h + 1) * 128], Px[mt], mt == 0, mt == 1, (Rvm[mt], RPx[mt]), (psNr,))
              for mt in range(2):
                  B.mm(psD[:, :], cb[:, CB_ONES, :], Px[mt], mt == 0, mt == 1, (R_const, RPx[mt]), (psDr,), inc=(mt == 1))
              B.act(rD, psD[:, :], AF.Ln, (psDr,), (RrD,))
              B.act(rD, rD, AF.Exp, (RrD,), (RrD,), scale=-1.0)
              B.tt("dve", mxX[:, h, tsl], psN[:, :], rD, ALU.mult, (psNr, RrD), ())

          run_pipelined([xat_iter(T, h, T * 4 + h) for T in range(4) for h in range(4)])
          B.barrier()
          if STOP == 7:
              raise _Stop()

          if DEBUG:
              B.dma("sp", "dbg", lambda e: e.dma_start(out=ddbg[s, :, 0:4, :], in_=mxA), (), ())
              B.dma("sp", "dbg", lambda e: e.dma_start(out=ddbg[s, :, 12:16, :], in_=mxX), (), ())
              for c in range(NT):
                  B.dma("sp", "dbg", lambda e, c=c: e.dma_start(out=ddbg[s, :, 4:12, c * 128:(c + 1) * 128],
                                                               in_=lo[:, c, :].rearrange("p (k t) -> p k t", k=8)), (), ())
              B.barrier()

          x1 = vf(O_HI, NT * 1024).rearrange("p (c f) -> p c f", c=NT)
          Rx1 = RL(NT)
          wo = vb(O_HT, 16 * 1024).rearrange("p (k n) -> p k n", k=16)
          Rwo = Res()
          for kh in range(2):
              wload(wo[:, kh * 8:(kh + 1) * 8, :], wout_v[:, kh * 8:(kh + 1) * 8, :], (Rwo,), "wo")
          xt = [vf(O_WK + 24576 + i * 4096, 1024) for i in range(2)]
          Rxt = RL(2)
          Rmx = Res()
          Rh2 = RL(NT)
          hgens = hbuild(lambda tt: x1[:, tt, :], NT, None, Rh2, PP_NWMLP, O_WK, src_res=Rx1,
                         dst_fn=lambda tt: lo[:, tt, :].rearrange("p (k t) -> p k t", k=8), run=False)
          prevg = None
          for tt in range(NT):
              tsl = slice(tt * 128, (tt + 1) * 128)
              b = tt % 2
              B.dma("sp", f"xt{b}", lambda e, o_=xt[b], i_=dx[s, tsl, :]: e.dma_start(out=o_, in_=i_), (), (Rxt[b],))
              for hf in range(2):
                  ps, psr = psf()
                  for kc in range(16):
                      if kc < 4:
                          lhs = mxA[:, kc, tsl]
                      elif kc < 12:
                          lhs = lo[:, tt, (kc - 4) * 128:(kc - 3) * 128]
                      else:
                          lhs = mxX[:, kc - 12, tsl]
                      B.mm(ps[:, :], lhs, wo[:, kc, hf * 512:(hf + 1) * 512], kc == 0, kc == 15, (Rwo, Rmx, Rh2[tt]), (psr,),
                           inc=(kc == 15))
                  B.tt("dve", x1[:, tt, hf * 512:(hf + 1) * 512], ps[:, :], xt[b][:, hf * 512:(hf + 1) * 512], ALU.add,
                       (psr, Rxt[b]), (Rx1[tt],))
              next(hgens[tt])
              if prevg is not None:
                  for _ in prevg:
                      pass
              prevg = hgens[tt]
          for _ in prevg:
              pass
          if STOP == 8:
              B.barrier()
              raise _Stop()
          Rh24 = [[Rh2[4 * T + i] for i in range(4)] for T in range(4)]
          if STOP == 8.5:
              B.barrier()
              raise _Stop()
          wbase = [O_MX, O_HT]
          wu = [vb(wbase[i], 8 * 1024).rearrange("p (k n) -> p k n", k=8) for i in range(2)]
          wd = [vb(wbase[i] + 16384, 8 * 1024).rearrange("p (k n) -> p k n", k=8) for i in range(2)]
          Rwu, Rwd = [Rmx, Rwo], [Res(), Res()]
          Rwd[0].r, Rwd[1].r = Rmx.r, Rwo.r
          uT = [vb(O_WK + 16384 + i * 8192, 8 * 512).rearrange("p (k t) -> p k t", k=8) for i in range(2)]
          RuT = RL(2)
          rl = [vb(O_WK + 32768 + i * 1024, 512) for i in range(2)]
          Rrl = RL(2)
          assert 32768 + 2048 <= WK_SIZE
          ui = 0

          def mlp_wload(fb_):
              wi_ = fb_ % 2
              wload(wu[wi_], wup_v[:, :, fb_ * 1024:(fb_ + 1) * 1024], (Rwu[wi_],), f"wu{wi_}")
              wload(wd[wi_], wdn_v[:, fb_ * 8:(fb_ + 1) * 8, :], (Rwd[wi_],), f"wd{wi_}")
          mlp_wload(0)
          mlp_wload(1)
          def mlp_iter(fb, T, u):
              wi = fb % 2
              tsl = slice(T * 512, (T + 1) * 512)
              for fc in range(8):
                  ps, psr = psf()
                  for k in range(8):
                      B.mm(ps[:, :], wu[wi][:, k, fc * 128:(fc + 1) * 128], lo[:, 4 * T:4 * T + 4, k * 128:(k + 1) * 128], k == 0,
                           k == 7, (Rwu[wi],) + tuple(Rh24[T]), (psr,), inc=(k == 7))
                  r = fc % 2
                  B.act(rl[r], ps[:, :], AF.Relu, (psr,), (Rrl[r],))
                  B.tt("pool", uT[u][:, fc, :], rl[r], rl[r], ALU.mult, (Rrl[r],), (RuT[u],) if u == 0 else (RuT[u], Rxt[0], Rxt[1]))
              yield
              for ti in range(4):
                  tt = T * 4 + ti
                  for hf in range(2):
                      ps, psr = psf()
                      for fc in range(8):
                          B.mm(ps[:, :], uT[u][:, fc, ti * 128:(ti + 1) * 128], wd[wi][:, fc, hf * 512:(hf + 1) * 512],
                               fc == 0, fc == 7, (RuT[u], Rwd[wi]), (psr,), inc=(fc == 7))
                      B.tt("dve", x1[:, tt, hf * 512:(hf + 1) * 512], x1[:, tt, hf * 512:(hf + 1) * 512], ps[:, :], ALU.add,
                           (Rx1[tt], psr), (Rx1[tt],))
                  if fb == 3:
                      B.dma("sp", "out", lambda e, o_=dout[s, tt * 128:(tt + 1) * 128, :], i_=x1[:, tt, :]:
                            e.dma_start(out=o_, in_=i_), (Rx1[tt],), ())
              if T == 3 and fb + 2 < 4:
                  mlp_wload(fb + 2)

          run_pipelined([mlp_iter(fb, T, (fb * 4 + T) % 2) for fb in range(4) for T in range(4)])
      except _Stop:
        pass
    B.barrier()
    for k, v in list(B.cnt.items()):
        B.need("sp", (k, v))

    keys = list(B.cnt.keys())
    sems = {k: es.enter_context(nc.semaphore(f"s_{k}")) for k in keys}

    def run(e, stream):
        for it in stream:
            if it[0] == 0:
                e.wait_ge(sems[it[1]], it[2])
            else:
                ins = it[1](e)
                if it[2] is not None:
                    ins.then_inc(sems[it[2]], it[3])

    with nc.Block() as block:
        @block.tensor
        def _(e):
            run(e, B.streams["pe"])

        @block.scalar
        def _(e):
            run(e, B.streams["act"])

        @block.vector
        def _(e):
            run(e, B.streams["dve"])

        @block.gpsimd
        def _(e):
            run(e, B.streams["pool"])

        @block.sync
        def _(e):
            run(e, B.streams["sp"])
    es.close()
    return nc


def make_consts():
    j = np.arange(128)[:, None]
    l = np.arange(128)[None, :]
    cbm = np.zeros((128, NCB, 128), np.float32)
    cbm[:, CB_ID] = (j == l)
    cbm[:, CB_ONES] = 1.0
    cbm[:, CB_BLK] = (j // 64 == l // 64)
    cbm[:, CB_OZ0] = (l < 64)
    cbm[:, CB_OZ1] = (l >= 64)
    cbm[:, CB_LSF] = (j > l)
    cbm[:, CB_LSB] = (j < l)
    rot = np.zeros((128, 128), np.float32)
    for hb in (0, 64):
        for d in range(8):
            rot[hb + d + 8, hb + d] = -1.0
            rot[hb + d, hb + d + 8] = 1.0
    cbm[:, CB_ROT] = rot
    cbm[:, CB_MPREV] = (j >= l)
    cbm[:, CB_MNEXT] = (j <= l)
    cbm[:, CB_MF] = (l >= j)
    cbm[:, CB_MB] = (l <= j)
    cfm = np.zeros((128, NCF, 128), np.float32)
    cfm[:, CF_TUI] = (j <= l)
    cfm[:, CF_TLI] = (j >= l)
    cfm[:, CF_TUS] = (j > l)
    cfm[:, CF_TLS] = (j < l)
    cfm[:, CF_ONES] = 1.0
    cfm[:, CF_ID] = (j == l)
    inv = 500000.0 ** (-np.arange(0, 16, 2, dtype=np.float32) / 16)
    t = np.arange(L, dtype=np.float32)
    ang = t[None, :] * inv[:, None]
    cs = np.zeros((128, 2, L), np.float32)
    cs[:, 0, :] = 1.0
    for p in range(128):
        d = p % 64
        if d < 16:
            cs[p, 0] = np.cos(ang[d % 8])
            cs[p, 1] = np.sin(ang[d % 8])
    return cbm.astype(ml_dtypes.bfloat16), cfm, cs


def pack_params(inp):
    pp = np.zeros((128, NPP), np.float32)
    p = np.arange(128)
    pp[:, PP_NWMIX:PP_NWMIX + 8] = inp["norm_mix_w"][0].reshape(8, 128).T
    pp[:, PP_NWMLP:PP_NWMLP + 8] = inp["norm_mlp_w"][0].reshape(8, 128).T
    pp[:, PP_NWMEM:PP_NWMEM + 8] = inp["mem_norm_w"][0].reshape(8, 128).T
    pp[:, PP_SSDNW:PP_SSDNW + 8] = inp["ssd_norm_w"][0].reshape(8, 128).T
    cw = inp["conv_w"][0]
    pp[:, PP_CONVW:PP_CONVW + 60] = cw.reshape(5, 12, 128).transpose(2, 1, 0).reshape(128, 60)
    pp[:, PP_CONVB:PP_CONVB + 12] = inp["conv_b"][0].reshape(12, 128).T
    pp[:, PP_QW] = inp["q_norm_w"][0][p % 64]
    pp[:, PP_KW] = inp["k_norm_w"][0][p % 64]
    pp[:, PP_XQW] = inp["xq_norm_w"][0]
    for c in range(4):
        pp[:, PP_SINK + c] = inp["attn_sink"][0][2 * c + p // 64]
    pp[:, PP_DTB:PP_DTB + 16] = inp["dt_bias_f"][0][None, :]
    pp[:, PP_DTB + 16:PP_DTB + 32] = inp["dt_bias_b"][0][None, :]
    pp[:, PP_ALOG:PP_ALOG + 16] = inp["a_log_f"][0][None, :]
    pp[:, PP_ALOG + 16:PP_ALOG + 32] = inp["a_log_b"][0][None, :]
    pp[:, PP_SSDD:PP_SSDD + 16] = inp["ssd_d"][0][None, :]
    pp[:, PP_XKW:PP_XKW + 128] = inp["xk_norm_w"][0][None, :]
    pp[:, PP_M0] = (p < 64)
    pp[:, PP_M1] = (p >= 64)
    return pp


_NC_CACHE = {}


def kernel(**inputs):
    inp = {k: np.asarray(v) for k, v in inputs.items()}
    if "nc" not in _NC_CACHE:
        _NC_CACHE["nc"] = build_program()
    nc = _NC_CACHE["nc"]
    cbm, cfm, cs = make_consts()
    pp = pack_params(inp)
    shared = {
        "w_in": np.ascontiguousarray(inp["w_in"][0]),
        "w_mem_kv": np.ascontiguousarray(inp["w_mem_kv"][0]),
        "w_out": np.ascontiguousarray(inp["w_out"][0]),
        "w_up": np.ascontiguousarray(inp["w_mlp_up"][0]),
        "w_down": np.ascontiguousarray(inp["w_mlp_down"][0]),
        "cbf": cbm, "cf32": cfm, "pp": pp, "cossin": cs,
    }
    in_maps = []
    for c in range(NCORES):
        m = dict(shared)
        m["x"] = np.ascontiguousarray(inp["x"][c * SEQ_PER_CORE:(c + 1) * SEQ_PER_CORE])
        m["mem"] = np.ascontiguousarray(inp["mem"][c * SEQ_PER_CORE:(c + 1) * SEQ_PER_CORE])
        in_maps.append(m)
    res = run_bass_kernel_spmd(nc, in_maps, core_ids=list(range(NCORES)))
    out = np.concatenate([np.asarray(r["out"]) for r in res.results], axis=0)
    return out.astype(np.float32)
```

```python
import numpy as np
import ml_dtypes
from contextlib import ExitStack
import concourse.bass as bass
import concourse.mybir as mybir
from concourse.bass_utils import run_bass_kernel_spmd

F32 = mybir.dt.float32
BF16 = mybir.dt.bfloat16
AF = mybir.ActivationFunctionType
ALU = mybir.AluOpType

NCORES = 8
SEQ_PER_CORE = 2
L = 2048
D = 1024
NT = 16
EPS = 1e-6
DEBUG = False
STOP = 99


class _Stop(Exception):
    pass
NSEQ_RUN = SEQ_PER_CORE

C_Q, C_K, C_V, C_Z, C_XBC, C_DT, C_QX = 0, 512, 640, 768, 1792, 3328, 3360

(CB_ID, CB_ONES, CB_BLK, CB_OZ0, CB_OZ1, CB_LSF, CB_LSB, CB_ROT, CB_MPREV, CB_MNEXT, CB_MF, CB_MB) = range(12)
NCB = 12
(CF_TUI, CF_TLI, CF_TUS, CF_TLS, CF_ONES, CF_ID) = range(6)
NCF = 6
PP_NWMIX, PP_NWMLP, PP_NWMEM, PP_SSDNW = 0, 8, 16, 24
PP_CONVW = 32
PP_CONVB = 92
PP_QW, PP_KW, PP_XQW = 104, 105, 106
PP_SINK = 107
PP_DTB = 111
PP_ALOG = 143
PP_SSDD = 175
PP_XKW = 191
PP_M0, PP_M1 = 320, 321
NPP = 322

ENG = ["pe", "act", "dve", "pool", "sp"]


class Res:
    __slots__ = ("w", "r")

    def __init__(self):
        self.w = None
        self.r = {}


def RL(n):
    return [Res() for _ in range(n)]


class Bld:
    def __init__(self):
        self.streams = {e: [] for e in ENG}
        self.cnt = {}
        self.known = {e: {} for e in ENG}
        self.psi = 0

    def need(self, eng, tok, skip_same=False):
        if tok is None:
            return
        k, v = tok
        if skip_same and k == eng:
            return
        if self.known[eng].get(k, 0) >= v:
            return
        self.known[eng][k] = v
        self.streams[eng].append((0, k, v))

    def op(self, eng, fn, reads=(), writes=(), inc=True):
        for r in reads:
            self.need(eng, r.w)
        for w in writes:
            self.need(eng, w.w, True)
            for k, v in w.r.items():
                self.need(eng, (k, v), True)
        c = self.cnt.get(eng, 0) + 1
        if inc:
            self.cnt[eng] = c
        self.streams[eng].append((1, fn, eng if inc else None, 1))
        for r in reads:
            r.r[eng] = c
        for w in writes:
            w.w = (eng, c)
            w.r = {}

    def dma(self, q, chan, fn, reads=(), writes=()):
        for r in reads:
            self.need(q, r.w)
        for w in writes:
            if not (w.w is not None and w.w[0] == chan):
                self.need(q, w.w)
            for k, v in w.r.items():
                self.need(q, (k, v))
        c = self.cnt.get(chan, 0) + 16
        self.cnt[chan] = c
        self.streams[q].append((1, fn, chan, 16))
        for r in reads:
            r.r[chan] = c
        for w in writes:
            w.w = (chan, c)
            w.r = {}

    def barrier(self):
        toks = list(self.cnt.items())
        for e in ENG:
            for t in toks:
                self.need(e, t, True)

    def mm(self, out, lhsT, rhs, start, stop, reads, writes, inc=False):
        self.op("pe", lambda e: e.matmul(out, lhsT=lhsT, rhs=rhs, start=start, stop=stop), reads, writes, inc)

    def tr(self, out, in_, ident, reads, writes, inc=False):
        self.op("pe", lambda e: e.transpose(out=out, in_=in_, identity=ident), reads, writes, inc)

    def act(self, out, in_, func, reads, writes, scale=1.0, bias=0.0, accum=None):
        if accum is None:
            self.op("act", lambda e: e.activation(out=out, in_=in_, func=func, bias=bias, scale=scale), reads, writes)
        else:
            self.op("act", lambda e: e.activation(out=out, in_=in_, func=func, bias=bias, scale=scale,
                                                  accum_out=accum), reads, writes)

    def tt(self, eng, out, in0, in1, op, reads, writes):
        self.op(eng, lambda e: e.tensor_tensor(out=out, in0=in0, in1=in1, op=op), reads, writes)

    def ts(self, eng, out, in0, s1, s2, op0, op1, reads, writes):
        if s2 is None:
            self.op(eng, lambda e: e.tensor_scalar(out=out, in0=in0, scalar1=s1, scalar2=None, op0=op0), reads, writes)
        else:
            self.op(eng, lambda e: e.tensor_scalar(out=out, in0=in0, scalar1=s1, scalar2=s2, op0=op0, op1=op1),
                    reads, writes)

    def stt(self, eng, out, in0, scalar, in1, op0, op1, reads, writes):
        self.op(eng, lambda e: e.scalar_tensor_tensor(out=out, in0=in0, scalar=scalar, in1=in1, op0=op0, op1=op1),
                reads, writes)

    def cp(self, eng, out, in_, reads, writes):
        if eng == "act":
            self.op("act", lambda e: e.activation(out=out, in_=in_, func=AF.Copy), reads, writes)
        else:
            self.op(eng, lambda e: e.tensor_copy(out=out, in_=in_), reads, writes)

    def ms(self, eng, out, val, reads, writes):
        self.op(eng, lambda e: e.memset(out, val), reads, writes)


def run_pipelined(gens):
    prev = None
    for g in gens:
        next(g)
        if prev is not None:
            for _ in prev:
                pass
        prev = g
    if prev is not None:
        for _ in prev:
            pass


def build_program():
    nc = bass.Bass("TRN2", target_bir_lowering=False)
    dx = nc.dram_tensor("x", [SEQ_PER_CORE, L, D], F32, kind="ExternalInput").ap()
    dmem = nc.dram_tensor("mem", [SEQ_PER_CORE, 256, D], F32, kind="ExternalInput").ap()
    dwin = nc.dram_tensor("w_in", [D, 3872], F32, kind="ExternalInput").ap()
    dwkv = nc.dram_tensor("w_mem_kv", [D, 1024], F32, kind="ExternalInput").ap()
    dwout = nc.dram_tensor("w_out", [2048, D], F32, kind="ExternalInput").ap()
    dwup = nc.dram_tensor("w_up", [D, 4096], F32, kind="ExternalInput").ap()
    dwdn = nc.dram_tensor("w_down", [4096, D], F32, kind="ExternalInput").ap()
    dcb = nc.dram_tensor("cbf", [128, NCB, 128], BF16, kind="ExternalInput").ap()
    dcf = nc.dram_tensor("cf32", [128, NCF, 128], F32, kind="ExternalInput").ap()
    dpp = nc.dram_tensor("pp", [128, NPP], F32, kind="ExternalInput").ap()
    dcs = nc.dram_tensor("cossin", [128, 2, L], F32, kind="ExternalInput").ap()
    dout = nc.dram_tensor("out", [SEQ_PER_CORE, L, D], F32, kind="ExternalOutput").ap()
    if DEBUG:
        ddbg = nc.dram_tensor("dbg", [SEQ_PER_CORE, 128, 16, L], BF16, kind="ExternalOutput").ap()

    win_v = dwin.rearrange("(k p) n -> p k n", p=128)
    wkv_v = dwkv.rearrange("(k p) n -> p k n", p=128)
    wout_v = dwout.rearrange("(k p) n -> p k n", p=128)
    wup_v = dwup.rearrange("(k p) n -> p k n", p=128)
    wdn_v = dwdn.rearrange("(k p) n -> p k n", p=128)

    B = Bld()
    es = ExitStack()
    ARENA_ELEMS = 106400
    arena = es.enter_context(nc.sbuf_tensor("arena", [128, ARENA_ELEMS], BF16))
    PF = [es.enter_context(nc.psum_tensor(f"pf{i}", [128, 512], F32)) for i in range(6)]
    PB = [es.enter_context(nc.psum_tensor(f"pb{i}", [128, 1024], BF16)) for i in range(2)]
    PFR = RL(6)
    PBR = RL(2)
    pst = {"f": 0, "b": 0, "n": 6}

    def psf():
        i = pst["f"] % pst["n"]
        pst["f"] = (i + 1) % pst["n"]
        return PF[i], PFR[i]

    def psb():
        i = pst["b"]
        pst["b"] = (i + 1) % 2
        return PB[i], PBR[i]

    def vb(off, n):
        assert off % 4 == 0 and off // 2 + n <= ARENA_ELEMS, (off, n)
        return arena[:, off // 2: off // 2 + n]

    def vf(off, n):
        assert off % 4 == 0 and off // 2 + 2 * n <= ARENA_ELEMS, (off, n)
        return arena[:, off // 2: off // 2 + 2 * n].bitcast(F32)

    o = 0
    O_CB = o; o += NCB * 256
    O_CF = o; o += NCF * 512
    O_PP = o; o += NPP * 4
    O_SM = o; o += 1024
    O_HT = o; o += 32768
    O_MX = o; o += 32768
    O_LO = o; o += 32768
    O_HI = o; o += 65536
    O_WK = o
    WK_SIZE = ARENA_ELEMS * 2 - O_WK
    assert WK_SIZE >= 33000, WK_SIZE

    cb = vb(O_CB, NCB * 128).rearrange("p (m n) -> p m n", m=NCB)
    cf = vf(O_CF, NCF * 128).rearrange("p (m n) -> p m n", m=NCF)
    pp = vf(O_PP, NPP)
    esink = vf(O_SM, 4)
    aneg = vf(O_SM + 16, 32)
    R_const = Res()

    hT = vb(O_HT, 8 * L).rearrange("p (k t) -> p k t", k=8)
    mxA = vb(O_MX, 4 * L).rearrange("p (k t) -> p k t", k=4)
    mxX = vb(O_MX + 16384, 4 * L).rearrange("p (k t) -> p k t", k=4)
    lo = vb(O_LO, NT * 1024).rearrange("p (c f) -> p c f", c=NT)

    ident = cb[:, CB_ID, :]

    B.dma("sp", "c0", lambda e: e.dma_start(out=cb, in_=dcb), (), (R_const,))
    B.dma("sp", "c0", lambda e: e.dma_start(out=cf, in_=dcf), (), (R_const,))
    B.dma("sp", "c0", lambda e: e.dma_start(out=pp, in_=dpp), (), (R_const,))
    B.act(esink, pp[:, PP_SINK:PP_SINK + 4], AF.Exp, (R_const,), (R_const,))
    B.act(aneg, pp[:, PP_ALOG:PP_ALOG + 32], AF.Exp, (R_const,), (R_const,))
    B.ts("dve", aneg, aneg, -1.0, None, ALU.mult, None, (R_const,), (R_const,))

    def wload(dst, src, reads_w, chan):
        B.dma("pool", chan, lambda e: e.dma_start(out=dst, in_=src), (), reads_w)

    def hbuild(src_fn, ntiles, dstT, dst_res, nwcol, wk_off, src_res=None, dst_fn=None, run=True):
        xt = [vf(wk_off + i * 4096, 1024) for i in range(2)]
        xn = [vb(wk_off + 8192 + i * 2048, 1024) for i in range(2)]
        junk = vb(wk_off + 12288, 1024)
        st = vf(wk_off + 14336, 3 * ntiles)
        Rxt, Rxn, Rj, Rst = RL(2), RL(2), Res(), Res()
        B.ms("dve", st, 0.0, (), (Rst,))
        def hb_iter(tt):
            b = tt % 2
            if src_res is None:
                src = src_fn(tt)
                B.dma("sp", f"xt{b}", lambda e, o_=xt[b], i_=src: e.dma_start(out=o_, in_=i_), (), (Rxt[b],))
                xin, rin = xt[b], Rxt[b]
            else:
                xin, rin = src_fn(tt), src_res[tt]
            B.act(junk, xin, AF.Square, (rin, Rst), (Rj, Rst), accum=st[:, 3 * tt:3 * tt + 1])
            B.act(st[:, 3 * tt + 1:3 * tt + 2], st[:, 3 * tt:3 * tt + 1], AF.Ln, (Rst,), (Rst,), scale=1.0 / D, bias=EPS)
            B.act(st[:, 3 * tt + 2:3 * tt + 3], st[:, 3 * tt + 1:3 * tt + 2], AF.Exp, (Rst,), (Rst,), scale=-0.5)
            B.ts("dve", xn[b], xin, st[:, 3 * tt + 2:3 * tt + 3], None, ALU.mult, None, (rin, Rst), (Rxn[b],))
            yield
            pb, pbr = psb()
            for k in range(8):
                B.tr(pb[:, k * 128:(k + 1) * 128], xn[b][:, k * 128:(k + 1) * 128], ident, (Rxn[b], R_const), (pbr,),
                     inc=(k == 7))
            dst_ap = dstT[:, :, tt * 128:(tt + 1) * 128] if dst_fn is None else dst_fn(tt)
            B.tt("dve", dst_ap, pb[:, :].rearrange("p (k t) -> p k t", k=8),
                 pp[:, nwcol:nwcol + 8].unsqueeze(2).to_broadcast([128, 8, 128]), ALU.mult,
                 (pbr, R_const), (dst_res[tt],))

        gens_ = [hb_iter(tt) for tt in range(ntiles)]
        if not run:
            return gens_
        run_pipelined(gens_)

    def rstd_from_ps(psB, psBr, n_feat, lnv, rstd, Rln, Rrs):
        B.act(lnv, psB, AF.Ln, (psBr,), (Rln,), scale=1.0 / n_feat, bias=EPS)
        B.act(rstd, lnv, AF.Exp, (Rln,), (Rrs,), scale=-0.5)

    for s in range(NSEQ_RUN):
      try:
          B.barrier()
          RhT = RL(NT)
          hbuild(lambda tt: dx[s, tt * 128:(tt + 1) * 128, :], NT, hT, RhT, PP_NWMIX, O_WK)
          RhT4 = [[RhT[4 * T + i] for i in range(4)] for T in range(4)]
          B.barrier()
          if STOP == 1:
              raise _Stop()

          Rlo = RL(NT)
          prevb = vb(O_HI, NT * 1024).rearrange("p (c f) -> p c f", c=NT)
          BT = vb(O_HI + 32768, 2 * L).rearrange("p (g t) -> p g t", g=2)
          CT = vb(O_HI + 40960, 2 * L).rearrange("p (g t) -> p g t", g=2)
          Btok = vb(O_HI + 49152, NT * 256).rearrange("p (c f) -> p c f", c=NT)
          Rprevb, RBT, RCT, RBtok = RL(NT), RL(NT), RL(NT), RL(NT)
          Wz = vb(O_MX, 8 * 1024).rearrange("p (k n) -> p k n", k=8)
          RWz = Res()
          dtall = vf(O_MX + 16384, NT * 32).rearrange("p (c f) -> p c f", c=NT)
          aall = vf(O_MX + 18432, NT * 32).rearrange("p (c f) -> p c f", c=NT)
          exall = vf(O_MX + 20480, NT * 64).rearrange("p (c f) -> p c f", c=NT)
          cdec = vf(O_MX + 24576, NT * 32).rearrange("p (c f) -> p c f", c=NT)
          Rst = RL(NT)
          DI = vb(O_MX + 26624, 16 * 128).rearrange("p (h n) -> p h n", h=16)
          RDI = Res()
          wb = [vb(O_WK + i * 8192, 8 * 512).rearrange("p (k n) -> p k n", k=8) for i in range(2)]
          Rwb = RL(2)
          pre = [vb(O_WK + 16384 + i * 4112, 2052) for i in range(2)]
          Rpre = RL(2)
          actT = [vb(O_WK + 24640 + i * 4096, 2048) for i in range(2)]
          RactT = RL(2)
          diag = vb(O_HI, 5 * 128 * 2).rearrange("p (b k n) -> p b k n", b=2, k=5)
          Rdiag = RL(2)
          tmp32 = vf(O_HI + 4096, 64)
          Rtmp = Res()

          wload(Wz, win_v[:, :, C_Z:C_Z + 1024], (RWz,), "wz")
          for h in range(16):
              B.ts("dve", DI[:, h, :], cf[:, CF_ID, :], pp[:, PP_SSDD + h:PP_SSDD + h + 1], None, ALU.mult, None,
                   (R_const,), (RDI,))

          wdt = vb(O_HI + 8192, 8 * 32).rearrange("p (k n) -> p k n", k=8)
          Rwdt = Res()
          wload(wdt, win_v[:, :, C_DT:C_DT + 32], (Rwdt,), "wdt")
          for c in range(NT):
              ps, psr = psf()
              for k in range(8):
                  B.mm(ps[:, 0:32], hT[:, k, c * 128:(c + 1) * 128], wdt[:, k, :], k == 0, k == 7, (RhT[c], Rwdt), (psr,),
                       inc=(k == 7))
              B.tt("dve", tmp32[:, 0:32], ps[:, 0:32], pp[:, PP_DTB:PP_DTB + 32], ALU.add, (psr, R_const), (Rtmp,))
              B.act(tmp32[:, 32:64], tmp32[:, 0:32], AF.Exp, (Rtmp,), (Rtmp,))
              B.act(dtall[:, c, :], tmp32[:, 32:64], AF.Ln, (Rtmp,), (Rst[c],), bias=1.0)
              B.tt("dve", aall[:, c, :], dtall[:, c, :], aneg, ALU.mult, (Rst[c], R_const), (Rst[c],))
              ps, psr = psf()
              B.mm(ps[:, 0:16], cf[:, CF_TUI, :], aall[:, c, 0:16], True, True, (R_const, Rst[c]), (psr,))
              B.mm(ps[:, 16:32], cf[:, CF_TLI, :], aall[:, c, 16:32], True, True, (R_const, Rst[c]), (psr,))
              B.mm(ps[:, 32:48], cf[:, CF_TUS, :], aall[:, c, 0:16], True, True, (R_const, Rst[c]), (psr,))
              B.mm(ps[:, 48:64], cf[:, CF_TLS, :], aall[:, c, 16:32], True, True, (R_const, Rst[c]), (psr,))
              B.mm(ps[:, 64:96], cf[:, CF_ONES, :], aall[:, c, :], True, True, (R_const, Rst[c]), (psr,), inc=True)
              B.act(exall[:, c, :], ps[:, 0:64], AF.Exp, (psr,), (Rst[c],))
              B.act(cdec[:, c, :], ps[:, 64:96], AF.Exp, (psr,), (Rst[c],))

          def xbc_iter(j):
              blk, jj = j // 4, j % 4
              wbi = blk % 2
              if jj == 0:
                  wload(wb[wbi], win_v[:, :, C_XBC + blk * 512:C_XBC + (blk + 1) * 512], (Rwb[wbi],), f"wb{wbi}")
              pb_ = j % 2
              if j < 2:
                  B.ms("dve", pre[pb_][:, 0:2], 0.0, (), (Rpre[pb_],))
                  B.ms("dve", pre[pb_][:, 2050:2052], 0.0, (), (Rpre[pb_],))
              for tap in range(5):
                  B.ts("dve", diag[:, pb_, tap, :], cf[:, CF_ID, :],
                       pp[:, PP_CONVW + j * 5 + tap:PP_CONVW + j * 5 + tap + 1], None, ALU.mult, None,
                       (R_const,), (Rdiag[pb_],))
              for T in range(4):
                  ps, psr = psf()
                  for k in range(8):
                      B.mm(ps[:, :], wb[wbi][:, k, jj * 128:(jj + 1) * 128], hT[:, k, T * 512:(T + 1) * 512], k == 0, k == 7,
                           (Rwb[wbi],) + tuple(RhT4[T]), (psr,), inc=(k == 7))
                  B.cp("act", pre[pb_][:, 2 + T * 512:2 + (T + 1) * 512], ps[:, :], (psr,), (Rpre[pb_],))
              yield
              if j < 8:
                  dstF, dres = actT[pb_], None
              elif j < 10:
                  dstF = BT[:, j - 8, :]
              else:
                  dstF = CT[:, j - 10, :]
              for T in range(4):
                  ps, psr = psf()
                  for tap in range(5):
                      B.mm(ps[:, :], diag[:, pb_, tap, :], pre[pb_][:, T * 512 + tap:T * 512 + tap + 512], tap == 0, tap == 4,
                           (Rdiag[pb_], Rpre[pb_]), (psr,), inc=(tap == 4))
                  if j < 8:
                      wr = (RactT[pb_],)
                  elif j < 10:
                      wr = tuple(RBT[4 * T:4 * T + 4])
                  else:
                      wr = tuple(RCT[4 * T:4 * T + 4])
                  B.act(dstF[:, T * 512:(T + 1) * 512], ps[:, :], AF.Silu, (psr, R_const), wr,
                        bias=pp[:, PP_CONVB + j:PP_CONVB + j + 1])
              if j < 10:
                  for q4 in range(4):
                      pb, pbr = psb()
                      for i in range(4):
                          c = q4 * 4 + i
                          rd = (RactT[pb_], R_const) if j < 8 else (RBT[c], R_const)
                          B.tr(pb[:, i * 128:(i + 1) * 128], dstF[:, c * 128:(c + 1) * 128], ident, rd, (pbr,), inc=(i == 3))
                      if j < 8:
                          B.cp("act", lo[:, q4 * 4:(q4 + 1) * 4, j * 128:(j + 1) * 128],
                               pb[:, 0:512].rearrange("p (c f) -> p c f", c=4), (pbr,), tuple(Rlo[q4 * 4:q4 * 4 + 4]))
                      else:
                          B.cp("act", Btok[:, q4 * 4:(q4 + 1) * 4, (j - 8) * 128:(j - 7) * 128],
                               pb[:, 0:512].rearrange("p (c f) -> p c f", c=4), (pbr,), tuple(RBtok[q4 * 4:q4 * 4 + 4]))
          run_pipelined([xbc_iter(j) for j in range(12)])
          B.barrier()
          if STOP == 3:
              raise _Stop()

          w0 = O_WK
          Hs = vf(w0, 1024); w0 += 4096
          xd = [vb(w0 + i * 2048, 1024) for i in range(2)]; w0 += 4096
          wsm = vf(w0, 64); w0 += 256
          Ebuf = vb(w0, 4096).rearrange("p (q n) -> p q n", q=8); w0 += 8192
          rhsb2 = vb(w0, 2048); rhsb = rhsb2.rearrange("p (h n) -> p h n", h=16); w0 += 4096
          xdt = [vb(w0 + i * 2048, 1024) for i in range(2)]; w0 += 4096
          cbm = vb(w0, 512).rearrange("p (g d n) -> p g d n", g=2, d=2); w0 += 1024
          prevf = vb(w0, 1024); w0 += 2048
          szb = vb(w0, 1024); w0 += 2048
          t1 = vf(w0, 1024); w0 += 4096
          t2 = vf(w0, 1024); w0 += 4096
          gst = vf(w0, 8); w0 += 32
          ynb = vb(w0, 1024); w0 += 2048
          assert w0 - O_WK <= WK_SIZE, (w0 - O_WK, WK_SIZE)
          RH, Rxd, Rws, RE, Rrhs, Rxdt, Rcbm, Rpf, Rsz, Rt1, Rt2, Rg, Ryn = (Res(), RL(2), Res(), RL(8), Res(), RL(2),
                                                                              Res(), Res(), Res(), Res(), Res(), Res(), Res())

          def bc16(ap16):
              return ap16.unsqueeze(2).to_broadcast([128, 16, 64])

          def v3(ap1024):
              return ap1024.rearrange("p (h d) -> p h d", h=16)

          def state_prep(c, d, wcol, ecol, banks=None):
              B.tt("dve", wsm[:, d * 16:(d + 1) * 16], dtall[:, c, wcol:wcol + 16], exall[:, c, ecol:ecol + 16], ALU.mult,
                   (Rst[c],), (Rws,))
              B.tt("dve", v3(xd[d]), v3(lo[:, c, :]), bc16(wsm[:, d * 16:(d + 1) * 16]), ALU.mult, (Rlo[c], Rws), (Rxd[d],))
              pss = []
              for g in range(2):
                  ps, psr = psf() if banks is None else (PF[banks[g]], PFR[banks[g]])
                  B.mm(ps[:, :], Btok[:, c, g * 128:(g + 1) * 128], xd[d][:, g * 512:(g + 1) * 512], True, True,
                       (RBtok[c], Rxd[d]), (psr,), inc=True)
                  pss.append((ps, psr))
              return pss

          def state_apply(c, dcol, pss):
              B.tt("dve", v3(Hs), v3(Hs), bc16(cdec[:, c, dcol:dcol + 16]), ALU.mult, (RH, Rst[c]), (RH,))
              for g in range(2):
                  B.tt("dve", Hs[:, g * 512:(g + 1) * 512], Hs[:, g * 512:(g + 1) * 512], pss[g][0][:, :], ALU.add,
                       (RH, pss[g][1]), (RH,))

          B.ms("dve", Hs, 0.0, (), (RH,))

          def p1_iter(c):
              pss = state_prep(c, 1, 16, 48) if c > 0 else None
              yield
              B.cp("act", prevb[:, c, :], Hs, (RH,), (Rprevb[c],))
              if c > 0:
                  state_apply(c, 16, pss)

          run_pipelined([p1_iter(c) for c in range(NT - 1, -1, -1)])

          E2 = [Ebuf, vb(O_HI + 57344, 4096).rearrange("p (q n) -> p q n", q=8)]
          RE2 = [RE, RL(8)]
          xdt2 = [xdt, [vb(O_WK + 4096 + 2048, 1024), vb(O_MX + 30720, 1024)]]
          Rxdt2 = [Rxdt, [Rxd[1], Res()]]
          B.ms("dve", Hs, 0.0, (), (RH,))

          s1rot = [0]
          Rdm = Res()

          def s1bank():
              i = 4 + s1rot[0] % 2
              s1rot[0] += 1
              return PF[i], PFR[i]

          def p2_s1(c):
              tsl = slice(c * 128, (c + 1) * 128)
              pi = c % 2
              Eb, REb, xdtb, Rxdtb = E2[pi], RE2[pi], xdt2[pi], Rxdt2[pi]
              for g in range(2):
                  ps, psr = s1bank()
                  B.mm(ps[:, 0:128], BT[:, g, tsl], CT[:, g, tsl], True, True, (RBT[c], RCT[c]), (psr,), inc=True)
                  B.tt("dve", cbm[:, g, :, :], ps[:, 0:128].unsqueeze(1).to_broadcast([128, 2, 128]),
                       cb[:, CB_MF:CB_MF + 2, :], ALU.mult, (psr, R_const), (Rcbm,))
              for d in range(2):
                  B.tt("dve" if d == 1 else "pool", v3(xdtb[d]), v3(lo[:, c, :]), bc16(dtall[:, c, d * 16:(d + 1) * 16]), ALU.mult,
                       (Rlo[c], Rst[c]), (Rxdtb[d],))
              for d in range(2):
                  tri = cf[:, CF_TUI, :] if d == 0 else cf[:, CF_TLI, :]
                  lsm = cb[:, CB_LSF, :] if d == 0 else cb[:, CB_LSB, :]
                  B.tt("dve" if d == 1 else "pool", rhsb, aall[:, c, d * 16:(d + 1) * 16].unsqueeze(2).to_broadcast([128, 16, 128]),
                       tri.unsqueeze(1).to_broadcast([128, 16, 128]), ALU.mult, (Rst[c], R_const), (Rrhs,))
                  for q in range(4):
                      ps, psr = s1bank()
                      B.mm(ps[:, :], lsm, rhsb2[:, q * 512:(q + 1) * 512], True, True, (R_const, Rrhs), (psr,), inc=True)
                      qi = d * 4 + q
                      B.act(Eb[:, qi, :], ps[:, :], AF.Exp, (psr,), (REb[qi],))
                  yield
                  for q in range(4):
                      qi = d * 4 + q
                      g = q // 2
                      B.tt("dve", Eb[:, qi, :].rearrange("p (h n) -> p h n", h=4),
                           Eb[:, qi, :].rearrange("p (h n) -> p h n", h=4),
                           cbm[:, g, d, :].unsqueeze(1).to_broadcast([128, 4, 128]), ALU.mult, (REb[qi], Rcbm), (REb[qi],))

          def p2_s2(c):
              tsl = slice(c * 128, (c + 1) * 128)
              pi = c % 2
              Eb, REb, xdtb, Rxdtb = E2[pi], RE2[pi], xdt2[pi], Rxdt2[pi]
              B.cp("act", prevf, Hs, (RH,), (Rpf,))
              if c < NT - 1:
                  state_apply(c, 0, [(PF[2], PFR[2]), (PF[3], PFR[3])])
              for g in range(2):
                  ps, psr = PF[2 + g], PFR[2 + g]
                  B.mm(ps[:, :], CT[:, g, tsl], prevf[:, g * 512:(g + 1) * 512], True, True, (RCT[c], Rpf), (psr,), inc=True)
                  B.tt("dve", v3(t1)[:, g * 8:(g + 1) * 8, :], ps[:, :].rearrange("p (h d) -> p h d", h=8),
                       exall[:, c, g * 8:(g + 1) * 8].unsqueeze(2).to_broadcast([128, 8, 64]), ALU.mult,
                       (psr, Rst[c]), (Rt1,))
              yps = []
              for hf in range(2):
                  ps, psr = PF[hf], PFR[hf]
                  for h8 in range(8):
                      h = hf * 8 + h8
                      osl = ps[:, h8 * 64:(h8 + 1) * 64]
                      B.mm(osl, Eb[:, h // 4, (h % 4) * 128:(h % 4 + 1) * 128], xdtb[0][:, h * 64:(h + 1) * 64], True, False,
                           (REb[h // 4], Rxdtb[0]), (psr,))
                      B.mm(osl, Eb[:, 4 + h // 4, (h % 4) * 128:(h % 4 + 1) * 128], xdtb[1][:, h * 64:(h + 1) * 64], False,
                           False, (REb[4 + h // 4], Rxdtb[1]), (psr,))
                      B.mm(osl, DI[:, h, :], lo[:, c, h * 64:(h + 1) * 64], False, True, (RDI, Rlo[c]), (psr,), inc=(h8 == 7))
                  yps.append((ps, psr))
              for g in range(2):
                  ps, psr = PF[2 + g], PFR[2 + g]
                  B.mm(ps[:, :], CT[:, g, tsl], prevb[:, c, g * 512:(g + 1) * 512], True, True, (RCT[c], Rprevb[c]), (psr,),
                       inc=True)
                  B.tt("dve", v3(t2)[:, g * 8:(g + 1) * 8, :], ps[:, :].rearrange("p (h d) -> p h d", h=8),
                       exall[:, c, 16 + g * 8:16 + (g + 1) * 8].unsqueeze(2).to_broadcast([128, 8, 64]), ALU.mult,
                       (psr, Rst[c]), (Rt2,))
              yield
              for hf in range(2):
                  ps, psr = PF[2 + hf], PFR[2 + hf]
                  for k in range(8):
                      B.mm(ps[:, :], hT[:, k, tsl], Wz[:, k, hf * 512:(hf + 1) * 512], k == 0, k == 7, (RhT[c], RWz), (psr,),
                           inc=(k == 7))
                  B.act(szb[:, hf * 512:(hf + 1) * 512], ps[:, :], AF.Silu, (psr,), (Rsz,))
              B.act(gst[:, 7:8], cf[:, CF_ONES, 0:1], AF.Ln, (R_const,), (Rdm,))
              B.tt("dve", t1, t1, t2, ALU.add, (Rt1, Rt2), (Rt1,))
              for hf in range(2):
                  B.tt("dve", t1[:, hf * 512:(hf + 1) * 512], t1[:, hf * 512:(hf + 1) * 512], yps[hf][0][:, :], ALU.add,
                       (Rt1, yps[hf][1]), (Rt1,))
              yield
              B.tt("dve", t1, t1, szb, ALU.mult, (Rt1, Rsz), (Rt1,))
              B.ms("dve", gst[:, 0:6], 0.0, (), (Rg,))
              for g in range(2):
                  B.act(t2[:, g * 512:(g + 1) * 512], t1[:, g * 512:(g + 1) * 512], AF.Square, (Rt1, Rg), (Rt2, Rg),
                        accum=gst[:, g:g + 1])
              B.act(gst[:, 2:4], gst[:, 0:2], AF.Ln, (Rg,), (Rg,), scale=1.0 / 512, bias=EPS)
              B.act(gst[:, 4:6], gst[:, 2:4], AF.Exp, (Rg,), (Rg,), scale=-0.5)
              for g in range(2):
                  B.ts("dve", ynb[:, g * 512:(g + 1) * 512], t1[:, g * 512:(g + 1) * 512], gst[:, 4 + g:5 + g], None, ALU.mult,
                       None, (Rt1, Rg), (Ryn,))
              yield
              pb, pbr = psb()
              for k in range(8):
                  B.tr(pb[:, k * 128:(k + 1) * 128], ynb[:, k * 128:(k + 1) * 128], ident, (Ryn, R_const), (pbr,), inc=(k == 7))
              B.tt("dve", lo[:, c, :].rearrange("p (k t) -> p k t", k=8), pb[:, :].rearrange("p (k t) -> p k t", k=8),
                   pp[:, PP_SSDNW:PP_SSDNW + 8].unsqueeze(2).to_broadcast([128, 8, 128]), ALU.mult,
                   (pbr, R_const), (Rlo[c],))
              if c + 1 < NT - 1:
                  state_prep(c + 1, 0, 0, 32, banks=(2, 3))

          for _ in p2_s1(0):
              pass
          state_prep(0, 0, 0, 32, banks=(2, 3))
          for c in range(NT):
              g2_ = p2_s2(c)
              g1_ = p2_s1(c + 1) if c + 1 < NT else None
              for seg in range(4):
                  next(g2_, None)
                  if g1_ is not None and seg < 3:
                      next(g1_, None)
          pst["f"] = 0
          B.barrier()
          if STOP == 5:
              raise _Stop()

          qT = vb(O_HI, 4 * L).rearrange("p (k t) -> p k t", k=4)
          kz = vb(O_HI + 16384, 4 * L).rearrange("p (g h t) -> p g h t", g=2, h=2)
          vz = vb(O_HI + 32768, NT * 512).rearrange("p (c f) -> p c f", c=NT)
          cs = vf(O_HI + 49152, 2 * L).rearrange("p (a t) -> p a t", a=2)
          RqT, Rkd, Rvz, Rcs = RL(NT), RL(NT), RL(NT), Res()
          B.dma("sp", "c0", lambda e: e.dma_start(out=cs, in_=dcs), (), (Rcs,))
          wv = vb(O_WK, 8 * 512).rearrange("p (k n) -> p k n", k=8)
          wk = vb(O_WK + 8192, 8 * 256).rearrange("p (k n) -> p k n", k=8)
          wq = vb(O_WK + 12288, 8 * 512).rearrange("p (k n) -> p k n", k=8)
          Rwv, Rwk, Rwq = Res(), Res(), Res()
          def qkset(w0):
              d_ = {}
              d_["sq"] = vb(w0, 512); w0 += 1024
              d_["lnv"] = vf(w0, 512); w0 += 2048
              d_["qn"] = vb(w0, 512); w0 += 1024
              d_["ta"] = vf(w0, 512); w0 += 2048
              d_["tb"] = vf(w0, 512); w0 += 2048
              for nm in ("Rsq", "Rln", "Rqn", "Rta", "Rtb"):
                  d_[nm] = Res()
              return d_
          qks = [qkset(O_WK + 20480), qkset(O_WK)]
          w0 = O_WK + 28672
          Pt = [vb(w0 + i * 1536, 768) for i in range(2)]; w0 += 3072
          dpl = vf(w0, 512); w0 += 2048
          ktmp2 = [vb(w0 + i * 1024, 512) for i in range(2)]; w0 += 2048
          Rkt2 = RL(2)
          assert w0 - O_WK <= WK_SIZE
          RPt, Rdp = RL(2), Res()
          qkc = [0]

          B.ms("dve", vb(O_WK, 4096), 0.0, (), (Rwv,))
          for g in range(2):
              for hf in range(2):
                  c0 = (g * 2 + hf) * 128 + hf * 64
                  wload(wv[:, :, c0:c0 + 64], win_v[:, :, C_V + g * 64:C_V + (g + 1) * 64], (Rwv,), "wv")
              for hf in range(2):
                  wload(wk[:, :, g * 128 + hf * 64:g * 128 + (hf + 1) * 64], win_v[:, :, C_K + g * 64:C_K + (g + 1) * 64],
                        (Rwk,), "wk")
          wload(wq, win_v[:, :, C_Q:C_Q + 512], (Rwq,), "wq")
          for tt in range(NT):
              ps, psr = psf()
              for k in range(8):
                  B.mm(ps[:, :], hT[:, k, tt * 128:(tt + 1) * 128], wv[:, k, :], k == 0, k == 7, (RhT[tt], Rwv), (psr,),
                       inc=(k == 7))
              B.cp("act", vz[:, tt, :], ps[:, :], (psr,), (Rvz[tt],))

          def qk_chunk(wtile, wres, col0, dst_ap, dres4, pcol, T, post=None):
              S_ = qks[qkc[0] % 2]
              qkc[0] += 1
              sq, lnv, qn, ta, tb = S_["sq"], S_["lnv"], S_["qn"], S_["ta"], S_["tb"]
              Rsq, Rln, Rqn, Rta, Rtb = S_["Rsq"], S_["Rln"], S_["Rqn"], S_["Rta"], S_["Rtb"]
              tsl = slice(T * 512, (T + 1) * 512)
              psA, psAr = psf()
              for k in range(8):
                  B.mm(psA[:, :], wtile[:, k, col0:col0 + 128], hT[:, k, tsl], k == 0, k == 7, (wres,) + tuple(RhT4[T]),
                       (psAr,), inc=(k == 7))
              B.act(sq, psA[:, :], AF.Square, (psAr,), (Rsq,))
              psB, psBr = psf()
              B.mm(psB[:, :], cb[:, CB_BLK, :], sq, True, True, (R_const, Rsq), (psBr,), inc=True)
              rstd_from_ps(psB[:, :], psBr, 64, lnv, lnv, Rln, Rln)
              B.stt("dve", qn, psA[:, :], pp[:, pcol:pcol + 1], lnv, ALU.mult, ALU.mult, (psAr, R_const, Rln), (Rqn,))
              yield
              psR, psRr = psf()
              B.mm(psR[:, :], cb[:, CB_ROT, :], qn, True, True, (R_const, Rqn), (psRr,), inc=True)
              B.tt("dve", ta, psR[:, :], cs[:, 1, tsl], ALU.mult, (psRr, Rcs), (Rta,))
              B.tt("dve", tb, qn, cs[:, 0, tsl], ALU.mult, (Rqn, Rcs), (Rtb,))
              B.tt("dve", dst_ap, ta, tb, ALU.add, (Rta, Rtb), tuple(dres4))
              if post is not None:
                  post()

          B.barrier()
          def kpost(g, T):
              def f():
                  for hf in range(2):
                      B.ts("dve", kz[:, g, hf, T * 512:(T + 1) * 512], ktmp2[g], pp[:, PP_M0 + hf:PP_M0 + hf + 1], None,
                           ALU.mult, None, (Rkt2[g], R_const), tuple(Rkd[4 * T:4 * T + 4]))
              return f
          gens = []
          for T in range(4):
              for g in range(2):
                  gens.append(qk_chunk(wk, Rwk, g * 128, ktmp2[g], (Rkt2[g],), PP_KW, T, post=kpost(g, T)))
              for c4 in range(4):
                  gens.append(qk_chunk(wq, Rwq, c4 * 128, qT[:, c4, T * 512:(T + 1) * 512], RqT[4 * T:4 * T + 4], PP_QW, T))
          run_pipelined(gens)

          if STOP == 5.5:
              B.barrier()
              raise _Stop()
          pst["n"] = 4
          pst["f"] = 0
          psN, psNr = PF[4], PFR[4]
          psD, psDr = PF[5], PFR[5]

          def att_iter(n, c4):
              qsl = slice(n * 128, (n + 1) * 128)
              js = [j for j in (n - 1, n, n + 1) if 0 <= j < NT]
              g = c4 // 2
              pi = c4 % 2
              psS0, psS0r = psf()
              psS1, psS1r = psf()
              for ji, j in enumerate(js):
                  for hf in range(2):
                      slot = ji * 2 + hf
                      pS, pSr = (psS0, psS0r) if slot < 4 else (psS1, psS1r)
                      so = (slot % 4) * 128
                      B.mm(pS[:, so:so + 128], kz[:, g, hf, j * 128:(j + 1) * 128],
                           qT[:, c4, qsl], True, True, (Rkd[j], RqT[n]), (pSr,),
                           inc=(slot == 3 or slot == len(js) * 2 - 1))
              n0 = min(4, len(js) * 2)
              B.act(Pt[pi][:, 0:n0 * 128], psS0[:, 0:n0 * 128], AF.Exp, (psS0r,), (RPt[pi],), scale=0.125)
              if len(js) * 2 > 4:
                  B.act(Pt[pi][:, 512:768], psS1[:, 0:256], AF.Exp, (psS1r,), (RPt[pi],), scale=0.125)
              for ji, j in enumerate(js):
                  if j != n:
                      mk = cb[:, CB_MPREV, :] if j < n else cb[:, CB_MNEXT, :]
                      pv = Pt[pi][:, ji * 256:(ji + 1) * 256].rearrange("p (h q) -> p h q", h=2)
                      B.tt("dve", pv, pv, mk.unsqueeze(1).to_broadcast([128, 2, 128]), ALU.mult, (RPt[pi], R_const),
                           (RPt[pi],))
              yield
              nmm = len(js) * 2
              i = 0
              for ji, j in enumerate(js):
                  for hf in range(2):
                      B.mm(psN[:, c4 * 128:(c4 + 1) * 128], vz[:, j, (g * 2 + hf) * 128:(g * 2 + hf + 1) * 128],
                           Pt[pi][:, (ji * 2 + hf) * 128:(ji * 2 + hf + 1) * 128], i == 0, i == nmm - 1,
                           (Rvz[j], RPt[pi]), (psNr,))
                      i += 1
              i = 0
              for ji, j in enumerate(js):
                  for hf in range(2):
                      B.mm(psD[:, c4 * 128:(c4 + 1) * 128], cb[:, CB_OZ0 + hf, :],
                           Pt[pi][:, (ji * 2 + hf) * 128:(ji * 2 + hf + 1) * 128], i == 0, i == nmm - 1,
                           (R_const, RPt[pi]), (psDr,), inc=(i == nmm - 1))
                      i += 1
              if c4 == 3:
                  for cc in range(4):
                      B.act(dpl[:, cc * 128:(cc + 1) * 128], psD[:, cc * 128:(cc + 1) * 128], AF.Ln, (psDr, R_const), (Rdp,),
                            bias=esink[:, cc:cc + 1])
                  B.act(dpl, dpl, AF.Exp, (Rdp,), (Rdp,), scale=-1.0)
                  B.tt("dve", mxA[:, :, qsl], psN[:, :].rearrange("p (c q) -> p c q", c=4),
                       dpl.rearrange("p (c q) -> p c q", c=4), ALU.mult, (psNr, Rdp), ())

          run_pipelined([att_iter(n, c4) for n in range(NT) for c4 in range(4)])
          pst["n"] = 6
          pst["f"] = 0
          B.barrier()
          if STOP == 6:
              raise _Stop()

          memT = vb(O_HI, 8 * 256).rearrange("p (k t) -> p k t", k=8)
          kmT = vb(O_HI + 4096, 4 * 256).rearrange("p (h t) -> p h t", h=4)
          vm = vb(O_HI + 6144, 2 * 512).rearrange("p (m f) -> p m f", m=2)
          kmn = vb(O_HI + 8192, 512)
          kst = vf(O_HI + 9216, 16)
          wkv = vb(O_HI + 16384, 8 * 1024).rearrange("p (k n) -> p k n", k=8)
          wqx = vb(O_HI + 32768, 8 * 512).rearrange("p (k n) -> p k n", k=8)
          RmemT, RkmT, Rvm, Rkmn, Rkst, Rwkv, Rwqx = RL(2), Res(), RL(2), Res(), Res(), Res(), Res()
          wload(wkv, wkv_v, (Rwkv,), "wkv")
          wload(wqx, win_v[:, :, C_QX:C_QX + 512], (Rwqx,), "wqx")
          hbuild(lambda tt: dmem[s, tt * 128:(tt + 1) * 128, :], 2, memT, RmemT, PP_NWMEM, O_WK)
          def xset(w0):
              d_ = {}
              d_["sq"] = vb(w0, 512); w0 += 1024
              d_["lnv"] = vf(w0, 512); w0 += 2048
              d_["qn"] = vb(w0, 512); w0 += 1024
              d_["rD"] = vf(w0, 512); w0 += 2048
              d_["Px"] = [vb(w0 + i * 1024, 512) for i in range(2)]; w0 += 2048
              for nm in ("Rsq", "Rln", "Rqn", "RrD"):
                  d_[nm] = Res()
              d_["RPx"] = RL(2)
              return d_
          xs_ = [xset(O_WK + 16384), xset(O_WK + 16384 + 8192)]
          tk = vf(O_WK + 32768, 512)
          Rtk = Res()
          assert 32768 + 2048 <= WK_SIZE
          for mt in range(2):
              msl = slice(mt * 128, (mt + 1) * 128)
              psK, psKr = psf()
              for k in range(8):
                  B.mm(psK[:, :], memT[:, k, msl], wkv[:, k, 0:512], k == 0, k == 7, (RmemT[mt], Rwkv), (psKr,), inc=(k == 7))
              B.ms("dve", kst, 0.0, (), (Rkst,))
              for h in range(4):
                  B.act(tk[:, h * 128:(h + 1) * 128], psK[:, h * 128:(h + 1) * 128], AF.Square, (psKr, Rkst), (Rtk, Rkst),
                        accum=kst[:, h:h + 1])
              B.act(kst[:, 4:8], kst[:, 0:4], AF.Ln, (Rkst,), (Rkst,), scale=1.0 / 128, bias=EPS)
              B.act(kst[:, 8:12], kst[:, 4:8], AF.Exp, (Rkst,), (Rkst,), scale=-0.5)
              B.tt("dve", tk.rearrange("p (h d) -> p h d", h=4), psK[:, :].rearrange("p (h d) -> p h d", h=4),
                   kst[:, 8:12].unsqueeze(2).to_broadcast([128, 4, 128]), ALU.mult, (psKr, Rkst), (Rtk,))
              B.tt("dve", kmn.rearrange("p (h d) -> p h d", h=4), tk.rearrange("p (h d) -> p h d", h=4),
                   pp[:, PP_XKW:PP_XKW + 128].unsqueeze(1).to_broadcast([128, 4, 128]), ALU.mult, (Rtk, R_const), (Rkmn,))
              pb, pbr = psb()
              for h in range(4):
                  B.tr(pb[:, h * 128:(h + 1) * 128], kmn[:, h * 128:(h + 1) * 128], ident, (Rkmn, R_const), (pbr,), inc=(h == 3))
              B.cp("act", kmT[:, :, msl], pb[:, 0:512].rearrange("p (h t) -> p h t", h=4), (pbr,), (RkmT,))
              psV, psVr = psf()
              for k in range(8):
                  B.mm(psV[:, :], memT[:, k, msl], wkv[:, k, 512:1024], k == 0, k == 7, (RmemT[mt], Rwkv), (psVr,), inc=(k == 7))
              B.cp("act", vm[:, mt, :], psV[:, :], (psVr,), (Rvm[mt],))
          def xat_iter(T, h, xc):
              tsl = slice(T * 512, (T + 1) * 512)
              S_ = xs_[xc % 2]
              sq, lnv, qn, rD, Px = S_["sq"], S_["lnv"], S_["qn"], S_["rD"], S_["Px"]
              Rsq, Rln, Rqn, RrD, RPx = S_["Rsq"], S_["Rln"], S_["Rqn"], S_["RrD"], S_["RPx"]
              psA, psAr = psf()
              for k in range(8):
                  B.mm(psA[:, :], wqx[:, k, h * 128:(h + 1) * 128], hT[:, k, tsl], k == 0, k == 7, (Rwqx,) + tuple(RhT4[T]),
                       (psAr,), inc=(k == 7))
              B.act(sq, psA[:, :], AF.Square, (psAr,), (Rsq,))
              psB, psBr = psf()
              B.mm(psB[:, :], cb[:, CB_ONES, :], sq, True, True, (R_const, Rsq), (psBr,), inc=True)
              rstd_from_ps(psB[:, :], psBr, 128, lnv, lnv, Rln, Rln)
              B.stt("dve", qn, psA[:, :], pp[:, PP_XQW:PP_XQW + 1], lnv, ALU.mult, ALU.mult, (psAr, R_const, Rln), (Rqn,))
              yield
              for mt in range(2):
                  psS, psSr = psf()
                  B.mm(psS[:, :], kmT[:, h, mt * 128:(mt + 1) * 128], qn, True, True, (RkmT, Rqn), (psSr,), inc=True)
                  B.act(Px[mt], psS[:, :], AF.Exp, (psSr,), (RPx[mt],), scale=128 ** -0.5)
              psN, psNr = psf()
              psD, psDr = psf()
              for mt in range(2):
                  B.mm(psN[:, :], vm[:, mt, h * 128:(h + 1) * 128], Px[mt], mt == 0, mt == 1, (Rvm[mt], RPx[mt]), (psNr,))
              for mt in range(2):
                  B.mm(psD[:, :], cb[:, CB_ONES, :], Px[mt], mt == 0, mt == 1, (R_const, RPx[mt]), (psDr,), inc=(mt == 1))
              B.act(rD, psD[:, :], AF.Ln, (psDr,), (RrD,))
              B.act(rD, rD, AF.Exp, (RrD,), (RrD,), scale=-1.0)
              B.tt("dve", mxX[:, h, tsl], psN[:, :], rD, ALU.mult, (psNr, RrD), ())

          run_pipelined([xat_iter(T, h, T * 4 + h) for T in range(4) for h in range(4)])
          B.barrier()
          if STOP == 7:
              raise _Stop()

          if DEBUG:
              B.dma("sp", "dbg", lambda e: e.dma_start(out=ddbg[s, :, 0:4, :], in_=mxA), (), ())
              B.dma("sp", "dbg", lambda e: e.dma_start(out=ddbg[s, :, 12:16, :], in_=mxX), (), ())
              for c in range(NT):
                  B.dma("sp", "dbg", lambda e, c=c: e.dma_start(out=ddbg[s, :, 4:12, c * 128:(c + 1) * 128],
                                                               in_=lo[:, c, :].rearrange("p (k t) -> p k t", k=8)), (), ())
              B.barrier()

          x1 = vf(O_HI, NT * 1024).rearrange("p (c f) -> p c f", c=NT)
          Rx1 = RL(NT)
          wo = vb(O_HT, 16 * 1024).rearrange("p (k n) -> p k n", k=16)
          Rwo = Res()
          for kh in range(2):
              wload(wo[:, kh * 8:(kh + 1) * 8, :], wout_v[:, kh * 8:(kh + 1) * 8, :], (Rwo,), "wo")
          xt = [vf(O_WK + 24576 + i * 4096, 1024) for i in range(2)]
          Rxt = RL(2)
          Rmx = Res()
          Rh2 = RL(NT)
          hgens = hbuild(lambda tt: x1[:, tt, :], NT, None, Rh2, PP_NWMLP, O_WK, src_res=Rx1,
                         dst_fn=lambda tt: lo[:, tt, :].rearrange("p (k t) -> p k t", k=8), run=False)
          prevg = None
          for tt in range(NT):
              tsl = slice(tt * 128, (tt + 1) * 128)
              b = tt % 2
              B.dma("sp", f"xt{b}", lambda e, o_=xt[b], i_=dx[s, tsl, :]: e.dma_start(out=o_, in_=i_), (), (Rxt[b],))
              for hf in range(2):
                  ps, psr = psf()
                  for kc in range(16):
                      if kc < 4:
                          lhs = mxA[:, kc, tsl]
                      elif kc < 12:
                          lhs = lo[:, tt, (kc - 4) * 128:(kc - 3) * 128]
                      else:
                          lhs = mxX[:, kc - 12, tsl]
                      B.mm(ps[:, :], lhs, wo[:, kc, hf * 512:(hf + 1) * 512], kc == 0, kc == 15, (Rwo, Rmx, Rh2[tt]), (psr,),
                           inc=(kc == 15))
                  B.tt("dve", x1[:, tt, hf * 512:(hf + 1) * 512], ps[:, :], xt[b][:, hf * 512:(hf + 1) * 512], ALU.add,
                       (psr, Rxt[b]), (Rx1[tt],))
              next(hgens[tt])
              if prevg is not None:
                  for _ in prevg:
                      pass
              prevg = hgens[tt]
          for _ in prevg:
              pass
          if STOP == 8:
              B.barrier()
              raise _Stop()
          Rh24 = [[Rh2[4 * T + i] for i in range(4)] for T in range(4)]
          if STOP == 8.5:
              B.barrier()
              raise _Stop()
          wbase = [O_MX, O_HT]
          wu = [vb(wbase[i], 8 * 1024).rearrange("p (k n) -> p k n", k=8) for i in range(2)]
          wd = [vb(wbase[i] + 16384, 8 * 1024).rearrange("p (k n) -> p k n", k=8) for i in range(2)]
          Rwu, Rwd = [Rmx, Rwo], [Res(), Res()]
          Rwd[0].r, Rwd[1].r = Rmx.r, Rwo.r
          uT = [vb(O_WK + 16384 + i * 8192, 8 * 512).rearrange("p (k t) -> p k t", k=8) for i in range(2)]
          RuT = RL(2)
          rl = [vb(O_WK + 32768 + i * 1024, 512) for i in range(2)]
          Rrl = RL(2)
          assert 32768 + 2048 <= WK_SIZE
          ui = 0

          def mlp_wload(fb_):
              wi_ = fb_ % 2
              wload(wu[wi_], wup_v[:, :, fb_ * 1024:(fb_ + 1) * 1024], (Rwu[wi_],), f"wu{wi_}")
              wload(wd[wi_], wdn_v[:, fb_ * 8:(fb_ + 1) * 8, :], (Rwd[wi_],), f"wd{wi_}")
          mlp_wload(0)
          mlp_wload(1)
          def mlp_iter(fb, T, u):
              wi = fb % 2
              tsl = slice(T * 512, (T + 1) * 512)
              for fc in range(8):
                  ps, psr = psf()
                  for k in range(8):
                      B.mm(ps[:, :], wu[wi][:, k, fc * 128:(fc + 1) * 128], lo[:, 4 * T:4 * T + 4, k * 128:(k + 1) * 128], k == 0,
                           k == 7, (Rwu[wi],) + tuple(Rh24[T]), (psr,), inc=(k == 7))
                  r = fc % 2
                  B.act(rl[r], ps[:, :], AF.Relu, (psr,), (Rrl[r],))
                  B.tt("pool", uT[u][:, fc, :], rl[r], rl[r], ALU.mult, (Rrl[r],), (RuT[u],) if u == 0 else (RuT[u], Rxt[0], Rxt[1]))
              yield
              for ti in range(4):
                  tt = T * 4 + ti
                  for hf in range(2):
                      ps, psr = psf()
                      for fc in range(8):
                          B.mm(ps[:, :], uT[u][:, fc, ti * 128:(ti + 1) * 128], wd[wi][:, fc, hf * 512:(hf + 1) * 512],
                               fc == 0, fc == 7, (RuT[u], Rwd[wi]), (psr,), inc=(fc == 7))
                      B.tt("dve", x1[:, tt, hf * 512:(hf + 1) * 512], x1[:, tt, hf * 512:(hf + 1) * 512], ps[:, :], ALU.add,
                           (Rx1[tt], psr), (Rx1[tt],))
                  if fb == 3:
                      B.dma("sp", "out", lambda e, o_=dout[s, tt * 128:(tt + 1) * 128, :], i_=x1[:, tt, :]:
                            e.dma_start(out=o_, in_=i_), (Rx1[tt],), ())
              if T == 3 and fb + 2 < 4:
                  mlp_wload(fb + 2)

          run_pipelined([mlp_iter(fb, T, (fb * 4 + T) % 2) for fb in range(4) for T in range(4)])
      except _Stop:
        pass
    B.barrier()
    for k, v in list(B.cnt.items()):
        B.need("sp", (k, v))

    keys = list(B.cnt.keys())
    sems = {k: es.enter_context(nc.semaphore(f"s_{k}")) for k in keys}

    def run(e, stream):
        for it in stream:
            if it[0] == 0:
                e.wait_ge(sems[it[1]], it[2])
            else:
                ins = it[1](e)
                if it[2] is not None:
                    ins.then_inc(sems[it[2]], it[3])

    with nc.Block() as block:
        @block.tensor
        def _(e):
            run(e, B.streams["pe"])

        @block.scalar
        def _(e):
            run(e, B.streams["act"])

        @block.vector
        def _(e):
            run(e, B.streams["dve"])

        @block.gpsimd
        def _(e):
            run(e, B.streams["pool"])

        @block.sync
        def _(e):
            run(e, B.streams["sp"])
    es.close()
    return nc


def make_consts():
    j = np.arange(128)[:, None]
    l = np.arange(128)[None, :]
    cbm = np.zeros((128, NCB, 128), np.float32)
    cbm[:, CB_ID] = (j == l)
    cbm[:, CB_ONES] = 1.0
    cbm[:, CB_BLK] = (j // 64 == l // 64)
    cbm[:, CB_OZ0] = (l < 64)
    cbm[:, CB_OZ1] = (l >= 64)
    cbm[:, CB_LSF] = (j > l)
    cbm[:, CB_LSB] = (j < l)
    rot = np.zeros((128, 128), np.float32)
    for hb in (0, 64):
        for d in range(8):
            rot[hb + d + 8, hb + d] = -1.0
            rot[hb + d, hb + d + 8] = 1.0
    cbm[:, CB_ROT] = rot
    cbm[:, CB_MPREV] = (j >= l)
    cbm[:, CB_MNEXT] = (j <= l)
    cbm[:, CB_MF] = (l >= j)
    cbm[:, CB_MB] = (l <= j)
    cfm = np.zeros((128, NCF, 128), np.float32)
    cfm[:, CF_TUI] = (j <= l)
    cfm[:, CF_TLI] = (j >= l)
    cfm[:, CF_TUS] = (j > l)
    cfm[:, CF_TLS] = (j < l)
    cfm[:, CF_ONES] = 1.0
    cfm[:, CF_ID] = (j == l)
    inv = 500000.0 ** (-np.arange(0, 16, 2, dtype=np.float32) / 16)
    t = np.arange(L, dtype=np.float32)
    ang = t[None, :] * inv[:, None]
    cs = np.zeros((128, 2, L), np.float32)
    cs[:, 0, :] = 1.0
    for p in range(128):
        d = p % 64
        if d < 16:
            cs[p, 0] = np.cos(ang[d % 8])
            cs[p, 1] = np.sin(ang[d % 8])
    return cbm.astype(ml_dtypes.bfloat16), cfm, cs


def pack_params(inp):
    pp = np.zeros((128, NPP), np.float32)
    p = np.arange(128)
    pp[:, PP_NWMIX:PP_NWMIX + 8] = inp["norm_mix_w"][0].reshape(8, 128).T
    pp[:, PP_NWMLP:PP_NWMLP + 8] = inp["norm_mlp_w"][0].reshape(8, 128).T
    pp[:, PP_NWMEM:PP_NWMEM + 8] = inp["mem_norm_w"][0].reshape(8, 128).T
    pp[:, PP_SSDNW:PP_SSDNW + 8] = inp["ssd_norm_w"][0].reshape(8, 128).T
    cw = inp["conv_w"][0]
    pp[:, PP_CONVW:PP_CONVW + 60] = cw.reshape(5, 12, 128).transpose(2, 1, 0).reshape(128, 60)
    pp[:, PP_CONVB:PP_CONVB + 12] = inp["conv_b"][0].reshape(12, 128).T
    pp[:, PP_QW] = inp["q_norm_w"][0][p % 64]
    pp[:, PP_KW] = inp["k_norm_w"][0][p % 64]
    pp[:, PP_XQW] = inp["xq_norm_w"][0]
    for c in range(4):
        pp[:, PP_SINK + c] = inp["attn_sink"][0][2 * c + p // 64]
    pp[:, PP_DTB:PP_DTB + 16] = inp["dt_bias_f"][0][None, :]
    pp[:, PP_DTB + 16:PP_DTB + 32] = inp["dt_bias_b"][0][None, :]
    pp[:, PP_ALOG:PP_ALOG + 16] = inp["a_log_f"][0][None, :]
    pp[:, PP_ALOG + 16:PP_ALOG + 32] = inp["a_log_b"][0][None, :]
    pp[:, PP_SSDD:PP_SSDD + 16] = inp["ssd_d"][0][None, :]
    pp[:, PP_XKW:PP_XKW + 128] = inp["xk_norm_w"][0][None, :]
    pp[:, PP_M0] = (p < 64)
    pp[:, PP_M1] = (p >= 64)
    return pp


_NC_CACHE = {}


def kernel(**inputs):
    inp = {k: np.asarray(v) for k, v in inputs.items()}
    if "nc" not in _NC_CACHE:
        _NC_CACHE["nc"] = build_program()
    nc = _NC_CACHE["nc"]
    cbm, cfm, cs = make_consts()
    pp = pack_params(inp)
    shared = {
        "w_in": np.ascontiguousarray(inp["w_in"][0]),
        "w_mem_kv": np.ascontiguousarray(inp["w_mem_kv"][0]),
        "w_out": np.ascontiguousarray(inp["w_out"][0]),
        "w_up": np.ascontiguousarray(inp["w_mlp_up"][0]),
        "w_down": np.ascontiguousarray(inp["w_mlp_down"][0]),
        "cbf": cbm, "cf32": cfm, "pp": pp, "cossin": cs,
    }
    in_maps = []
    for c in range(NCORES):
        m = dict(shared)
        m["x"] = np.ascontiguousarray(inp["x"][c * SEQ_PER_CORE:(c + 1) * SEQ_PER_CORE])
        m["mem"] = np.ascontiguousarray(inp["mem"][c * SEQ_PER_CORE:(c + 1) * SEQ_PER_CORE])
        in_maps.append(m)
    res = run_bass_kernel_spmd(nc, in_maps, core_ids=list(range(NCORES)))
    out = np.concatenate([np.asarray(r["out"]) for r in res.results], axis=0)
    return out.astype(np.float32)
```

```python
import numpy as np
import ml_dtypes
from contextlib import ExitStack
import concourse.bass as bass
import concourse.mybir as mybir
from concourse.bass_utils import run_bass_kernel_spmd

F32 = mybir.dt.float32
BF16 = mybir.dt.bfloat16
AF = mybir.ActivationFunctionType
ALU = mybir.AluOpType

NCORES = 8
SEQ_PER_CORE = 2
L = 2048
D = 1024
NT = 16
EPS = 1e-6
DEBUG = False
STOP = 99


class _Stop(Exception):
    pass
NSEQ_RUN = SEQ_PER_CORE

C_Q, C_K, C_V, C_Z, C_XBC, C_DT, C_QX = 0, 512, 640, 768, 1792, 3328, 3360

(CB_ID, CB_ONES, CB_BLK, CB_OZ0, CB_OZ1, CB_LSF, CB_LSB, CB_ROT, CB_MPREV, CB_MNEXT, CB_MF, CB_MB) = range(12)
NCB = 12
(CF_TUI, CF_TLI, CF_TUS, CF_TLS, CF_ONES, CF_ID) = range(6)
NCF = 6
PP_NWMIX, PP_NWMLP, PP_NWMEM, PP_SSDNW = 0, 8, 16, 24
PP_CONVW = 32
PP_CONVB = 92
PP_QW, PP_KW, PP_XQW = 104, 105, 106
PP_SINK = 107
PP_DTB = 111
PP_ALOG = 143
PP_SSDD = 175
PP_XKW = 191
PP_M0, PP_M1 = 320, 321
NPP = 322

ENG = ["pe", "act", "dve", "pool", "sp"]


class Res:
    __slots__ = ("w", "r")

    def __init__(self):
        self.w = None
        self.r = {}


def RL(n):
    return [Res() for _ in range(n)]


class Bld:
    def __init__(self):
        self.streams = {e: [] for e in ENG}
        self.cnt = {}
        self.known = {e: {} for e in ENG}
        self.psi = 0

    def need(self, eng, tok, skip_same=False):
        if tok is None:
            return
        k, v = tok
        if skip_same and k == eng:
            return
        if self.known[eng].get(k, 0) >= v:
            return
        self.known[eng][k] = v
        self.streams[eng].append((0, k, v))

    def op(self, eng, fn, reads=(), writes=(), inc=True):
        for r in reads:
            self.need(eng, r.w)
        for w in writes:
            self.need(eng, w.w, True)
            for k, v in w.r.items():
                self.need(eng, (k, v), True)
        c = self.cnt.get(eng, 0) + 1
        if inc:
            self.cnt[eng] = c
        self.streams[eng].append((1, fn, eng if inc else None, 1))
        for r in reads:
            r.r[eng] = c
        for w in writes:
            w.w = (eng, c)
            w.r = {}

    def dma(self, q, chan, fn, reads=(), writes=()):
        for r in reads:
            self.need(q, r.w)
        for w in writes:
            if not (w.w is not None and w.w[0] == chan):
                self.need(q, w.w)
            for k, v in w.r.items():
                self.need(q, (k, v))
        c = self.cnt.get(chan, 0) + 16
        self.cnt[chan] = c
        self.streams[q].append((1, fn, chan, 16))
        for r in reads:
            r.r[chan] = c
        for w in writes:
            w.w = (chan, c)
            w.r = {}

    def barrier(self):
        toks = list(self.cnt.items())
        for e in ENG:
            for t in toks:
                self.need(e, t, True)

    def mm(self, out, lhsT, rhs, start, stop, reads, writes, inc=False):
        self.op("pe", lambda e: e.matmul(out, lhsT=lhsT, rhs=rhs, start=start, stop=stop), reads, writes, inc)

    def tr(self, out, in_, ident, reads, writes, inc=False):
        self.op("pe", lambda e: e.transpose(out=out, in_=in_, identity=ident), reads, writes, inc)

    def act(self, out, in_, func, reads, writes, scale=1.0, bias=0.0, accum=None):
        if accum is None:
            self.op("act", lambda e: e.activation(out=out, in_=in_, func=func, bias=bias, scale=scale), reads, writes)
        else:
            self.op("act", lambda e: e.activation(out=out, in_=in_, func=func, bias=bias, scale=scale,
                                                  accum_out=accum), reads, writes)

    def tt(self, eng, out, in0, in1, op, reads, writes):
        self.op(eng, lambda e: e.tensor_tensor(out=out, in0=in0, in1=in1, op=op), reads, writes)

    def ts(self, eng, out, in0, s1, s2, op0, op1, reads, writes):
        if s2 is None:
            self.op(eng, lambda e: e.tensor_scalar(out=out, in0=in0, scalar1=s1, scalar2=None, op0=op0), reads, writes)
        else:
            self.op(eng, lambda e: e.tensor_scalar(out=out, in0=in0, scalar1=s1, scalar2=s2, op0=op0, op1=op1),
                    reads, writes)

    def stt(self, eng, out, in0, scalar, in1, op0, op1, reads, writes):
        self.op(eng, lambda e: e.scalar_tensor_tensor(out=out, in0=in0, scalar=scalar, in1=in1, op0=op0, op1=op1),
                reads, writes)

    def cp(self, eng, out, in_, reads, writes):
        if eng == "act":
            self.op("act", lambda e: e.activation(out=out, in_=in_, func=AF.Copy), reads, writes)
        else:
            self.op(eng, lambda e: e.tensor_copy(out=out, in_=in_), reads, writes)

    def ms(self, eng, out, val, reads, writes):
        self.op(eng, lambda e: e.memset(out, val), reads, writes)


def run_pipelined(gens):
    prev = None
    for g in gens:
        next(g)
        if prev is not None:
            for _ in prev:
                pass
        prev = g
    if prev is not None:
        for _ in prev:
            pass


def build_program():
    nc = bass.Bass("TRN2", target_bir_lowering=False)
    dx = nc.dram_tensor("x", [SEQ_PER_CORE, L, D], F32, kind="ExternalInput").ap()
    dmem = nc.dram_tensor("mem", [SEQ_PER_CORE, 256, D], F32, kind="ExternalInput").ap()
    dwin = nc.dram_tensor("w_in", [D, 3872], F32, kind="ExternalInput").ap()
    dwkv = nc.dram_tensor("w_mem_kv", [D, 1024], F32, kind="ExternalInput").ap()
    dwout = nc.dram_tensor("w_out", [2048, D], F32, kind="ExternalInput").ap()
    dwup = nc.dram_tensor("w_up", [D, 4096], F32, kind="ExternalInput").ap()
    dwdn = nc.dram_tensor("w_down", [4096, D], F32, kind="ExternalInput").ap()
    dcb = nc.dram_tensor("cbf", [128, NCB, 128], BF16, kind="ExternalInput").ap()
    dcf = nc.dram_tensor("cf32", [128, NCF, 128], F32, kind="ExternalInput").ap()
    dpp = nc.dram_tensor("pp", [128, NPP], F32, kind="ExternalInput").ap()
    dcs = nc.dram_tensor("cossin", [128, 2, L], F32, kind="ExternalInput").ap()
    dout = nc.dram_tensor("out", [SEQ_PER_CORE, L, D], F32, kind="ExternalOutput").ap()
    if DEBUG:
        ddbg = nc.dram_tensor("dbg", [SEQ_PER_CORE, 128, 16, L], BF16, kind="ExternalOutput").ap()

    win_v = dwin.rearrange("(k p) n -> p k n", p=128)
    wkv_v = dwkv.rearrange("(k p) n -> p k n", p=128)
    wout_v = dwout.rearrange("(k p) n -> p k n", p=128)
    wup_v = dwup.rearrange("(k p) n -> p k n", p=128)
    wdn_v = dwdn.rearrange("(k p) n -> p k n", p=128)

    B = Bld()
    es = ExitStack()
    ARENA_ELEMS = 106400
    arena = es.enter_context(nc.sbuf_tensor("arena", [128, ARENA_ELEMS], BF16))
    PF = [es.enter_context(nc.psum_tensor(f"pf{i}", [128, 512], F32)) for i in range(6)]
    PB = [es.enter_context(nc.psum_tensor(f"pb{i}", [128, 1024], BF16)) for i in range(2)]
    PFR = RL(6)
    PBR = RL(2)
    pst = {"f": 0, "b": 0, "n": 6}

    def psf():
        i = pst["f"] % pst["n"]
        pst["f"] = (i + 1) % pst["n"]
        return PF[i], PFR[i]

    def psb():
        i = pst["b"]
        pst["b"] = (i + 1) % 2
        return PB[i], PBR[i]

    def vb(off, n):
        assert off % 4 == 0 and off // 2 + n <= ARENA_ELEMS, (off, n)
        return arena[:, off // 2: off // 2 + n]

    def vf(off, n):
        assert off % 4 == 0 and off // 2 + 2 * n <= ARENA_ELEMS, (off, n)
        return arena[:, off // 2: off // 2 + 2 * n].bitcast(F32)

    o = 0
    O_CB = o; o += NCB * 256
    O_CF = o; o += NCF * 512
    O_PP = o; o += NPP * 4
    O_SM = o; o += 1024
    O_HT = o; o += 32768
    O_MX = o; o += 32768
    O_LO = o; o += 32768
    O_HI = o; o += 65536
    O_WK = o
    WK_SIZE = ARENA_ELEMS * 2 - O_WK
    assert WK_SIZE >= 33000, WK_SIZE

    cb = vb(O_CB, NCB * 128).rearrange("p (m n) -> p m n", m=NCB)
    cf = vf(O_CF, NCF * 128).rearrange("p (m n) -> p m n", m=NCF)
    pp = vf(O_PP, NPP)
    esink = vf(O_SM, 4)
    aneg = vf(O_SM + 16, 32)
    R_const = Res()

    hT = vb(O_HT, 8 * L).rearrange("p (k t) -> p k t", k=8)
    mxA = vb(O_MX, 4 * L).rearrange("p (k t) -> p k t", k=4)
    mxX = vb(O_MX + 16384, 4 * L).rearrange("p (k t) -> p k t", k=4)
    lo = vb(O_LO, NT * 1024).rearrange("p (c f) -> p c f", c=NT)

    ident = cb[:, CB_ID, :]

    B.dma("sp", "c0", lambda e: e.dma_start(out=cb, in_=dcb), (), (R_const,))
    B.dma("sp", "c0", lambda e: e.dma_start(out=cf, in_=dcf), (), (R_const,))
    B.dma("sp", "c0", lambda e: e.dma_start(out=pp, in_=dpp), (), (R_const,))
    B.act(esink, pp[:, PP_SINK:PP_SINK + 4], AF.Exp, (R_const,), (R_const,))
    B.act(aneg, pp[:, PP_ALOG:PP_ALOG + 32], AF.Exp, (R_const,), (R_const,))
    B.ts("dve", aneg, aneg, -1.0, None, ALU.mult, None, (R_const,), (R_const,))

    def wload(dst, src, reads_w, chan):
        B.dma("pool", chan, lambda e: e.dma_start(out=dst, in_=src), (), reads_w)

    def hbuild(src_fn, ntiles, dstT, dst_res, nwcol, wk_off, src_res=None, dst_fn=None, run=True):
        xt = [vf(wk_off + i * 4096, 1024) for i in range(2)]
        xn = [vb(wk_off + 8192 + i * 2048, 1024) for i in range(2)]
        junk = vb(wk_off + 12288, 1024)
        st = vf(wk_off + 14336, 3 * ntiles)
        Rxt, Rxn, Rj, Rst = RL(2), RL(2), Res(), Res()
        B.ms("dve", st, 0.0, (), (Rst,))
        def hb_iter(tt):
            b = tt % 2
            if src_res is None:
                src = src_fn(tt)
                B.dma("sp", f"xt{b}", lambda e, o_=xt[b], i_=src: e.dma_start(out=o_, in_=i_), (), (Rxt[b],))
                xin, rin = xt[b], Rxt[b]
            else:
                xin, rin = src_fn(tt), src_res[tt]
            B.act(junk, xin, AF.Square, (rin, Rst), (Rj, Rst), accum=st[:, 3 * tt:3 * tt + 1])
            B.act(st[:, 3 * tt + 1:3 * tt + 2], st[:, 3 * tt:3 * tt + 1], AF.Ln, (Rst,), (Rst,), scale=1.0 / D, bias=EPS)
            B.act(st[:, 3 * tt + 2:3 * tt + 3], st[:, 3 * tt + 1:3 * tt + 2], AF.Exp, (Rst,), (Rst,), scale=-0.5)
            B.ts("dve", xn[b], xin, st[:, 3 * tt + 2:3 * tt + 3], None, ALU.mult, None, (rin, Rst), (Rxn[b],))
            yield
            pb, pbr = psb()
            for k in range(8):
                B.tr(pb[:, k * 128:(k + 1) * 128], xn[b][:, k * 128:(k + 1) * 128], ident, (Rxn[b], R_const), (pbr,),
                     inc=(k == 7))
            dst_ap = dstT[:, :, tt * 128:(tt + 1) * 128] if dst_fn is None else dst_fn(tt)
            B.tt("dve", dst_ap, pb[:, :].rearrange("p (k t) -> p k t", k=8),
                 pp[:, nwcol:nwcol + 8].unsqueeze(2).to_broadcast([128, 8, 128]), ALU.mult,
                 (pbr, R_const), (dst_res[tt],))

        gens_ = [hb_iter(tt) for tt in range(ntiles)]
        if not run:
            return gens_
        run_pipelined(gens_)

    def rstd_from_ps(psB, psBr, n_feat, lnv, rstd, Rln, Rrs):
        B.act(lnv, psB, AF.Ln, (psBr,), (Rln,), scale=1.0 / n_feat, bias=EPS)
        B.act(rstd, lnv, AF.Exp, (Rln,), (Rrs,), scale=-0.5)

    for s in range(NSEQ_RUN):
      try:
          B.barrier()
          RhT = RL(NT)
          hbuild(lambda tt: dx[s, tt * 128:(tt + 1) * 128, :], NT, hT, RhT, PP_NWMIX, O_WK)
          RhT4 = [[RhT[4 * T + i] for i in range(4)] for T in range(4)]
          B.barrier()
          if STOP == 1:
              raise _Stop()

          Rlo = RL(NT)
          prevb = vb(O_HI, NT * 1024).rearrange("p (c f) -> p c f", c=NT)
          BT = vb(O_HI + 32768, 2 * L).rearrange("p (g t) -> p g t", g=2)
          CT = vb(O_HI + 40960, 2 * L).rearrange("p (g t) -> p g t", g=2)
          Btok = vb(O_HI + 49152, NT * 256).rearrange("p (c f) -> p c f", c=NT)
          Rprevb, RBT, RCT, RBtok = RL(NT), RL(NT), RL(NT), RL(NT)
          Wz = vb(O_MX, 8 * 1024).rearrange("p (k n) -> p k n", k=8)
          RWz = Res()
          dtall = vf(O_MX + 16384, NT * 32).rearrange("p (c f) -> p c f", c=NT)
          aall = vf(O_MX + 18432, NT * 32).rearrange("p (c f) -> p c f", c=NT)
          exall = vf(O_MX + 20480, NT * 64).rearrange("p (c f) -> p c f", c=NT)
          cdec = vf(O_MX + 24576, NT * 32).rearrange("p (c f) -> p c f", c=NT)
          Rst = RL(NT)
          DI = vb(O_MX + 26624, 16 * 128).rearrange("p (h n) -> p h n", h=16)
          RDI = Res()
          wb = [vb(O_WK + i * 8192, 8 * 512).rearrange("p (k n) -> p k n", k=8) for i in range(2)]
          Rwb = RL(2)
          pre = [vb(O_WK + 16384 + i * 4112, 2052) for i in range(2)]
          Rpre = RL(2)
          actT = [vb(O_WK + 24640 + i * 4096, 2048) for i in range(2)]
          RactT = RL(2)
          diag = vb(O_HI, 5 * 128 * 2).rearrange("p (b k n) -> p b k n", b=2, k=5)
          Rdiag = RL(2)
          tmp32 = vf(O_HI + 4096, 64)
          Rtmp = Res()

          wload(Wz, win_v[:, :, C_Z:C_Z + 1024], (RWz,), "wz")
          for h in range(16):
              B.ts("dve", DI[:, h, :], cf[:, CF_ID, :], pp[:, PP_SSDD + h:PP_SSDD + h + 1], None, ALU.mult, None,
                   (R_const,), (RDI,))

          wdt = vb(O_HI + 8192, 8 * 32).rearrange("p (k n) -> p k n", k=8)
          Rwdt = Res()
          wload(wdt, win_v[:, :, C_DT:C_DT + 32], (Rwdt,), "wdt")
          tmpb = [tmp32, vf(O_HI + 4096 + 256, 64)]
          Rtmpb = [Rtmp, Res()]

          def dt_iter(c):
              tb_, Rtb_ = tmpb[c % 2], Rtmpb[c % 2]
              ps, psr = psf()
              for k in range(8):
                  B.mm(ps[:, 0:32], hT[:, k, c * 128:(c + 1) * 128], wdt[:, k, :], k == 0, k == 7, (RhT[c], Rwdt), (psr,),
                       inc=(k == 7))
              B.tt("dve", tb_[:, 0:32], ps[:, 0:32], pp[:, PP_DTB:PP_DTB + 32], ALU.add, (psr, R_const), (Rtb_,))
              yield
              B.act(tb_[:, 32:64], tb_[:, 0:32], AF.Exp, (Rtb_,), (Rtb_,))
              B.act(dtall[:, c, :], tb_[:, 32:64], AF.Ln, (Rtb_,), (Rst[c],), bias=1.0)
              B.tt("dve", aall[:, c, :], dtall[:, c, :], aneg, ALU.mult, (Rst[c], R_const), (Rst[c],))

          run_pipelined([dt_iter(c) for c in range(NT)])
          allst = tuple(Rst)
          psA, psAr = psf()
          B.mm(psA[:, 0:256], cf[:, CF_TUI, :], aall[:, :, 0:16], True, True, (R_const,) + allst, (psAr,))
          B.mm(psA[:, 256:512], cf[:, CF_TLI, :], aall[:, :, 16:32], True, True, (R_const,) + allst, (psAr,), inc=True)
          psB, psBr = psf()
          B.mm(psB[:, 0:256], cf[:, CF_TUS, :], aall[:, :, 0:16], True, True, (R_const,) + allst, (psBr,))
          B.mm(psB[:, 256:512], cf[:, CF_TLS, :], aall[:, :, 16:32], True, True, (R_const,) + allst, (psBr,), inc=True)
          psC, psCr = psf()
          B.mm(psC[:, :], cf[:, CF_ONES, :], aall[:, :, :], True, True, (R_const,) + allst, (psCr,), inc=True)
          B.act(exall[:, :, 0:16], psA[:, 0:256].rearrange("p (c h) -> p c h", c=NT), AF.Exp, (psAr,), allst)
          B.act(exall[:, :, 16:32], psA[:, 256:512].rearrange("p (c h) -> p c h", c=NT), AF.Exp, (psAr,), allst)
          B.act(exall[:, :, 32:48], psB[:, 0:256].rearrange("p (c h) -> p c h", c=NT), AF.Exp, (psBr,), allst)
          B.act(exall[:, :, 48:64], psB[:, 256:512].rearrange("p (c h) -> p c h", c=NT), AF.Exp, (psBr,), allst)
          B.act(cdec[:, :, :], psC[:, :].rearrange("p (c h) -> p c h", c=NT), AF.Exp, (psCr,), allst)

          def xbc_iter(j):
              blk, jj = j // 4, j % 4
              wbi = blk % 2
              if jj == 0:
                  wload(wb[wbi], win_v[:, :, C_XBC + blk * 512:C_XBC + (blk + 1) * 512], (Rwb[wbi],), f"wb{wbi}")
              pb_ = j % 2
              if j < 2:
                  B.ms("dve", pre[pb_][:, 0:2], 0.0, (), (Rpre[pb_],))
                  B.ms("dve", pre[pb_][:, 2050:2052], 0.0, (), (Rpre[pb_],))
              for tap in range(5):
                  B.ts("dve", diag[:, pb_, tap, :], cf[:, CF_ID, :],
                       pp[:, PP_CONVW + j * 5 + tap:PP_CONVW + j * 5 + tap + 1], None, ALU.mult, None,
                       (R_const,), (Rdiag[pb_],))
              for T in range(4):
                  ps, psr = psf()
                  for k in range(8):
                      B.mm(ps[:, :], wb[wbi][:, k, jj * 128:(jj + 1) * 128], hT[:, k, T * 512:(T + 1) * 512], k == 0, k == 7,
                           (Rwb[wbi],) + tuple(RhT4[T]), (psr,), inc=(k == 7))
                  B.cp("act", pre[pb_][:, 2 + T * 512:2 + (T + 1) * 512], ps[:, :], (psr,), (Rpre[pb_],))
              yield
              if j < 8:
                  dstF, dres = actT[pb_], None
              elif j < 10:
                  dstF = BT[:, j - 8, :]
              else:
                  dstF = CT[:, j - 10, :]
              for T in range(4):
                  ps, psr = psf()
                  for tap in range(5):
                      B.mm(ps[:, :], diag[:, pb_, tap, :], pre[pb_][:, T * 512 + tap:T * 512 + tap + 512], tap == 0, tap == 4,
                           (Rdiag[pb_], Rpre[pb_]), (psr,), inc=(tap == 4))
                  if j < 8:
                      wr = (RactT[pb_],)
                  elif j < 10:
                      wr = tuple(RBT[4 * T:4 * T + 4])
                  else:
                      wr = tuple(RCT[4 * T:4 * T + 4])
                  B.act(dstF[:, T * 512:(T + 1) * 512], ps[:, :], AF.Silu, (psr, R_const), wr,
                        bias=pp[:, PP_CONVB + j:PP_CONVB + j + 1])
              if j < 10:
                  for q4 in range(4):
                      pb, pbr = psb()
                      for i in range(4):
                          c = q4 * 4 + i
                          rd = (RactT[pb_], R_const) if j < 8 else (RBT[c], R_const)
                          B.tr(pb[:, i * 128:(i + 1) * 128], dstF[:, c * 128:(c + 1) * 128], ident, rd, (pbr,), inc=(i == 3))
                      if j < 8:
                          B.cp("act", lo[:, q4 * 4:(q4 + 1) * 4, j * 128:(j + 1) * 128],
                               pb[:, 0:512].rearrange("p (c f) -> p c f", c=4), (pbr,), tuple(Rlo[q4 * 4:q4 * 4 + 4]))
                      else:
                          B.cp("act", Btok[:, q4 * 4:(q4 + 1) * 4, (j - 8) * 128:(j - 7) * 128],
                               pb[:, 0:512].rearrange("p (c f) -> p c f", c=4), (pbr,), tuple(RBtok[q4 * 4:q4 * 4 + 4]))
          run_pipelined([xbc_iter(j) for j in range(12)])
          B.barrier()
          if STOP == 3:
              raise _Stop()

          w0 = O_WK
          Hs = vf(w0, 1024); w0 += 4096
          xd = [vb(w0 + i * 2048, 1024) for i in range(2)]; w0 += 4096
          wsm = vf(w0, 64); w0 += 256
          Ebuf = vb(w0, 4096).rearrange("p (q n) -> p q n", q=8); w0 += 8192
          rhsb2 = vb(w0, 2048); rhsb = rhsb2.rearrange("p (h n) -> p h n", h=16); w0 += 4096
          xdt = [vb(w0 + i * 2048, 1024) for i in range(2)]; w0 += 4096
          cbm = vb(w0, 512).rearrange("p (g d n) -> p g d n", g=2, d=2); w0 += 1024
          prevf = vb(w0, 1024); w0 += 2048
          szb = vb(w0, 1024); w0 += 2048
          t1 = vf(w0, 1024); w0 += 4096
          t2 = vf(w0, 1024); w0 += 4096
          gst = vf(w0, 8); w0 += 32
          ynb = vb(w0, 1024); w0 += 2048
          assert w0 - O_WK <= WK_SIZE, (w0 - O_WK, WK_SIZE)
          RH, Rxd, Rws, RE, Rrhs, Rxdt, Rcbm, Rpf, Rsz, Rt1, Rt2, Rg, Ryn = (Res(), RL(2), Res(), RL(8), Res(), RL(2),
                                                                              Res(), Res(), Res(), Res(), Res(), Res(), Res())

          def bc16(ap16):
              return ap16.unsqueeze(2).to_broadcast([128, 16, 64])

          def v3(ap1024):
              return ap1024.rearrange("p (h d) -> p h d", h=16)

          def state_prep(c, d, wcol, ecol, banks=None):
              B.tt("dve", wsm[:, d * 16:(d + 1) * 16], dtall[:, c, wcol:wcol + 16], exall[:, c, ecol:ecol + 16], ALU.mult,
                   (Rst[c],), (Rws,))
              B.tt("dve", v3(xd[d]), v3(lo[:, c, :]), bc16(wsm[:, d * 16:(d + 1) * 16]), ALU.mult, (Rlo[c], Rws), (Rxd[d],))
              pss = []
              for g in range(2):
                  ps, psr = psf() if banks is None else (PF[banks[g]], PFR[banks[g]])
                  B.mm(ps[:, :], Btok[:, c, g * 128:(g + 1) * 128], xd[d][:, g * 512:(g + 1) * 512], True, True,
                       (RBtok[c], Rxd[d]), (psr,), inc=True)
                  pss.append((ps, psr))
              return pss

          def state_apply(c, dcol, pss):
              B.tt("dve", v3(Hs), v3(Hs), bc16(cdec[:, c, dcol:dcol + 16]), ALU.mult, (RH, Rst[c]), (RH,))
              for g in range(2):
                  B.tt("dve", Hs[:, g * 512:(g + 1) * 512], Hs[:, g * 512:(g + 1) * 512], pss[g][0][:, :], ALU.add,
                       (RH, pss[g][1]), (RH,))

          B.ms("dve", Hs, 0.0, (), (RH,))

          def p1_iter(c):
              pss = state_prep(c, 1, 16, 48) if c > 0 else None
              yield
              B.cp("act", prevb[:, c, :], Hs, (RH,), (Rprevb[c],))
              if c > 0:
                  state_apply(c, 16, pss)

          run_pipelined([p1_iter(c) for c in range(NT - 1, -1, -1)])

          E2 = [Ebuf, vb(O_HI + 57344, 4096).rearrange("p (q n) -> p q n", q=8)]
          RE2 = [RE, RL(8)]
          xdt2 = [xdt, [vb(O_WK + 4096 + 2048, 1024), vb(O_MX + 30720, 1024)]]
          Rxdt2 = [Rxdt, [Rxd[1], Res()]]
          B.ms("dve", Hs, 0.0, (), (RH,))

          s1rot = [0]
          Rdm = Res()

          def s1bank():
              i = 4 + s1rot[0] % 2
              s1rot[0] += 1
              return PF[i], PFR[i]

          def p2_s1(c):
              tsl = slice(c * 128, (c + 1) * 128)
              pi = c % 2
              Eb, REb, xdtb, Rxdtb = E2[pi], RE2[pi], xdt2[pi], Rxdt2[pi]
              for g in range(2):
                  ps, psr = s1bank()
                  B.mm(ps[:, 0:128], BT[:, g, tsl], CT[:, g, tsl], True, True, (RBT[c], RCT[c]), (psr,), inc=True)
                  B.tt("dve", cbm[:, g, :, :], ps[:, 0:128].unsqueeze(1).to_broadcast([128, 2, 128]),
                       cb[:, CB_MF:CB_MF + 2, :], ALU.mult, (psr, R_const), (Rcbm,))
              for d in range(2):
                  B.tt("dve" if d == 1 else "pool", v3(xdtb[d]), v3(lo[:, c, :]), bc16(dtall[:, c, d * 16:(d + 1) * 16]), ALU.mult,
                       (Rlo[c], Rst[c]), (Rxdtb[d],))
              for d in range(2):
                  tri = cf[:, CF_TUI, :] if d == 0 else cf[:, CF_TLI, :]
                  lsm = cb[:, CB_LSF, :] if d == 0 else cb[:, CB_LSB, :]
                  B.tt("dve" if d == 1 else "pool", rhsb, aall[:, c, d * 16:(d + 1) * 16].unsqueeze(2).to_broadcast([128, 16, 128]),
                       tri.unsqueeze(1).to_broadcast([128, 16, 128]), ALU.mult, (Rst[c], R_const), (Rrhs,))
                  for q in range(4):
                      ps, psr = s1bank()
                      B.mm(ps[:, :], lsm, rhsb2[:, q * 512:(q + 1) * 512], True, True, (R_const, Rrhs), (psr,), inc=True)
                      qi = d * 4 + q
                      B.act(Eb[:, qi, :], ps[:, :], AF.Exp, (psr,), (REb[qi],))
                  yield
                  for q in range(4):
                      qi = d * 4 + q
                      g = q // 2
                      B.tt("dve", Eb[:, qi, :].rearrange("p (h n) -> p h n", h=4),
                           Eb[:, qi, :].rearrange("p (h n) -> p h n", h=4),
                           cbm[:, g, d, :].unsqueeze(1).to_broadcast([128, 4, 128]), ALU.mult, (REb[qi], Rcbm), (REb[qi],))

          def p2_s2(c):
              tsl = slice(c * 128, (c + 1) * 128)
              pi = c % 2
              Eb, REb, xdtb, Rxdtb = E2[pi], RE2[pi], xdt2[pi], Rxdt2[pi]
              B.cp("act", prevf, Hs, (RH,), (Rpf,))
              if c < NT - 1:
                  state_apply(c, 0, [(PF[2], PFR[2]), (PF[3], PFR[3])])
              for g in range(2):
                  ps, psr = PF[2 + g], PFR[2 + g]
                  B.mm(ps[:, :], CT[:, g, tsl], prevf[:, g * 512:(g + 1) * 512], True, True, (RCT[c], Rpf), (psr,), inc=True)
                  B.tt("dve", v3(t1)[:, g * 8:(g + 1) * 8, :], ps[:, :].rearrange("p (h d) -> p h d", h=8),
                       exall[:, c, g * 8:(g + 1) * 8].unsqueeze(2).to_broadcast([128, 8, 64]), ALU.mult,
                       (psr, Rst[c]), (Rt1,))
              yps = []
              for hf in range(2):
                  ps, psr = PF[hf], PFR[hf]
                  for h8 in range(8):
                      h = hf * 8 + h8
                      osl = ps[:, h8 * 64:(h8 + 1) * 64]
                      B.mm(osl, Eb[:, h // 4, (h % 4) * 128:(h % 4 + 1) * 128], xdtb[0][:, h * 64:(h + 1) * 64], True, False,
                           (REb[h // 4], Rxdtb[0]), (psr,))
                      B.mm(osl, Eb[:, 4 + h // 4, (h % 4) * 128:(h % 4 + 1) * 128], xdtb[1][:, h * 64:(h + 1) * 64], False,
                           False, (REb[4 + h // 4], Rxdtb[1]), (psr,))
                      B.mm(osl, DI[:, h, :], lo[:, c, h * 64:(h + 1) * 64], False, True, (RDI, Rlo[c]), (psr,), inc=(h8 == 7))
                  yps.append((ps, psr))
              for g in range(2):
                  ps, psr = PF[2 + g], PFR[2 + g]
                  B.mm(ps[:, :], CT[:, g, tsl], prevb[:, c, g * 512:(g + 1) * 512], True, True, (RCT[c], Rprevb[c]), (psr,),
                       inc=True)
                  B.tt("dve", v3(t2)[:, g * 8:(g + 1) * 8, :], ps[:, :].rearrange("p (h d) -> p h d", h=8),
                       exall[:, c, 16 + g * 8:16 + (g + 1) * 8].unsqueeze(2).to_broadcast([128, 8, 64]), ALU.mult,
                       (psr, Rst[c]), (Rt2,))
              yield
              for hf in range(2):
                  ps, psr = PF[2 + hf], PFR[2 + hf]
                  for k in range(8):
                      B.mm(ps[:, :], hT[:, k, tsl], Wz[:, k, hf * 512:(hf + 1) * 512], k == 0, k == 7, (RhT[c], RWz), (psr,),
                           inc=(k == 7))
                  B.act(szb[:, hf * 512:(hf + 1) * 512], ps[:, :], AF.Silu, (psr,), (Rsz,))
              B.act(gst[:, 7:8], cf[:, CF_ONES, 0:1], AF.Ln, (R_const,), (Rdm,))
              B.tt("dve", t1, t1, t2, ALU.add, (Rt1, Rt2), (Rt1,))
              for hf in range(2):
                  B.tt("dve", t1[:, hf * 512:(hf + 1) * 512], t1[:, hf * 512:(hf + 1) * 512], yps[hf][0][:, :], ALU.add,
                       (Rt1, yps[hf][1]), (Rt1,))
              yield
              B.tt("dve", t1, t1, szb, ALU.mult, (Rt1, Rsz), (Rt1,))
              B.ms("dve", gst[:, 0:6], 0.0, (), (Rg,))
              for g in range(2):
                  B.act(t2[:, g * 512:(g + 1) * 512], t1[:, g * 512:(g + 1) * 512], AF.Square, (Rt1, Rg), (Rt2, Rg),
                        accum=gst[:, g:g + 1])
              B.act(gst[:, 2:4], gst[:, 0:2], AF.Ln, (Rg,), (Rg,), scale=1.0 / 512, bias=EPS)
              B.act(gst[:, 4:6], gst[:, 2:4], AF.Exp, (Rg,), (Rg,), scale=-0.5)
              for g in range(2):
                  B.ts("dve", ynb[:, g * 512:(g + 1) * 512], t1[:, g * 512:(g + 1) * 512], gst[:, 4 + g:5 + g], None, ALU.mult,
                       None, (Rt1, Rg), (Ryn,))
              yield
              pb, pbr = psb()
              for k in range(8):
                  B.tr(pb[:, k * 128:(k + 1) * 128], ynb[:, k * 128:(k + 1) * 128], ident, (Ryn, R_const), (pbr,), inc=(k == 7))
              B.tt("dve", lo[:, c, :].rearrange("p (k t) -> p k t", k=8), pb[:, :].rearrange("p (k t) -> p k t", k=8),
                   pp[:, PP_SSDNW:PP_SSDNW + 8].unsqueeze(2).to_broadcast([128, 8, 128]), ALU.mult,
                   (pbr, R_const), (Rlo[c],))
              if c + 1 < NT - 1:
                  state_prep(c + 1, 0, 0, 32, banks=(2, 3))

          for _ in p2_s1(0):
              pass
          state_prep(0, 0, 0, 32, banks=(2, 3))
          for c in range(NT):
              g2_ = p2_s2(c)
              g1_ = p2_s1(c + 1) if c + 1 < NT else None
              for seg in range(4):
                  next(g2_, None)
                  if g1_ is not None and seg < 3:
                      next(g1_, None)
          pst["f"] = 0
          B.barrier()
          if STOP == 5:
              raise _Stop()

          qT = vb(O_HI, 4 * L).rearrange("p (k t) -> p k t", k=4)
          kz = vb(O_HI + 16384, 4 * L).rearrange("p (g h t) -> p g h t", g=2, h=2)
          vz = vb(O_HI + 32768, NT * 512).rearrange("p (c f) -> p c f", c=NT)
          cs = vf(O_HI + 49152, 2 * L).rearrange("p (a t) -> p a t", a=2)
          RqT, Rkd, Rvz, Rcs = RL(NT), RL(NT), RL(NT), Res()
          B.dma("sp", "c0", lambda e: e.dma_start(out=cs, in_=dcs), (), (Rcs,))
          wv = vb(O_WK, 8 * 512).rearrange("p (k n) -> p k n", k=8)
          wk = vb(O_WK + 8192, 8 * 256).rearrange("p (k n) -> p k n", k=8)
          wq = vb(O_WK + 12288, 8 * 512).rearrange("p (k n) -> p k n", k=8)
          Rwv, Rwk, Rwq = Res(), Res(), Res()
          def qkset(w0):
              d_ = {}
              d_["sq"] = vb(w0, 512); w0 += 1024
              d_["lnv"] = vf(w0, 512); w0 += 2048
              d_["qn"] = vb(w0, 512); w0 += 1024
              d_["ta"] = vf(w0, 512); w0 += 2048
              d_["tb"] = vf(w0, 512); w0 += 2048
              for nm in ("Rsq", "Rln", "Rqn", "Rta", "Rtb"):
                  d_[nm] = Res()
              return d_
          qks = [qkset(O_WK + 20480), qkset(O_WK)]
          w0 = O_WK + 28672
          Pt = [vb(w0 + i * 1536, 768) for i in range(2)]; w0 += 3072
          dpl = vf(w0, 512); w0 += 2048
          ktmp2 = [vb(w0 + i * 1024, 512) for i in range(2)]; w0 += 2048
          Rkt2 = RL(2)
          assert w0 - O_WK <= WK_SIZE
          RPt, Rdp = RL(2), Res()
          qkc = [0]

          B.ms("dve", vb(O_WK, 4096), 0.0, (), (Rwv,))
          for g in range(2):
              for hf in range(2):
                  c0 = (g * 2 + hf) * 128 + hf * 64
                  wload(wv[:, :, c0:c0 + 64], win_v[:, :, C_V + g * 64:C_V + (g + 1) * 64], (Rwv,), "wv")
              for hf in range(2):
                  wload(wk[:, :, g * 128 + hf * 64:g * 128 + (hf + 1) * 64], win_v[:, :, C_K + g * 64:C_K + (g + 1) * 64],
                        (Rwk,), "wk")
          wload(wq, win_v[:, :, C_Q:C_Q + 512], (Rwq,), "wq")
          for tt in range(NT):
              ps, psr = psf()
              for k in range(8):
                  B.mm(ps[:, :], hT[:, k, tt * 128:(tt + 1) * 128], wv[:, k, :], k == 0, k == 7, (RhT[tt], Rwv), (psr,),
                       inc=(k == 7))
              B.cp("act", vz[:, tt, :], ps[:, :], (psr,), (Rvz[tt],))

          def qk_chunk(wtile, wres, col0, dst_ap, dres4, pcol, T, post=None):
              S_ = qks[qkc[0] % 2]
              qkc[0] += 1
              sq, lnv, qn, ta, tb = S_["sq"], S_["lnv"], S_["qn"], S_["ta"], S_["tb"]
              Rsq, Rln, Rqn, Rta, Rtb = S_["Rsq"], S_["Rln"], S_["Rqn"], S_["Rta"], S_["Rtb"]
              tsl = slice(T * 512, (T + 1) * 512)
              psA, psAr = psf()
              for k in range(8):
                  B.mm(psA[:, :], wtile[:, k, col0:col0 + 128], hT[:, k, tsl], k == 0, k == 7, (wres,) + tuple(RhT4[T]),
                       (psAr,), inc=(k == 7))
              B.act(sq, psA[:, :], AF.Square, (psAr,), (Rsq,))
              psB, psBr = psf()
              B.mm(psB[:, :], cb[:, CB_BLK, :], sq, True, True, (R_const, Rsq), (psBr,), inc=True)
              rstd_from_ps(psB[:, :], psBr, 64, lnv, lnv, Rln, Rln)
              B.stt("dve", qn, psA[:, :], pp[:, pcol:pcol + 1], lnv, ALU.mult, ALU.mult, (psAr, R_const, Rln), (Rqn,))
              yield
              psR, psRr = psf()
              B.mm(psR[:, :], cb[:, CB_ROT, :], qn, True, True, (R_const, Rqn), (psRr,), inc=True)
              B.tt("dve", ta, psR[:, :], cs[:, 1, tsl], ALU.mult, (psRr, Rcs), (Rta,))
              B.tt("dve", tb, qn, cs[:, 0, tsl], ALU.mult, (Rqn, Rcs), (Rtb,))
              B.tt("dve", dst_ap, ta, tb, ALU.add, (Rta, Rtb), tuple(dres4))
              if post is not None:
                  post()

          B.barrier()
          def kpost(g, T):
              def f():
                  for hf in range(2):
                      B.ts("dve", kz[:, g, hf, T * 512:(T + 1) * 512], ktmp2[g], pp[:, PP_M0 + hf:PP_M0 + hf + 1], None,
                           ALU.mult, None, (Rkt2[g], R_const), tuple(Rkd[4 * T:4 * T + 4]))
              return f
          gens = []
          for T in range(4):
              for g in range(2):
                  gens.append(qk_chunk(wk, Rwk, g * 128, ktmp2[g], (Rkt2[g],), PP_KW, T, post=kpost(g, T)))
              for c4 in range(4):
                  gens.append(qk_chunk(wq, Rwq, c4 * 128, qT[:, c4, T * 512:(T + 1) * 512], RqT[4 * T:4 * T + 4], PP_QW, T))
          run_pipelined(gens)

          if STOP == 5.5:
              B.barrier()
              raise _Stop()
          pst["n"] = 4
          pst["f"] = 0
          psN, psNr = PF[4], PFR[4]
          psD, psDr = PF[5], PFR[5]

          def att_iter(n, c4):
              qsl = slice(n * 128, (n + 1) * 128)
              js = [j for j in (n - 1, n, n + 1) if 0 <= j < NT]
              g = c4 // 2
              pi = c4 % 2
              psS0, psS0r = psf()
              psS1, psS1r = psf()
              for ji, j in enumerate(js):
                  for hf in range(2):
                      slot = ji * 2 + hf
                      pS, pSr = (psS0, psS0r) if slot < 4 else (psS1, psS1r)
                      so = (slot % 4) * 128
                      B.mm(pS[:, so:so + 128], kz[:, g, hf, j * 128:(j + 1) * 128],
                           qT[:, c4, qsl], True, True, (Rkd[j], RqT[n]), (pSr,),
                           inc=(slot == 3 or slot == len(js) * 2 - 1))
              n0 = min(4, len(js) * 2)
              B.act(Pt[pi][:, 0:n0 * 128], psS0[:, 0:n0 * 128], AF.Exp, (psS0r,), (RPt[pi],), scale=0.125)
              if len(js) * 2 > 4:
                  B.act(Pt[pi][:, 512:768], psS1[:, 0:256], AF.Exp, (psS1r,), (RPt[pi],), scale=0.125)
              for ji, j in enumerate(js):
                  if j != n:
                      mk = cb[:, CB_MPREV, :] if j < n else cb[:, CB_MNEXT, :]
                      pv = Pt[pi][:, ji * 256:(ji + 1) * 256].rearrange("p (h q) -> p h q", h=2)
                      B.tt("dve", pv, pv, mk.unsqueeze(1).to_broadcast([128, 2, 128]), ALU.mult, (RPt[pi], R_const),
                           (RPt[pi],))
              yield
              nmm = len(js) * 2
              i = 0
              for ji, j in enumerate(js):
                  for hf in range(2):
                      B.mm(psN[:, c4 * 128:(c4 + 1) * 128], vz[:, j, (g * 2 + hf) * 128:(g * 2 + hf + 1) * 128],
                           Pt[pi][:, (ji * 2 + hf) * 128:(ji * 2 + hf + 1) * 128], i == 0, i == nmm - 1,
                           (Rvz[j], RPt[pi]), (psNr,))
                      i += 1
              i = 0
              for ji, j in enumerate(js):
                  for hf in range(2):
                      B.mm(psD[:, c4 * 128:(c4 + 1) * 128], cb[:, CB_OZ0 + hf, :],
                           Pt[pi][:, (ji * 2 + hf) * 128:(ji * 2 + hf + 1) * 128], i == 0, i == nmm - 1,
                           (R_const, RPt[pi]), (psDr,), inc=(i == nmm - 1))
                      i += 1
              if c4 == 3:
                  for cc in range(4):
                      B.act(dpl[:, cc * 128:(cc + 1) * 128], psD[:, cc * 128:(cc + 1) * 128], AF.Ln, (psDr, R_const), (Rdp,),
                            bias=esink[:, cc:cc + 1])
                  B.act(dpl, dpl, AF.Exp, (Rdp,), (Rdp,), scale=-1.0)
                  B.tt("dve", mxA[:, :, qsl], psN[:, :].rearrange("p (c q) -> p c q", c=4),
                       dpl.rearrange("p (c q) -> p c q", c=4), ALU.mult, (psNr, Rdp), ())

          run_pipelined([att_iter(n, c4) for n in range(NT) for c4 in range(4)])
          pst["n"] = 6
          pst["f"] = 0
          B.barrier()
          if STOP == 6:
              raise _Stop()

          memT = vb(O_HI, 8 * 256).rearrange("p (k t) -> p k t", k=8)
          kmT = vb(O_HI + 4096, 4 * 256).rearrange("p (h t) -> p h t", h=4)
          vm = vb(O_HI + 6144, 2 * 512).rearrange("p (m f) -> p m f", m=2)
          kmn = vb(O_HI + 8192, 512)
          kst = vf(O_HI + 9216, 16)
          wkv = vb(O_HI + 16384, 8 * 1024).rearrange("p (k n) -> p k n", k=8)
          wqx = vb(O_HI + 32768, 8 * 512).rearrange("p (k n) -> p k n", k=8)
          RmemT, RkmT, Rvm, Rkmn, Rkst, Rwkv, Rwqx = RL(2), Res(), RL(2), Res(), Res(), Res(), Res()
          wload(wkv, wkv_v, (Rwkv,), "wkv")
          wload(wqx, win_v[:, :, C_QX:C_QX + 512], (Rwqx,), "wqx")
          hbuild(lambda tt: dmem[s, tt * 128:(tt + 1) * 128, :], 2, memT, RmemT, PP_NWMEM, O_WK)
          def xset(w0):
              d_ = {}
              d_["sq"] = vb(w0, 512); w0 += 1024
              d_["lnv"] = vf(w0, 512); w0 += 2048
              d_["qn"] = vb(w0, 512); w0 += 1024
              d_["rD"] = vf(w0, 512); w0 += 2048
              d_["Px"] = [vb(w0 + i * 1024, 512) for i in range(2)]; w0 += 2048
              for nm in ("Rsq", "Rln", "Rqn", "RrD"):
                  d_[nm] = Res()
              d_["RPx"] = RL(2)
              return d_
          xs_ = [xset(O_WK + 16384), xset(O_WK + 16384 + 8192)]
          tk = vf(O_WK + 32768, 512)
          Rtk = Res()
          assert 32768 + 2048 <= WK_SIZE
          for mt in range(2):
              msl = slice(mt * 128, (mt + 1) * 128)
              psK, psKr = psf()
              for k in range(8):
                  B.mm(psK[:, :], memT[:, k, msl], wkv[:, k, 0:512], k == 0, k == 7, (RmemT[mt], Rwkv), (psKr,), inc=(k == 7))
              B.ms("dve", kst, 0.0, (), (Rkst,))
              for h in range(4):
                  B.act(tk[:, h * 128:(h + 1) * 128], psK[:, h * 128:(h + 1) * 128], AF.Square, (psKr, Rkst), (Rtk, Rkst),
                        accum=kst[:, h:h + 1])
              B.act(kst[:, 4:8], kst[:, 0:4], AF.Ln, (Rkst,), (Rkst,), scale=1.0 / 128, bias=EPS)
              B.act(kst[:, 8:12], kst[:, 4:8], AF.Exp, (Rkst,), (Rkst,), scale=-0.5)
              B.tt("dve", tk.rearrange("p (h d) -> p h d", h=4), psK[:, :].rearrange("p (h d) -> p h d", h=4),
                   kst[:, 8:12].unsqueeze(2).to_broadcast([128, 4, 128]), ALU.mult, (psKr, Rkst), (Rtk,))
              B.tt("dve", kmn.rearrange("p (h d) -> p h d", h=4), tk.rearrange("p (h d) -> p h d", h=4),
                   pp[:, PP_XKW:PP_XKW + 128].unsqueeze(1).to_broadcast([128, 4, 128]), ALU.mult, (Rtk, R_const), (Rkmn,))
              pb, pbr = psb()
              for h in range(4):
                  B.tr(pb[:, h * 128:(h + 1) * 128], kmn[:, h * 128:(h + 1) * 128], ident, (Rkmn, R_const), (pbr,), inc=(h == 3))
              B.cp("act", kmT[:, :, msl], pb[:, 0:512].rearrange("p (h t) -> p h t", h=4), (pbr,), (RkmT,))
              psV, psVr = psf()
              for k in range(8):
                  B.mm(psV[:, :], memT[:, k, msl], wkv[:, k, 512:1024], k == 0, k == 7, (RmemT[mt], Rwkv), (psVr,), inc=(k == 7))
              B.cp("act", vm[:, mt, :], psV[:, :], (psVr,), (Rvm[mt],))
          def xat_iter(T, h, xc):
              tsl = slice(T * 512, (T + 1) * 512)
              S_ = xs_[xc % 2]
              sq, lnv, qn, rD, Px = S_["sq"], S_["lnv"], S_["qn"], S_["rD"], S_["Px"]
              Rsq, Rln, Rqn, RrD, RPx = S_["Rsq"], S_["Rln"], S_["Rqn"], S_["RrD"], S_["RPx"]
              psA, psAr = psf()
              for k in range(8):
                  B.mm(psA[:, :], wqx[:, k, h * 128:(h + 1) * 128], hT[:, k, tsl], k == 0, k == 7, (Rwqx,) + tuple(RhT4[T]),
                       (psAr,), inc=(k == 7))
              B.act(sq, psA[:, :], AF.Square, (psAr,), (Rsq,))
              psB, psBr = psf()
              B.mm(psB[:, :], cb[:, CB_ONES, :], sq, True, True, (R_const, Rsq), (psBr,), inc=True)
              rstd_from_ps(psB[:, :], psBr, 128, lnv, lnv, Rln, Rln)
              B.stt("dve", qn, psA[:, :], pp[:, PP_XQW:PP_XQW + 1], lnv, ALU.mult, ALU.mult, (psAr, R_const, Rln), (Rqn,))
              yield
              for mt in range(2):
                  psS, psSr = psf()
                  B.mm(psS[:, :], kmT[:, h, mt * 128:(mt + 1) * 128], qn, True, True, (RkmT, Rqn), (psSr,), inc=True)
                  B.act(Px[mt], psS[:, :], AF.Exp, (psSr,), (RPx[mt],), scale=128 ** -0.5)
              psN, psNr = psf()
              psD, psDr = psf()
              for mt in range(2):
                  B.mm(psN[:, :], vm[:, mt, h * 128:(h + 1) * 128], Px[mt], mt == 0, mt == 1, (Rvm[mt], RPx[mt]), (psNr,))
              for mt in range(2):
                  B.mm(psD[:, :], cb[:, CB_ONES, :], Px[mt], mt == 0, mt == 1, (R_const, RPx[mt]), (psDr,), inc=(mt == 1))
              B.act(rD, psD[:, :], AF.Ln, (psDr,), (RrD,))
              B.act(rD, rD, AF.Exp, (RrD,), (RrD,), scale=-1.0)
              B.tt("dve", mxX[:, h, tsl], psN[:, :], rD, ALU.mult, (psNr, RrD), ())

          run_pipelined([xat_iter(T, h, T * 4 + h) for T in range(4) for h in range(4)])
          B.barrier()
          if STOP == 7:
              raise _Stop()

          if DEBUG:
              B.dma("sp", "dbg", lambda e: e.dma_start(out=ddbg[s, :, 0:4, :], in_=mxA), (), ())
              B.dma("sp", "dbg", lambda e: e.dma_start(out=ddbg[s, :, 12:16, :], in_=mxX), (), ())
              for c in range(NT):
                  B.dma("sp", "dbg", lambda e, c=c: e.dma_start(out=ddbg[s, :, 4:12, c * 128:(c + 1) * 128],
                                                               in_=lo[:, c, :].rearrange("p (k t) -> p k t", k=8)), (), ())
              B.barrier()

          x1 = vf(O_HI, NT * 1024).rearrange("p (c f) -> p c f", c=NT)
          Rx1 = RL(NT)
          wo = vb(O_HT, 16 * 1024).rearrange("p (k n) -> p k n", k=16)
          Rwo = Res()
          for kh in range(2):
              wload(wo[:, kh * 8:(kh + 1) * 8, :], wout_v[:, kh * 8:(kh + 1) * 8, :], (Rwo,), "wo")
          xt = [vf(O_WK + 24576 + i * 4096, 1024) for i in range(2)]
          Rxt = RL(2)
          Rmx = Res()
          Rh2 = RL(NT)
          hgens = hbuild(lambda tt: x1[:, tt, :], NT, None, Rh2, PP_NWMLP, O_WK, src_res=Rx1,
                         dst_fn=lambda tt: lo[:, tt, :].rearrange("p (k t) -> p k t", k=8), run=False)
          prevg = None
          for tt in range(NT):
              tsl = slice(tt * 128, (tt + 1) * 128)
              b = tt % 2
              B.dma("sp", f"xt{b}", lambda e, o_=xt[b], i_=dx[s, tsl, :]: e.dma_start(out=o_, in_=i_), (), (Rxt[b],))
              for hf in range(2):
                  ps, psr = psf()
                  for kc in range(16):
                      if kc < 4:
                          lhs = mxA[:, kc, tsl]
                      elif kc < 12:
                          lhs = lo[:, tt, (kc - 4) * 128:(kc - 3) * 128]
                      else:
                          lhs = mxX[:, kc - 12, tsl]
                      B.mm(ps[:, :], lhs, wo[:, kc, hf * 512:(hf + 1) * 512], kc == 0, kc == 15, (Rwo, Rmx, Rh2[tt]), (psr,),
                           inc=(kc == 15))
                  B.tt("dve", x1[:, tt, hf * 512:(hf + 1) * 512], ps[:, :], xt[b][:, hf * 512:(hf + 1) * 512], ALU.add,
                       (psr, Rxt[b]), (Rx1[tt],))
              next(hgens[tt])
              if prevg is not None:
                  for _ in prevg:
                      pass
              prevg = hgens[tt]
          for _ in prevg:
              pass
          if STOP == 8:
              B.barrier()
              raise _Stop()
          Rh24 = [[Rh2[4 * T + i] for i in range(4)] for T in range(4)]
          if STOP == 8.5:
              B.barrier()
              raise _Stop()
          wbase = [O_MX, O_HT]
          wu = [vb(wbase[i], 8 * 1024).rearrange("p (k n) -> p k n", k=8) for i in range(2)]
          wd = [vb(wbase[i] + 16384, 8 * 1024).rearrange("p (k n) -> p k n", k=8) for i in range(2)]
          Rwu, Rwd = [Rmx, Rwo], [Res(), Res()]
          Rwd[0].r, Rwd[1].r = Rmx.r, Rwo.r
          uT = [vb(O_WK + 16384 + i * 8192, 8 * 512).rearrange("p (k t) -> p k t", k=8) for i in range(2)]
          RuT = RL(2)
          rl = [vb(O_WK + 32768 + i * 1024, 512) for i in range(2)]
          Rrl = RL(2)
          assert 32768 + 2048 <= WK_SIZE
          ui = 0

          def mlp_wload(fb_):
              wi_ = fb_ % 2
              wload(wu[wi_], wup_v[:, :, fb_ * 1024:(fb_ + 1) * 1024], (Rwu[wi_],), f"wu{wi_}")
              wload(wd[wi_], wdn_v[:, fb_ * 8:(fb_ + 1) * 8, :], (Rwd[wi_],), f"wd{wi_}")
          mlp_wload(0)
          mlp_wload(1)
          def mlp_iter(fb, T, u):
              wi = fb % 2
              tsl = slice(T * 512, (T + 1) * 512)
              for fc in range(8):
                  ps, psr = psf()
                  for k in range(8):
                      B.mm(ps[:, :], wu[wi][:, k, fc * 128:(fc + 1) * 128], lo[:, 4 * T:4 * T + 4, k * 128:(k + 1) * 128], k == 0,
                           k == 7, (Rwu[wi],) + tuple(Rh24[T]), (psr,), inc=(k == 7))
                  r = fc % 2
                  B.act(rl[r], ps[:, :], AF.Relu, (psr,), (Rrl[r],))
                  B.tt("pool", uT[u][:, fc, :], rl[r], rl[r], ALU.mult, (Rrl[r],), (RuT[u],) if u == 0 else (RuT[u], Rxt[0], Rxt[1]))
              yield
              for ti in range(4):
                  tt = T * 4 + ti
                  for hf in range(2):
                      ps, psr = psf()
                      for fc in range(8):
                          B.mm(ps[:, :], uT[u][:, fc, ti * 128:(ti + 1) * 128], wd[wi][:, fc, hf * 512:(hf + 1) * 512],
                               fc == 0, fc == 7, (RuT[u], Rwd[wi]), (psr,), inc=(fc == 7))
                      B.tt("dve", x1[:, tt, hf * 512:(hf + 1) * 512], x1[:, tt, hf * 512:(hf + 1) * 512], ps[:, :], ALU.add,
                           (Rx1[tt], psr), (Rx1[tt],))
                  if fb == 3:
                      B.dma("sp", "out", lambda e, o_=dout[s, tt * 128:(tt + 1) * 128, :], i_=x1[:, tt, :]:
                            e.dma_start(out=o_, in_=i_), (Rx1[tt],), ())
              if T == 3 and fb + 2 < 4:
                  mlp_wload(fb + 2)

          run_pipelined([mlp_iter(fb, T, (fb * 4 + T) % 2) for fb in range(4) for T in range(4)])
      except _Stop:
        pass
    B.barrier()
    for k, v in list(B.cnt.items()):
        B.need("sp", (k, v))

    keys = list(B.cnt.keys())
    sems = {k: es.enter_context(nc.semaphore(f"s_{k}")) for k in keys}

    def run(e, stream):
        for it in stream:
            if it[0] == 0:
                e.wait_ge(sems[it[1]], it[2])
            else:
                ins = it[1](e)
                if it[2] is not None:
                    ins.then_inc(sems[it[2]], it[3])

    with nc.Block() as block:
        @block.tensor
        def _(e):
            run(e, B.streams["pe"])

        @block.scalar
        def _(e):
            run(e, B.streams["act"])

        @block.vector
        def _(e):
            run(e, B.streams["dve"])

        @block.gpsimd
        def _(e):
            run(e, B.streams["pool"])

        @block.sync
        def _(e):
            run(e, B.streams["sp"])
    es.close()
    return nc


def make_consts():
    j = np.arange(128)[:, None]
    l = np.arange(128)[None, :]
    cbm = np.zeros((128, NCB, 128), np.float32)
    cbm[:, CB_ID] = (j == l)
    cbm[:, CB_ONES] = 1.0
    cbm[:, CB_BLK] = (j // 64 == l // 64)
    cbm[:, CB_OZ0] = (l < 64)
    cbm[:, CB_OZ1] = (l >= 64)
    cbm[:, CB_LSF] = (j > l)
    cbm[:, CB_LSB] = (j < l)
    rot = np.zeros((128, 128), np.float32)
    for hb in (0, 64):
        for d in range(8):
            rot[hb + d + 8, hb + d] = -1.0
            rot[hb + d, hb + d + 8] = 1.0
    cbm[:, CB_ROT] = rot
    cbm[:, CB_MPREV] = (j >= l)
    cbm[:, CB_MNEXT] = (j <= l)
    cbm[:, CB_MF] = (l >= j)
    cbm[:, CB_MB] = (l <= j)
    cfm = np.zeros((128, NCF, 128), np.float32)
    cfm[:, CF_TUI] = (j <= l)
    cfm[:, CF_TLI] = (j >= l)
    cfm[:, CF_TUS] = (j > l)
    cfm[:, CF_TLS] = (j < l)
    cfm[:, CF_ONES] = 1.0
    cfm[:, CF_ID] = (j == l)
    inv = 500000.0 ** (-np.arange(0, 16, 2, dtype=np.float32) / 16)
    t = np.arange(L, dtype=np.float32)
    ang = t[None, :] * inv[:, None]
    cs = np.zeros((128, 2, L), np.float32)
    cs[:, 0, :] = 1.0
    for p in range(128):
        d = p % 64
        if d < 16:
            cs[p, 0] = np.cos(ang[d % 8])
            cs[p, 1] = np.sin(ang[d % 8])
    return cbm.astype(ml_dtypes.bfloat16), cfm, cs


def pack_params(inp):
    pp = np.zeros((128, NPP), np.float32)
    p = np.arange(128)
    pp[:, PP_NWMIX:PP_NWMIX + 8] = inp["norm_mix_w"][0].reshape(8, 128).T
    pp[:, PP_NWMLP:PP_NWMLP + 8] = inp["norm_mlp_w"][0].reshape(8, 128).T
    pp[:, PP_NWMEM:PP_NWMEM + 8] = inp["mem_norm_w"][0].reshape(8, 128).T
    pp[:, PP_SSDNW:PP_SSDNW + 8] = inp["ssd_norm_w"][0].reshape(8, 128).T
    cw = inp["conv_w"][0]
    pp[:, PP_CONVW:PP_CONVW + 60] = cw.reshape(5, 12, 128).transpose(2, 1, 0).reshape(128, 60)
    pp[:, PP_CONVB:PP_CONVB + 12] = inp["conv_b"][0].reshape(12, 128).T
    pp[:, PP_QW] = inp["q_norm_w"][0][p % 64]
    pp[:, PP_KW] = inp["k_norm_w"][0][p % 64]
    pp[:, PP_XQW] = inp["xq_norm_w"][0]
    for c in range(4):
        pp[:, PP_SINK + c] = inp["attn_sink"][0][2 * c + p // 64]
    pp[:, PP_DTB:PP_DTB + 16] = inp["dt_bias_f"][0][None, :]
    pp[:, PP_DTB + 16:PP_DTB + 32] = inp["dt_bias_b"][0][None, :]
    pp[:, PP_ALOG:PP_ALOG + 16] = inp["a_log_f"][0][None, :]
    pp[:, PP_ALOG + 16:PP_ALOG + 32] = inp["a_log_b"][0][None, :]
    pp[:, PP_SSDD:PP_SSDD + 16] = inp["ssd_d"][0][None, :]
    pp[:, PP_XKW:PP_XKW + 128] = inp["xk_norm_w"][0][None, :]
    pp[:, PP_M0] = (p < 64)
    pp[:, PP_M1] = (p >= 64)
    return pp


_NC_CACHE = {}


def kernel(**inputs):
    inp = {k: np.asarray(v) for k, v in inputs.items()}
    if "nc" not in _NC_CACHE:
        _NC_CACHE["nc"] = build_program()
    nc = _NC_CACHE["nc"]
    cbm, cfm, cs = make_consts()
    pp = pack_params(inp)
    shared = {
        "w_in": np.ascontiguousarray(inp["w_in"][0]),
        "w_mem_kv": np.ascontiguousarray(inp["w_mem_kv"][0]),
        "w_out": np.ascontiguousarray(inp["w_out"][0]),
        "w_up": np.ascontiguousarray(inp["w_mlp_up"][0]),
        "w_down": np.ascontiguousarray(inp["w_mlp_down"][0]),
        "cbf": cbm, "cf32": cfm, "pp": pp, "cossin": cs,
    }
    in_maps = []
    for c in range(NCORES):
        m = dict(shared)
        m["x"] = np.ascontiguousarray(inp["x"][c * SEQ_PER_CORE:(c + 1) * SEQ_PER_CORE])
        m["mem"] = np.ascontiguousarray(inp["mem"][c * SEQ_PER_CORE:(c + 1) * SEQ_PER_CORE])
        in_maps.append(m)
    res = run_bass_kernel_spmd(nc, in_maps, core_ids=list(range(NCORES)))
    out = np.concatenate([np.asarray(r["out"]) for r in res.results], axis=0)
    return out.astype(np.float32)
```
